# Optimizing a Trainium2 kernel written in Bass

```python
import jax, jax.numpy as jnp
from jax import lax
import numpy as np

D_MODEL = 1024
BATCH = 8
SEQ = 8192
DEPTH = 1
DEC_BATCH = 128
DEC_SEQ = 8
PAST_LEN = 8192
PAGE_SIZE = 128

GLA_HEADS = 4
GLA_DK = 64
GLA_DV = D_MODEL // 2 // GLA_HEADS
GLA_RANK = 16
GLA_TEMP = 16.0
GLA_CHUNK = 64
SWA_HEADS = 8
SWA_HD = D_MODEL // 2 // SWA_HEADS
DILATED_PATTERNS = ((128, 1), (512, 4), (2048, 16))
WIN_MAX = 2048
MIX_WIDTH = GLA_HEADS * GLA_DV + SWA_HEADS * SWA_HD
D_FF = -(-8 * D_MODEL // (3 * 256)) * 256
EPS = 1e-6
PROJ_SIZES = (GLA_HEADS * GLA_DK, GLA_HEADS * GLA_DK, GLA_HEADS * GLA_DV, GLA_HEADS * GLA_DV,
              GLA_RANK, SWA_HEADS * SWA_HD, SWA_HEADS * SWA_HD, SWA_HEADS * SWA_HD)
PROJ_WIDTH = sum(PROJ_SIZES)
SPLIT_IDX = tuple(int(c) for c in np.cumsum(PROJ_SIZES)[:-1])

kernel_name = 'hymba_gla_dilated_swa_decode_step'

F32 = jnp.float32


def rmsnorm(x, g):
    xf = x.astype(F32)
    y = xf * lax.rsqrt(jnp.mean(xf * xf, axis=-1, keepdims=True) + EPS) * g.astype(F32)
    return y.astype(x.dtype)


def alibi_slopes(n):
    return jnp.exp2(-8.0 * (jnp.arange(n, dtype=F32) + 1.0) / n)


def gla_chunked(q, k, v, log_a, s0, chunk):
    B, T, H, DK = q.shape
    DV = v.shape[-1]
    nc = T // chunk
    def to_chunks(a):
        return jnp.moveaxis(a.astype(F32).reshape(B, nc, chunk, H, a.shape[-1]), 1, 0)
    xs = (to_chunks(q), to_chunks(k), to_chunks(v), to_chunks(log_a))
    causal = jnp.tril(jnp.ones((chunk, chunk), dtype=bool))

    def step(S, inp):
        qc, kc, vc, ac = inp
        b = jnp.cumsum(ac, axis=1)
        diff = b[:, :, None] - b[:, None, :]
        decay = jnp.exp(jnp.where(causal[None, :, :, None, None], diff, -jnp.inf))
        A = jnp.einsum('bthk,bshk,btshk->bhts', qc, kc, decay)
        o_intra = jnp.einsum('bhts,bshv->bthv', A, vc)
        o_inter = jnp.einsum('bthk,bhkv->bthv', qc * jnp.exp(b), S)
        bl = b[:, -1]
        S_new = jnp.exp(bl)[..., None] * S + jnp.einsum(
            'bshk,bshv->bhkv', kc * jnp.exp(bl[:, None] - b), vc)
        return S_new, o_intra + o_inter

    S, o = lax.scan(step, s0.astype(F32), xs)
    o = jnp.moveaxis(o, 0, 1).reshape(B, T, H, DV)
    return S, o


def combine_patterns(parts):
    ms = jnp.stack([p[0] for p in parts])
    M = jnp.max(ms, axis=0)
    sc = jnp.exp(ms - M)
    den = jnp.sum(sc * jnp.stack([p[1] for p in parts]), axis=0)
    num = jnp.sum(sc[..., None] * jnp.stack([p[2] for p in parts]), axis=0)
    return num / den[..., None]


def dilated_prompt(q, k, v):
    B, T, H, E = q.shape
    slopes = alibi_slopes(H)
    parts = []
    for (W, d) in DILATED_PATTERNS:
        n = W // d
        L = T // d
        nb = -(-L // n)
        Lp = nb * n
        def to_blocks(a):
            a = a.reshape(B, L, d, H, E).transpose(0, 2, 1, 3, 4)
            a = jnp.pad(a, ((0, 0), (0, 0), (0, Lp - L), (0, 0), (0, 0)))
            return a.reshape(B, d, nb, n, H, E)
        qb, kb, vb = to_blocks(q), to_blocks(k), to_blocks(v)
        def with_prev(a):
            prev = jnp.pad(a[:, :, :-1], ((0, 0), (0, 0), (1, 0), (0, 0), (0, 0), (0, 0)))
            return jnp.concatenate([prev, a], axis=3)
        kk, vv = with_prev(kb), with_prev(vb)
        s = jnp.einsum('brnqhe,brnkhe->brnhqk', qb, kk, preferred_element_type=F32)
        i = jnp.arange(n)[:, None]
        j = jnp.arange(2 * n)[None, :]
        stp = i + n - j
        band = (stp >= 0) & (stp <= n)
        valid_blk = (jnp.arange(nb)[:, None, None] > 0) | (j[None] >= n)
        valid = band[None] & valid_blk
        bias = -slopes[:, None, None] * (stp * d).astype(F32)[None]
        s = jnp.where(valid[None, None, :, None], s + bias, -jnp.inf)
        m = jnp.max(s, axis=-1)
        p = jnp.exp(s - m[..., None])
        den = jnp.sum(p, axis=-1)
        o = jnp.einsum('brnhqk,brnkhe->brnqhe', p.astype(vv.dtype), vv, preferred_element_type=F32)
        o = o.reshape(B, d, Lp, H, E)[:, :, :L].transpose(0, 2, 1, 3, 4).reshape(B, T, H, E)
        def back(a):
            a = a.transpose(0, 1, 2, 4, 3).reshape(B, d, Lp, H)[:, :, :L]
            return a.transpose(0, 2, 1, 3).reshape(B, T, H)
        parts.append((back(m), back(den), o))
    return combine_patterns(parts)


def dilated_sample(q, k_new, v_new, k_buf, v_buf):
    Bd, Ts, H, E = q.shape
    Wb = k_buf.shape[1]
    slopes = alibi_slopes(H)
    kall = jnp.concatenate([k_buf.astype(k_new.dtype), k_new], axis=1)
    vall = jnp.concatenate([v_buf.astype(v_new.dtype), v_new], axis=1)
    parts = []
    for (W, d) in DILATED_PATTERNS:
        n = W // d
        steps = jnp.arange(n + 1)
        idx = Wb + jnp.arange(Ts)[:, None] - steps[None, :] * d
        valid = idx >= 0
        idxc = jnp.clip(idx, 0, None)
        kg = kall[:, idxc]
        vg = vall[:, idxc]
        s = jnp.einsum('bqhe,bqjhe->bhqj', q, kg, preferred_element_type=F32)
        s = s - slopes[:, None, None] * (steps * d).astype(F32)[None, None, :]
        s = jnp.where(valid[None, None], s, -jnp.inf)
        m = jnp.max(s, axis=-1)
        p = jnp.exp(s - m[..., None])
        den = jnp.sum(p, axis=-1)
        o = jnp.einsum('bhqj,bqjhe->bqhe', p.astype(vg.dtype), vg, preferred_element_type=F32)
        parts.append((m.transpose(0, 2, 1), den.transpose(0, 2, 1), o))
    return combine_patterns(parts)


def decoder_layer(x, s0, k_buf, v_buf, w_in, w_gup, b_g, g_mix, g_gla, w_out,
                  g_ffn, w_fg, w_fu, w_fd):
    B, T, _ = x.shape
    xn = rmsnorm(x, g_mix)
    proj = xn @ w_in
    qg, kg, vg, r, lr, qs, ks, vs = jnp.split(proj, SPLIT_IDX, axis=-1)
    qg = qg.reshape(B, T, GLA_HEADS, GLA_DK) * (GLA_DK ** -0.5)
    kg = kg.reshape(B, T, GLA_HEADS, GLA_DK)
    vg = vg.reshape(B, T, GLA_HEADS, GLA_DV)
    log_a = jax.nn.log_sigmoid((lr @ w_gup + b_g).astype(F32)).reshape(B, T, GLA_HEADS, GLA_DK) / GLA_TEMP
    qs = qs.reshape(B, T, SWA_HEADS, SWA_HD) * (SWA_HD ** -0.5)
    ks = ks.reshape(B, T, SWA_HEADS, SWA_HD)
    vs = vs.reshape(B, T, SWA_HEADS, SWA_HD)
    if k_buf is None:
        s0 = jnp.zeros((B, GLA_HEADS, GLA_DK, GLA_DV), F32)
        chunk = GLA_CHUNK
        o_s = dilated_prompt(qs, ks, vs)
        start = max(T - WIN_MAX, 0)
        k_keep, v_keep = ks[:, start:], vs[:, start:]
    else:
        chunk = T
        o_s = dilated_sample(qs, ks, vs, k_buf, v_buf)
        k_keep, v_keep = ks, vs
    S, o_g = gla_chunked(qg, kg, vg, log_a, s0, chunk)
    o_g = rmsnorm(o_g.astype(x.dtype), g_gla.reshape(GLA_HEADS, GLA_DV)).reshape(B, T, -1) * jax.nn.silu(r)
    mix = jnp.concatenate([o_g, o_s.reshape(B, T, -1).astype(x.dtype)], axis=-1) @ w_out
    h = x + mix
    hn = rmsnorm(h, g_ffn)
    y = h + (jax.nn.silu(hn @ w_fg) * (hn @ w_fu)) @ w_fd
    return y, S, k_keep, v_keep


def setup_inputs(seed: int = 0) -> dict:
    key = jax.random.key(seed)
    ks = jax.random.split(key, 20)
    def nrm(k, shape, scale):
        return jax.random.normal(k, shape, F32) * scale
    wb = min(WIN_MAX, PAST_LEN)
    return {
        'x_prompt': nrm(ks[0], (BATCH, SEQ, D_MODEL), 1.0),
        'x_sample': nrm(ks[1], (DEC_BATCH, DEC_SEQ, D_MODEL), 1.0),
        'state_gla': nrm(ks[2], (DEPTH, DEC_BATCH, GLA_HEADS, GLA_DK, GLA_DV), 1.0),
        'cache_swa_k': nrm(ks[3], (DEPTH, DEC_BATCH, wb, SWA_HEADS, SWA_HD), 1.0),
        'cache_swa_v': nrm(ks[4], (DEPTH, DEC_BATCH, wb, SWA_HEADS, SWA_HD), 1.0),
        'w_in': nrm(ks[5], (DEPTH, D_MODEL, PROJ_WIDTH), D_MODEL ** -0.5),
        'w_gate_up': nrm(ks[6], (DEPTH, GLA_RANK, GLA_HEADS * GLA_DK), GLA_RANK ** -0.5),
        'b_gate': nrm(ks[7], (DEPTH, GLA_HEADS * GLA_DK), 0.1),
        'g_mix_norm': 1.0 + nrm(ks[8], (DEPTH, D_MODEL), 0.02),
        'g_gla_norm': 1.0 + nrm(ks[9], (DEPTH, GLA_HEADS * GLA_DV), 0.02),
        'w_out': nrm(ks[10], (DEPTH, MIX_WIDTH, D_MODEL), MIX_WIDTH ** -0.5),
        'g_ffn_norm': 1.0 + nrm(ks[11], (DEPTH, D_MODEL), 0.02),
        'w_ffn_gate': nrm(ks[12], (DEPTH, D_MODEL, D_FF), D_MODEL ** -0.5),
        'w_ffn_up': nrm(ks[13], (DEPTH, D_MODEL, D_FF), D_MODEL ** -0.5),
        'w_ffn_down': nrm(ks[14], (DEPTH, D_FF, D_MODEL), D_FF ** -0.5),
        'g_final': 1.0 + nrm(ks[15], (D_MODEL,), 0.02),
    }


def reference(x_prompt, x_sample, state_gla, cache_swa_k, cache_swa_v, w_in, w_gate_up, b_gate,
              g_mix_norm, g_gla_norm, w_out, g_ffn_norm, w_ffn_gate, w_ffn_up, w_ffn_down, g_final):
    yp, ys = x_prompt, x_sample
    sp_l, ss_l, kp_l, vp_l, ksn_l, vsn_l = [], [], [], [], [], []
    for l in range(DEPTH):
        wl = (w_in[l], w_gate_up[l], b_gate[l], g_mix_norm[l], g_gla_norm[l], w_out[l],
              g_ffn_norm[l], w_ffn_gate[l], w_ffn_up[l], w_ffn_down[l])
        yp, sp, kp, vp = decoder_layer(yp, None, None, None, *wl)
        ys, ss, ksn, vsn = decoder_layer(ys, state_gla[l], cache_swa_k[l], cache_swa_v[l], *wl)
        sp_l.append(sp); ss_l.append(ss); kp_l.append(kp); vp_l.append(vp)
        ksn_l.append(ksn); vsn_l.append(vsn)
    y_prompt = rmsnorm(yp, g_final)
    y_sample = rmsnorm(ys, g_final)
    return (y_prompt, y_sample, jnp.stack(sp_l), jnp.stack(ss_l), jnp.stack(kp_l), jnp.stack(vp_l),
            jnp.stack(ksn_l), jnp.stack(vsn_l))
```

```python
import contextlib
import os
import numpy as np
import concourse.bass as bass
import concourse.mybir as mybir
from concourse.bass_utils import run_bass_kernel_spmd

F32 = mybir.dt.float32
BF16 = mybir.dt.bfloat16
AF = mybir.ActivationFunctionType
ALU = mybir.AluOpType

NCORES = 8
D = 1024
T = 8192
NS = 128
NSEQ = 16
TS = 8
NTOK = T + NS
PW = 3088
DFF = 2816
NF = DFF // 128
EPS = 1e-6
C_QG, C_KG, C_VG, C_R, C_LR, C_QS, C_KS, C_VS = 0, 256, 512, 1024, 1536, 1552, 2064, 2576
COMPUTE = ("pe", "act", "dve", "pool")


def ss(s, n, d=1):
    return slice(s, s + (n - 1) * d + 1, d)


class Tok:
    __slots__ = ("key", "val")

    def __init__(self, key, val):
        self.key, self.val = key, val


class Rec:
    def __init__(self):
        self.call = None

    def __getattr__(self, name):
        def f(*a, **k):
            self.call = (name, a, k)
            return self
        return f


def _rec(fn):
    if fn is None:
        return None
    r = Rec()
    fn(r)
    return r.call


class TT:
    __slots__ = ("t", "w", "r", "pr")

    def __init__(self, t):
        self.t = t
        self.w = {}
        self.r = {}
        self.pr = {}

    def __getitem__(self, idx):
        return self.t[idx]


def _merge(dst, src):
    for k, v in src.items():
        if dst.get(k, 0) < v:
            dst[k] = v


class Sched:
    def __init__(self, nc):
        self.nc = nc
        self.q = {e: [] for e in ("pe", "act", "dve", "pool", "sync")}
        self.cnt = {e: 0 for e in COMPUTE}
        self.seen = {e: {} for e in self.q}
        self.dma_cnt = {}
        self.sems = {}
        self.out_toks = []
        self.dead = False
        self.nstage = 0
        self.stop = int(os.environ.get('K_STOP', '1000000'))

    def _deps(self, reads, writes):
        need = {}
        for t in reads:
            _merge(need, t.w)
        for t in writes:
            if t.r:
                t.pr = t.r
                t.r = {}
                t.w = {}
            _merge(need, t.pr)
            _merge(need, t.w)
        return need

    def _note(self, reads, writes, tok):
        for t in reads:
            if t.r.get(tok.key, 0) < tok.val:
                t.r[tok.key] = tok.val
        for t in writes:
            if t.w.get(tok.key, 0) < tok.val:
                t.w[tok.key] = tok.val

    def _waits(self, q, need, skip_key=None):
        out = []
        seen = self.seen[q]
        for k, v in need.items():
            if k == skip_key:
                continue
            if seen.get(k, 0) >= v:
                continue
            seen[k] = v
            out.append((k, v))
        return out

    def stage(self):
        self.nstage += 1
        if self.nstage >= self.stop:
            self.dead = True

    def op(self, eng, fn, reads=(), writes=()):
        if self.dead:
            return None
        need = self._deps(reads, writes)
        w = self._waits(eng, need, skip_key="pe" if eng == "pe" else None)
        self.cnt[eng] += 1
        tok = Tok(eng, self.cnt[eng])
        self.q[eng].append((w, _rec(fn), eng, 1))
        self._note(reads, writes, tok)
        return tok

    def dma(self, queue, slot, fn, reads=(), writes=(), is_out=False):
        if self.dead:
            return None
        need = self._deps(reads, writes)
        w = self._waits(queue, need)
        key = "d_" + slot
        self.dma_cnt[key] = self.dma_cnt.get(key, 0) + 16
        tok = Tok(key, self.dma_cnt[key])
        self.q[queue].append((w, _rec(fn), key, 16))
        self._note(reads, writes, tok)
        if is_out:
            self.out_toks.append(tok)
        return tok

    def run(self, st):
        nc = self.nc
        keys = list(COMPUTE) + sorted(self.dma_cnt.keys())
        for k in keys:
            self.sems[k] = st.enter_context(nc.semaphore("s_" + k))
        need = {}
        for t in self.out_toks:
            if need.get(t.key, 0) < t.val:
                need[t.key] = t.val
        self.q["sync"].append((self._waits("sync", need), None, None, 0))
        block = st.enter_context(nc.Block())
        sems = self.sems

        def replay(e, items):
            for (w, fn, key, inc) in items:
                for (k, v) in w:
                    e.wait_ge(sems[k], v)
                if fn is None:
                    continue
                getattr(e, fn[0])(*fn[1], **fn[2]).then_inc(sems[key], inc)

        @block.tensor
        def _(e):
            replay(e, self.q["pe"])

        @block.scalar
        def _(e):
            replay(e, self.q["act"])

        @block.vector
        def _(e):
            replay(e, self.q["dve"])

        @block.gpsimd
        def _(e):
            replay(e, self.q["pool"])

        @block.sync
        def _(e):
            replay(e, self.q["sync"])


def _consts():
    c = {}
    c["ident"] = np.eye(128, dtype=np.float32)
    s = np.arange(128)[:, None]
    t = np.arange(128)[None, :]
    c["ucum"] = np.where(s <= t, -1.0 / 16.0, 0.0).astype(np.float32)
    c["lrem"] = np.where(s > t, -1.0 / 16.0, 0.0).astype(np.float32)
    c["caus"] = np.tile(np.where(s <= t, 1.0, 0.0).astype(np.float32)[:, None, :], (1, 4, 1))
    slopes = np.exp2(-8.0 * (np.arange(8) + 1.0) / 8.0)
    cm = np.zeros((3, 8, 128, 256), np.float32)
    j = np.arange(128)[:, None]
    i = np.arange(128)[None, :]
    for p, d in enumerate((1, 4, 16)):
        for h in range(8):
            prev = np.where(j >= i, np.exp(-slopes[h] * d * np.maximum(i + 128 - j, 0)), 0.0)
            cur = np.where(j <= i, np.exp(-slopes[h] * d * np.maximum(i - j, 0)), 0.0)
            cm[p, h, :, 0:128] = prev
            cm[p, h, :, 128:256] = cur
    c["cmask"] = cm
    ct = np.zeros((128, 17, 8, 8), np.float64)
    def cnt(dist):
        return ((dist >= 0) & (dist <= 128)).astype(np.float64) + \
               ((dist >= 0) & (dist % 4 == 0) & (dist <= 512)) + ((dist >= 0) & (dist % 16 == 0) & (dist <= 2048))
    for kt in range(16):
        pos = 128 * kt + np.arange(128)[:, None, None]
        tq = np.arange(8)[None, None, :]
        dist = 2048 + tq - pos
        ct[:, kt] = cnt(dist) * np.exp(-slopes[None, :, None] * dist)
    pos = np.arange(8)[:, None, None]
    tq = np.arange(8)[None, None, :]
    dist = np.broadcast_to(tq - pos, (8, 8, 8))
    ct[0:8, 16] = np.where(dist >= 0, cnt(dist) * np.exp(-slopes[None, :, None] * np.maximum(dist, 0)), 0.0)
    c["ctab"] = ct.reshape(128, 17, 64).astype(np.float32)
    bm = np.zeros((64, 8, 64), np.float32)
    for h in range(8):
        bm[8 * h:8 * h + 8, h, :] = 1.0
    c["bmask"] = bm.reshape(64, 512)
    sel = np.zeros((64, 8), np.float32)
    for h in range(8):
        sel[8 * h:8 * h + 8, :] = np.eye(8)
    c["sel"] = sel
    return c


def build_program():
    nc = bass.Bass("TRN2", target_bir_lowering=False)

    def din(name, shape, dt=F32):
        return nc.dram_tensor(name, list(shape), dt, kind="ExternalInput").ap()

    def dout(name, shape, dt=F32):
        return nc.dram_tensor(name, list(shape), dt, kind="ExternalOutput").ap()

    def dscr(name, shape, dt):
        return nc.dram_tensor(name, list(shape), dt, kind="Internal").ap()

    xp = din("xp", [T, D])
    xs = din("xs", [NS, D])
    st_in = din("st_in", [NSEQ, 4, 64, 128])
    ck = din("ck", [NSEQ, 2048, 512])
    cv = din("cv", [NSEQ, 2048, 512])
    w_in = din("w_in", [D, PW])
    wgup = din("wgup", [17, 256])
    gmix = din("gmix", [128, 8])
    ggla = din("ggla", [128, 4])
    gffn = din("gffn", [128, 8])
    gfin = din("gfin", [D])
    w_out = din("w_out", [D, D])
    w_fg = din("w_fg", [D, DFF])
    w_fu = din("w_fu", [D, DFF])
    w_fd = din("w_fd", [DFF, D])
    c_ident = din("ident", [128, 128])
    c_ucum = din("ucum", [128, 128])
    c_lrem = din("lrem", [128, 128])
    c_caus = din("caus", [128, 4, 128])
    c_cmask = din("cmask", [3, 8, 128, 256])
    c_ctab = din("ctab", [128, 17, 64])
    c_bmask = din("bmask", [64, 512])
    c_sel = din("sel", [64, 8])

    y_p = dout("y_p", [T, D])
    y_s = dout("y_s", [NS, D])
    st_p = dout("st_p", [4, 64, 128])
    st_s = dout("st_s", [NSEQ, 4, 64, 128])
    ck_p = dout("ck_p", [2048, 512])
    cv_p = dout("cv_p", [2048, 512])
    ck_s = dout("ck_s", [NS, 512])
    cv_s = dout("cv_s", [NS, 512])

    projT = dscr("projT", [12 * 128, NTOK], BF16)
    mixT = dscr("mixT", [8 * 128, NTOK], BF16)
    vnew = dscr("vnew", [NSEQ, 8, 512], BF16)
    projT_t, mixT_t, vnew_t = TT(projT), TT(mixT), TT(vnew)

    S = Sched(nc)
    outer = contextlib.ExitStack()
    with outer:
        def mk(stack):
            def sb(name, shape, dt=F32):
                return TT(stack.enter_context(nc.sbuf_tensor("sb_" + name, list(shape), dt)))

            def ps(name, shape, dt=F32):
                return TT(stack.enter_context(nc.psum_tensor("pp_" + name, list(shape), dt)))
            return sb, ps

        sb0, ps0 = mk(outer)
        identf = sb0("identf", [128, 128])
        identb = sb0("identb", [128, 128], BF16)
        eps_t = sb0("eps_t", [128, 1])
        one_t = sb0("one_t", [128, 1])
        onesb = sb0("onesb", [128, 128], BF16)
        onesf = sb0("onesf", [128, 64])
        S.dma("sync", "c0", lambda e: e.dma_start(out=identf[:], in_=c_ident[:, :]), writes=[identf])
        S.op("dve", lambda e: e.tensor_copy(out=identb[:], in_=identf[:]), [identf], [identb])
        S.op("pool", lambda e: e.memset(eps_t[:], EPS), writes=[eps_t])
        S.op("pool", lambda e: e.memset(one_t[:], 1.0), writes=[one_t])
        S.op("pool", lambda e: e.memset(onesb[:], 1.0), writes=[onesb])
        S.op("pool", lambda e: e.memset(onesf[:], 1.0), writes=[onesf])

        dcount = [0]

        def slot(prefix, n):
            dcount[0] += 1
            return "%s%d" % (prefix, dcount[0] % n)

        def rms_rstd(src, ssq, rstd, junk, dim):
            S.op("pool", lambda e: e.memset(ssq[:], 0.0), writes=[ssq])
            S.op("act", lambda e: e.activation(out=junk[:], in_=src[:], func=AF.Square, accum_out=ssq[:]),
                 [src], [junk, ssq])
            S.op("act", lambda e: e.activation(out=rstd[:], in_=ssq[:], func=AF.Ln, scale=1.0 / dim, bias=eps_t[:]),
                 [ssq, eps_t], [rstd])
            S.op("act", lambda e: e.activation(out=rstd[:], in_=rstd[:], func=AF.Exp, scale=-0.5),
                 [rstd], [rstd])

        p1 = contextlib.ExitStack()
        with p1:
            sb, ps = mk(p1)
            Wb = sb("Wb", [128, 8, PW], BF16)
            wst = [sb("wst%d" % i, [128, PW]) for i in range(2)]
            gmix_t = sb("gmix_t", [128, 8])
            wg_t = sb("wg_t", [17, 256])
            ucum = sb("ucum", [128, 128])
            lrem = sb("lrem", [128, 128])
            caus = sb("caus", [128, 4, 128])
            S.dma("sync", "c1", lambda e: e.dma_start(out=gmix_t[:], in_=gmix[:, :]), writes=[gmix_t])
            S.dma("sync", "c2", lambda e: e.dma_start(out=wg_t[:], in_=wgup[:, :]), writes=[wg_t])
            S.dma("sync", "c3", lambda e: e.dma_start(out=ucum[:], in_=c_ucum[:, :]), writes=[ucum])
            S.dma("sync", "c4", lambda e: e.dma_start(out=lrem[:], in_=c_lrem[:, :]), writes=[lrem])
            S.dma("sync", "c5", lambda e: e.dma_start(out=caus[:], in_=c_caus[:, :, :]), writes=[caus])
            for k in range(8):
                w = wst[k % 2]
                S.dma("sync", "w%d" % (k % 2), lambda e, w=w, k=k: e.dma_start(out=w[:], in_=w_in[128 * k:128 * k + 128, :]),
                      writes=[w])
                eng = "act" if k % 2 == 0 else "dve"
                if eng == "act":
                    S.op("act", lambda e, w=w, k=k: e.activation(out=Wb[:, k, :], in_=w[:], func=AF.Copy,
                                                                  scale=gmix_t[:, k:k + 1]), [w, gmix_t], [Wb])
                else:
                    S.op("dve", lambda e, w=w, k=k: e.tensor_scalar(out=Wb[:, k, :], in0=w[:], scalar1=gmix_t[:, k:k + 1],
                                                                     scalar2=None, op0=ALU.mult), [w, gmix_t], [Wb])

            S.stage()
            xt = [sb("xt%d" % i, [128, D]) for i in range(2)]
            junk = sb("junk", [128, D], BF16)
            ssq = [sb("ssq%d" % i, [128, 1]) for i in range(2)]
            rstd = [sb("rstd%d" % i, [128, 1]) for i in range(2)]
            xn = [sb("xn%d" % i, [128, D], BF16) for i in range(2)]
            xnT = sb("xnT", [128, 8, 512], BF16)
            swaT = [sb("swaT%d" % i, [128, 12, 512], BF16) for i in range(2)]
            glaT = sb("glaT", [128, 8, 512], BF16)
            lrT = sb("lrT", [32, 512])
            ktok = sb("ktok", [128, 4, 256], BF16)
            vtok = sb("vtok", [128, 4, 512], BF16)
            sp_t = sb("sp_t", [128, 4, 256])
            eg = sb("eg", [128, 256])
            cst = [sb("cst%d" % i, [128, 512]) for i in range(2)]
            mixg = [sb("mixg%d" % i, [128, 4, 512], BF16) for i in range(2)]
            Sst = sb("Sst", [128, 2, 128])
            Sbf = sb("Sbf", [128, 2, 128], BF16)
            eq = sb("eq", [128, 2, 128])
            ek = sb("ek", [128, 2, 128])
            ed = sb("ed", [128, 256])
            qz = sb("qz", [128, 4, 128], BF16)
            ktl = sb("ktl", [128, 2, 128], BF16)
            khat = sb("khat", [128, 256], BF16)
            Abf = sb("Abf", [128, 4, 128], BF16)
            osq = sb("osq", [128, 4, 128], BF16)
            rs_g = sb("rs_g", [128, 4, 128])
            vnew_st = sb("vnew_st", [8, 512], BF16)
            ps_tr = ps("ps_tr", [128, 8, 128], BF16)
            ps_f = [ps("ps_f%d" % i, [128, 512]) for i in range(2)]
            ps_t = [ps("ps_t%d" % i, [128, 512]) for i in range(2)]
            ps_g1 = ps("ps_g1", [128, 512])
            ps_g2 = ps("ps_g2", [128, 4, 128])
            ps_g3 = ps("ps_g3", [128, 4, 128])

            S.op("pool", lambda e: e.memset(lrT[:], 1.0), writes=[lrT])
            S.op("pool", lambda e: e.memset(Sst[:], 0.0), writes=[Sst])
            S.op("pool", lambda e: e.memset(qz[:], 0.0), writes=[qz])
            S.op("pool", lambda e: e.memset(Sbf[:], 0.0), writes=[Sbf])

            fchunks = ([(C_QS + 128 * i, 128, ("swa", i)) for i in range(4)] +
                       [(C_KS + 128 * i, 128, ("swa", 4 + i)) for i in range(4)] +
                       [(C_VS + 128 * i, 128, ("swa", 8 + i)) for i in range(4)] +
                       [(C_QG + 128 * i, 128, ("gla", i)) for i in range(2)] +
                       [(C_KG + 128 * i, 128, ("gla", 2 + i)) for i in range(2)] +
                       [(C_R + 128 * i, 128, ("glar", 4 + i)) for i in range(4)] +
                       [(C_LR, 16, ("lr", 0))])
            fcount = [0]
            tcount = [0]
            evc = [0]

            def gla_chunk(C, col0, ktk, vtk, spk, mixdst, mcol0):
                S.op("pe", lambda e: e.matmul(ps_g1[0:C, 256:512], lhsT=lrem[0:C, 0:C], rhs=spk, start=True, stop=True),
                     [lrem, sp_t], [ps_g1])
                for j in range(2):
                    S.op("pe", lambda e, j=j: e.matmul(ps_g2[:, j, 0:C], lhsT=spk[:, 128 * j:128 * j + 128],
                                                        rhs=ucum[0:C, 0:C], start=True, stop=True),
                         [ucum, sp_t], [ps_g2])
                S.op("act", lambda e: e.activation(out=eq[:, :, 0:C], in_=ps_g2[:, 0:2, 0:C], func=AF.Exp), [ps_g2], [eq])
                S.op("act", lambda e: e.activation(out=ek[:, :, 0:C], in_=ps_g2[:, 0:2, 0:C], func=AF.Exp, scale=-1.0),
                     [ps_g2], [ek])
                S.op("act", lambda e: e.activation(out=ed[0:C, :], in_=ps_g1[0:C, 256:512], func=AF.Exp), [ps_g1], [ed])
                S.stage()
                S.op("dve", lambda e: e.scalar_tensor_tensor(out=qz[0:64, 0:4:2, 0:C], in0=eq[0:64, :, 0:C], scalar=0.125,
                                                              in1=glaT[0:64, 0:2, col0:col0 + C], op0=ALU.mult, op1=ALU.mult),
                     [eq, glaT], [qz])
                S.op("dve", lambda e: e.scalar_tensor_tensor(out=qz[64:128, 1:4:2, 0:C], in0=eq[64:128, :, 0:C], scalar=0.125,
                                                              in1=glaT[64:128, 0:2, col0:col0 + C], op0=ALU.mult, op1=ALU.mult),
                     [eq, glaT], [qz])
                S.op("pool", lambda e: e.tensor_tensor(out=ktl[:, :, 0:C], in0=ek[:, :, 0:C],
                                                       in1=glaT[:, 2:4, col0:col0 + C], op=ALU.mult), [ek, glaT], [ktl])
                S.op("dve", lambda e: e.tensor_tensor(out=khat[0:C, :], in0=ed[0:C, :], in1=ktk, op=ALU.mult),
                     [ed, ktok], [khat])
                S.stage()
                for h in range(4):
                    j = h // 2
                    S.op("pe", lambda e, h=h, j=j: e.matmul(ps_g2[0:C, h, 0:C], lhsT=ktl[:, j, 0:C],
                                                             rhs=qz[:, h, 0:C], start=True, stop=True),
                         [ktl, qz], [ps_g2])
                S.op("dve", lambda e: e.tensor_tensor(out=Abf[0:C, :, 0:C], in0=ps_g2[0:C, :, 0:C], in1=caus[0:C, :, 0:C],
                                                      op=ALU.mult), [ps_g2, caus], [Abf])
                S.stage()
                for h in range(4):
                    j, b0 = h // 2, 64 * (h % 2)
                    S.op("pe", lambda e, h=h: e.matmul(ps_g3[:, h, 0:C], lhsT=vtk[:, 128 * h:128 * h + 128],
                                                        rhs=Abf[0:C, h, 0:C], start=True, stop=False), [vtok, Abf], [ps_g3])
                    S.op("pe", lambda e, h=h, j=j: e.matmul(ps_g3[:, h, 0:C], lhsT=Sbf[:, j, :],
                                                             rhs=qz[:, h, 0:C], start=False, stop=True),
                         [Sbf, qz], [ps_g3])
                S.stage()
                for h in range(4):
                    j = h // 2
                    S.op("pe", lambda e, h=h, j=j: e.matmul(ps_g2[:, h, :], lhsT=khat[0:C, 128 * j:128 * j + 128],
                                                             rhs=vtk[:, 128 * h:128 * h + 128], start=True, stop=True),
                         [khat, vtok], [ps_g2])
                for h in range(4):
                    j, b0 = h // 2, 64 * (h % 2)
                    S.op("dve", lambda e, h=h, j=j, b0=b0: e.scalar_tensor_tensor(
                        out=Sst[b0:b0 + 64, j, :], in0=Sst[b0:b0 + 64, j, :], scalar=eq[b0:b0 + 64, j, C - 1:C],
                        in1=ps_g2[b0:b0 + 64, h, :], op0=ALU.mult, op1=ALU.add), [Sst, eq, ps_g2], [Sst])
                S.stage()
                S.op("act", lambda e: e.activation(out=osq[:, :, 0:C], in_=ps_g3[:, :, 0:C], func=AF.Square), [ps_g3], [osq])
                S.op("pool", lambda e: e.tensor_copy(out=Sbf[:], in_=Sst[:]), [Sst], [Sbf])
                for h in range(4):
                    S.op("pe", lambda e, h=h: e.matmul(ps_g1[:, 128 * h:128 * h + C],
                                                        lhsT=onesb[:, :], rhs=osq[:, h, 0:C], start=True, stop=True),
                         [onesb, osq], [ps_g1])
                psv = ps_g1[:, :].rearrange("p (h c) -> p h c", h=4)
                S.op("act", lambda e: e.activation(out=rs_g[:, :, 0:C], in_=psv[:, :, 0:C], func=AF.Ln, scale=1.0 / 128,
                                                   bias=eps_t[:]), [ps_g1, eps_t], [rs_g])
                S.op("act", lambda e: e.activation(out=rs_g[:, :, 0:C], in_=rs_g[:, :, 0:C], func=AF.Exp, scale=-0.5),
                     [rs_g], [rs_g])
                S.op("dve", lambda e: e.tensor_tensor(out=rs_g[:, :, 0:C], in0=ps_g3[:, :, 0:C], in1=rs_g[:, :, 0:C],
                                                      op=ALU.mult), [ps_g3, rs_g], [rs_g])
                S.op("pool", lambda e: e.tensor_tensor(out=mixdst[:, :, mcol0:mcol0 + C], in0=rs_g[:, :, 0:C],
                                                       in1=glaT[:, 4:8, col0:col0 + C], op=ALU.mult),
                     [rs_g, glaT], [mixdst])

            nblk = T // 512 + 1
            preloaded = set()

            def load_x(blk_, i_):
                s__ = blk_ == nblk - 1
                src_ = xs if s__ else xp
                r_ = (0 if s__ else 512 * blk_) + 128 * i_
                x__ = xt[i_ % 2]
                S.dma("sync", "x%d" % (i_ % 2), lambda e: e.dma_start(out=x__[:], in_=src_[r_:r_ + 128, :]), writes=[x__])

            DBG_NB = int(os.environ.get('K_NBLK', '99'))
            DBG_S = int(os.environ.get('K_SAMPLE', '1'))
            for blk in range(nblk):
                is_s = blk == nblk - 1
                if (not is_s and blk >= DBG_NB) or (is_s and not DBG_S):
                    continue
                NT = NS if is_s else 512
                tok0 = T if is_s else 512 * blk
                xsrc = xs if is_s else xp
                row0 = 0 if is_s else tok0
                ntile = NT // 128
                sw = swaT[blk % 2]
                mg = mixg[blk % 2]
                for i in range(ntile):
                    x_ = xt[i % 2]
                    if (blk, i) not in preloaded:
                        load_x(blk, i)
                    rms_rstd(x_, ssq[i % 2], rstd[i % 2], junk, D)
                    xn_ = xn[i % 2]
                    S.op("act", lambda e, x_=x_, xn_=xn_, r_=rstd[i % 2]: e.activation(out=xn_[:], in_=x_[:], func=AF.Copy, scale=r_[:]),
                         [x_, rstd[i % 2]], [xn_])
                    for k in range(8):
                        S.op("pe", lambda e, k=k, xn_=xn_: e.transpose(out=ps_tr[:, k, :], in_=xn_[:, 128 * k:128 * k + 128],
                                                                        identity=identb[:]), [xn_, identb], [ps_tr])
                    S.op("dve", lambda e, i=i: e.tensor_copy(out=xnT[:, :, 128 * i:128 * i + 128], in_=ps_tr[:]), [ps_tr], [xnT])
                S.stage()
                for (c0, wd, (kind, ci)) in fchunks:
                    pf = ps_f[fcount[0] % 2]
                    fcount[0] += 1
                    for k in range(8):
                        S.op("pe", lambda e, pf=pf, c0=c0, wd=wd, k=k: e.matmul(pf[0:wd, 0:NT], lhsT=Wb[:, k, c0:c0 + wd],
                                                                                 rhs=xnT[:, k, 0:NT], start=(k == 0), stop=(k == 7)),
                             [Wb, xnT], [pf])
                    evc[0] += 1
                    if kind == "swa":
                        eng = "act" if evc[0] % 2 == 0 else "dve"
                        if eng == "act":
                            S.op("act", lambda e, pf=pf, ci=ci: e.activation(out=sw[:, ci, 0:NT], in_=pf[:, 0:NT], func=AF.Copy), [pf], [sw])
                        else:
                            S.op("dve", lambda e, pf=pf, ci=ci: e.tensor_copy(out=sw[:, ci, 0:NT], in_=pf[:, 0:NT]), [pf], [sw])
                    elif kind == "gla":
                        S.op("dve", lambda e, pf=pf, ci=ci: e.tensor_copy(out=glaT[:, ci, 0:NT], in_=pf[:, 0:NT]), [pf], [glaT])
                    elif kind == "glar":
                        S.op("act", lambda e, pf=pf, ci=ci: e.activation(out=glaT[:, ci, 0:NT], in_=pf[:, 0:NT], func=AF.Silu), [pf], [glaT])
                    else:
                        S.op("dve", lambda e, pf=pf: e.tensor_copy(out=lrT[0:16, 0:NT], in_=pf[0:16, 0:NT]), [pf], [lrT])
                S.stage()
                S.dma("sync", slot("sp", 2), lambda e, sw=sw, tok0=tok0, NT=NT: e.dma_start(
                    out=projT[:, tok0:tok0 + NT].rearrange("(c p) t -> p c t", p=128), in_=sw[:, :, 0:NT]), reads=[sw], writes=[projT_t])
                S.stage()
                for i in range(ntile):
                    cols = slice(128 * i, 128 * i + 128)
                    pt = ps_t[tcount[0] % 2]; tcount[0] += 1
                    for k in range(8):
                        S.op("pe", lambda e, pt=pt, k=k, cols=cols: e.matmul(pt[:, 0:256], lhsT=xnT[:, k, cols], rhs=Wb[:, k, C_KG:C_KG + 256],
                                                                            start=(k == 0), stop=(k == 7)), [xnT, Wb], [pt])
                    S.op("dve", lambda e, pt=pt, i=i: e.tensor_copy(out=ktok[:, i, :], in_=pt[:, 0:256]), [pt], [ktok])
                    pt = ps_t[tcount[0] % 2]; tcount[0] += 1
                    for k in range(8):
                        S.op("pe", lambda e, pt=pt, k=k, cols=cols: e.matmul(pt[:, :], lhsT=xnT[:, k, cols], rhs=Wb[:, k, C_VG:C_VG + 512],
                                                                            start=(k == 0), stop=(k == 7)), [xnT, Wb], [pt])
                    S.op("act", lambda e, pt=pt, i=i: e.activation(out=vtok[:, i, :], in_=pt[:, :], func=AF.Copy), [pt], [vtok])
                    pt = ps_t[tcount[0] % 2]; tcount[0] += 1
                    S.op("pe", lambda e, pt=pt, cols=cols: e.matmul(pt[:, 0:256], lhsT=lrT[0:17, cols], rhs=wg_t[:, :], start=True, stop=True),
                         [lrT, wg_t], [pt])
                    S.op("act", lambda e, pt=pt: e.activation(out=eg[:], in_=pt[:, 0:256], func=AF.Exp, scale=-1.0), [pt], [eg])
                    S.op("act", lambda e, i=i: e.activation(out=sp_t[:, i, :], in_=eg[:], func=AF.Ln, bias=one_t[:]), [eg, one_t], [sp_t])
                    if is_s or tok0 + 128 * i >= T - 2048:
                        for (cc, dst) in ((C_KS, ck_s if is_s else ck_p), (C_VS, cv_s if is_s else cv_p)):
                            pt = ps_t[tcount[0] % 2]; tcount[0] += 1
                            for k in range(8):
                                S.op("pe", lambda e, pt=pt, k=k, cols=cols, cc=cc: e.matmul(pt[:, :], lhsT=xnT[:, k, cols], rhs=Wb[:, k, cc:cc + 512],
                                                                                           start=(k == 0), stop=(k == 7)), [xnT, Wb], [pt])
                            cs_ = cst[tcount[0] % 2]
                            S.op("dve", lambda e, pt=pt, cs_=cs_: e.tensor_copy(out=cs_[:], in_=pt[:, :]), [pt], [cs_])
                            r = 128 * i if is_s else tok0 + 128 * i - (T - 2048)
                            S.dma("sync", slot("co", 2), lambda e, cs_=cs_, dst=dst, r=r: e.dma_start(out=dst[r:r + 128, :], in_=cs_[:]),
                                  reads=[cs_], is_out=True)
                S.stage()
                nxt = blk + 1
                if nxt < nblk and not ((nxt != nblk - 1 and nxt >= DBG_NB) or (nxt == nblk - 1 and not DBG_S)):
                    for i_ in range(2 if nxt != nblk - 1 else 1):
                        load_x(nxt, i_)
                        preloaded.add((nxt, i_))
                if not is_s:
                    for i in range(ntile):
                        gla_chunk(128, 128 * i, ktok[:, i, :], vtok[:, i, :], sp_t[:, i, :], mg, 128 * i)
                    if blk == nblk - 2:
                        for j in range(2):
                            S.dma("sync", "stp%d" % j, lambda e, j=j: e.dma_start(
                                out=st_p[2 * j:2 * j + 2].rearrange("h k v -> (h k) v"), in_=Sst[:, j, :]), reads=[Sst], is_out=True)
                else:
                    for b in range(NSEQ):
                        cols = slice(8 * b, 8 * b + 8)
                        pt = ps_t[tcount[0] % 2]; tcount[0] += 1
                        for k in range(8):
                            S.op("pe", lambda e, pt=pt, k=k, cols=cols: e.matmul(pt[0:8, 0:256], lhsT=xnT[:, k, cols], rhs=Wb[:, k, C_KG:C_KG + 256],
                                                                                start=(k == 0), stop=(k == 7)), [xnT, Wb], [pt])
                        S.op("dve", lambda e, pt=pt: e.tensor_copy(out=ktok[0:8, 0, :], in_=pt[0:8, 0:256]), [pt], [ktok])
                        pt = ps_t[tcount[0] % 2]; tcount[0] += 1
                        for k in range(8):
                            S.op("pe", lambda e, pt=pt, k=k, cols=cols: e.matmul(pt[0:8, :], lhsT=xnT[:, k, cols], rhs=Wb[:, k, C_VG:C_VG + 512],
                                                                                start=(k == 0), stop=(k == 7)), [xnT, Wb], [pt])
                        S.op("act", lambda e, pt=pt: e.activation(out=vtok[0:8, 0, :], in_=pt[0:8, :], func=AF.Copy), [pt], [vtok])
                        pt = ps_t[tcount[0] % 2]; tcount[0] += 1
                        for k in range(8):
                            S.op("pe", lambda e, pt=pt, k=k, cols=cols: e.matmul(pt[0:8, :], lhsT=xnT[:, k, cols], rhs=Wb[:, k, C_VS:C_VS + 512],
                                                                                start=(k == 0), stop=(k == 7)), [xnT, Wb], [pt])
                        S.op("act", lambda e, pt=pt: e.activation(out=vnew_st[:], in_=pt[0:8, :], func=AF.Copy), [pt], [vnew_st])
                        S.dma("sync", slot("vn", 2), lambda e, b=b: e.dma_start(out=vnew[b], in_=vnew_st[:]), reads=[vnew_st], writes=[vnew_t])
                        pt = ps_t[tcount[0] % 2]; tcount[0] += 1
                        S.op("pe", lambda e, pt=pt, cols=cols: e.matmul(pt[0:8, 0:256], lhsT=lrT[0:17, cols], rhs=wg_t[:, :], start=True, stop=True),
                             [lrT, wg_t], [pt])
                        S.op("act", lambda e, pt=pt: e.activation(out=eg[0:8, :], in_=pt[0:8, 0:256], func=AF.Exp, scale=-1.0), [pt], [eg])
                        S.op("act", lambda e: e.activation(out=sp_t[0:8, 0, :], in_=eg[0:8, :], func=AF.Ln, bias=one_t[0:8, :]), [eg, one_t], [sp_t])
                        for j in range(2):
                            S.dma("sync", "sti%d" % j, lambda e, j=j, b=b: e.dma_start(
                                out=Sst[:, j, :], in_=st_in[b, 2 * j:2 * j + 2].rearrange("h k v -> (h k) v")), writes=[Sst])
                        S.op("pool", lambda e: e.tensor_copy(out=Sbf[:], in_=Sst[:]), [Sst], [Sbf])
                        gla_chunk(8, 8 * b, ktok[0:8, 0, :], vtok[0:8, 0, :], sp_t[0:8, 0, :], mg, 8 * b)
                        for j in range(2):
                            S.dma("sync", "sto%d" % j, lambda e, j=j, b=b: e.dma_start(
                                out=st_s[b, 2 * j:2 * j + 2].rearrange("h k v -> (h k) v"), in_=Sst[:, j, :]), reads=[Sst], is_out=True)
                S.dma("sync", slot("mg", 2), lambda e, mg=mg, tok0=tok0, NT=NT: e.dma_start(
                    out=mixT[0:512, tok0:tok0 + NT].rearrange("(c p) t -> p c t", p=128), in_=mg[:, :, 0:NT]), reads=[mg], writes=[mixT_t])

        DBG_P2 = int(os.environ.get('K_P2', '1'))
        DBG_P2S = int(os.environ.get('K_P2S', '1'))
        DBG_P3 = int(os.environ.get('K_P3', '1'))
        p2 = contextlib.ExitStack()
        with p2:
            sb, ps = mk(p2)
            SKEW = int(os.environ.get('K_SKEW', '2'))
            Kc = sb("Kc", [128, T], BF16)
            Vc = sb("Vc", [128, T], BF16)
            Qzz = sb("Qzz", [128, 2, T], BF16)
            cmf = sb("cmf", [128, 6, 256])
            cmb = sb("cmb", [128, 6, 256], BF16)
            Acc = [sb("Acc%d" % i, [65, 2, 2048]) for i in range(2)]
            Vt = [sb("Vt%d" % i, [128, 2, 2, 65], BF16) for i in range(SKEW + 2)]
            Pe = [sb("Pe%d" % i, [128, 2, 256], BF16) for i in range(SKEW + 1)]
            Pm = [sb("Pm%d" % i, [128, 2, 256], BF16) for i in range(SKEW + 1)]
            rden = sb("rden", [1, 2048])
            oTs = [sb("oTs%d" % i, [128, 2048], BF16) for i in range(2)]
            ps_b = ps("ps_b", [128, 512])
            ps_S = [ps("ps_S%d" % i, [128, 2, 256]) for i in range(SKEW + 1)]
            ps_O = [ps("ps_O%d" % i, [128, 4, 128]) for i in range(2)]
            ps_vt = [ps("ps_vt%d" % i, [128, 8, 128], BF16) for i in range(2)]
            for v_ in Vt:
                S.op("pool", lambda e, v_=v_: e.memset(v_[:], 1.0), writes=[v_])
            gcnt = [0]
            for c in range(4 if DBG_P2 else 0):
                S.dma("sync", "ld0", lambda e: e.dma_start(out=Kc[:], in_=projT[(4 + c) * 128:(5 + c) * 128, 0:T]), reads=[projT_t], writes=[Kc])
                S.dma("sync", "ld1", lambda e: e.dma_start(out=Vc[:], in_=projT[(8 + c) * 128:(9 + c) * 128, 0:T]), reads=[projT_t], writes=[Vc])
                S.dma("sync", "ld2", lambda e: e.dma_start(out=Qzz[:, 0, :], in_=projT[c * 128:(c + 1) * 128, 0:T]), reads=[projT_t], writes=[Qzz])
                S.dma("act", "ld3", lambda e: e.dma_start(out=Qzz[:, 1, :], in_=projT[c * 128:(c + 1) * 128, 0:T]), reads=[projT_t], writes=[Qzz])
                S.op("pool", lambda e: e.memset(Qzz[64:128, 0, :], 0.0), [Qzz], [Qzz])
                S.op("pool", lambda e: e.memset(Qzz[0:64, 1, :], 0.0), [Qzz], [Qzz])
                for p_ in range(3):
                    S.dma("sync", "ld4", lambda e: e.dma_start(out=cmf[:, 2 * p_:2 * p_ + 2, :],
                                                               in_=c_cmask[p_, 2 * c:2 * c + 2].rearrange("h k q -> k h q")), writes=[cmf])
                S.op("dve", lambda e: e.tensor_copy(out=cmb[:], in_=cmf[:]), [cmf], [cmb])
                qlist = []
                for SBi in range(4):
                    base = 2048 * SBi
                    qbs = []
                    for i in range(16):
                        qbs.append((0, base + 128 * i, 1, (base + 128 * i - 128) if base + 128 * i >= 128 else None))
                    for g in range(4):
                        for r in range(4):
                            q0 = base + 512 * g + r
                            qbs.append((1, q0, 4, (q0 - 512) if q0 >= 512 else None))
                    for r in range(16):
                        q0 = base + r
                        qbs.append((2, q0, 16, (q0 - 2048) if q0 >= 2048 else None))
                    for qi_, (p, q0, d, pv) in enumerate(qbs):
                        qlist.append((SBi, p, q0, d, pv, qi_ == len(qbs) - 1))
                gbase = gcnt[0]
                gcnt[0] += len(qlist)

                def stageA(idx):
                    (SBi, p, q0, d, pv, last) = qlist[idx]
                    gi = gbase + idx
                    vt, pvt, pS = Vt[gi % (SKEW + 2)], ps_vt[gi % 2], ps_S[gi % (SKEW + 1)]
                    pe_, pm_ = Pe[gi % (SKEW + 1)], Pm[gi % (SKEW + 1)]
                    if pv is not None:
                        S.op("pe", lambda e: e.matmul(pS[:, :, 0:128], lhsT=Kc[:, ss(pv, 128, d)], rhs=Qzz[:, :, ss(q0, 128, d)],
                                                      start=True, stop=True), [Kc, Qzz], [pS])
                    S.op("pe", lambda e: e.matmul(pS[:, :, 128:256], lhsT=Kc[:, ss(q0, 128, d)], rhs=Qzz[:, :, ss(q0, 128, d)],
                                                  start=True, stop=True), [Kc, Qzz], [pS])
                    c0 = 0 if pv is not None else 128
                    S.op("act", lambda e: e.activation(out=pe_[:, :, c0:256], in_=pS[:, :, c0:256], func=AF.Exp, scale=0.125), [pS], [pe_])
                    if pv is not None:
                        S.op("pe", lambda e: e.transpose(out=pvt[:, 0, :], in_=Vc[:, ss(pv, 128, d)], identity=identb[:]), [Vc, identb], [pvt])
                    S.op("pe", lambda e: e.transpose(out=pvt[:, 1, :], in_=Vc[:, ss(q0, 128, d)], identity=identb[:]), [Vc, identb], [pvt])
                    u0 = 0 if pv is not None else 1
                    src = pvt[:, u0:2, :].rearrange("p u (h e) -> p u h e", h=2)
                    S.op("act", lambda e: e.activation(out=vt[:, u0:2, :, 0:64], in_=src, func=AF.Copy), [pvt], [vt])
                    S.op("pool" if gi % 3 == 0 else "dve", lambda e: e.tensor_tensor(out=pm_[:, :, c0:256], in0=pe_[:, :, c0:256],
                                                                                     in1=cmb[:, 2 * p:2 * p + 2, c0:256], op=ALU.mult), [pe_, cmb], [pm_])

                def stageB(idx):
                    (SBi, p, q0, d, pv, last) = qlist[idx]
                    gi = gbase + idx
                    base = 2048 * SBi
                    acc = Acc[SBi % 2]
                    vt, pm_, pO = Vt[gi % (SKEW + 2)], Pm[gi % (SKEW + 1)], ps_O[gi % 2]
                    for hh in range(2):
                        if pv is not None:
                            S.op("pe", lambda e: e.matmul(pO[0:65, hh, :], lhsT=vt[:, 0, hh, :], rhs=pm_[:, hh, 0:128], start=True, stop=False),
                                 [vt, pm_], [pO])
                        S.op("pe", lambda e: e.matmul(pO[0:65, hh, :], lhsT=vt[:, 1, hh, :], rhs=pm_[:, hh, 128:256], start=(pv is None), stop=True),
                             [vt, pm_], [pO])
                    dst = acc[0:65, :, ss(q0 - base, 128, d)]
                    if p == 0:
                        S.op("dve", lambda e: e.tensor_copy(out=dst, in_=pO[0:65, 0:2, :]), [pO], [acc])
                    else:
                        S.op("dve", lambda e: e.tensor_tensor(out=dst, in0=pO[0:65, 0:2, :], in1=dst, op=ALU.add), [pO, acc], [acc])
                    if last and not os.environ.get('K_NONORM'):
                        ot = oTs[SBi % 2]
                        for hh in range(2):
                            S.op("dve", lambda e: e.tensor_copy(out=rden[0:1, :], in_=acc[64:65, hh, :]), [acc], [rden])
                            S.op("act", lambda e: e.activation(out=rden[0:1, :], in_=rden[0:1, :], func=AF.Ln), [rden], [rden])
                            S.op("act", lambda e: e.activation(out=rden[0:1, :], in_=rden[0:1, :], func=AF.Exp, scale=-1.0), [rden], [rden])
                            for cb in range(4):
                                cs = slice(512 * cb, 512 * cb + 512)
                                S.op("pe", lambda e: e.matmul(ps_b[0:64, :], lhsT=onesf[0:1, 0:64], rhs=rden[0:1, cs], start=True, stop=True),
                                     [onesf, rden], [ps_b])
                                S.op("dve", lambda e: e.tensor_tensor(out=ot[64 * hh:64 * hh + 64, cs], in0=acc[0:64, hh, cs], in1=ps_b[0:64, :],
                                                                      op=ALU.mult), [acc, ps_b], [ot])
                        S.dma("sync", slot("ao", 2), lambda e: e.dma_start(out=mixT[(4 + c) * 128:(5 + c) * 128, base:base + 2048], in_=ot[:]),
                              reads=[ot], writes=[mixT_t])

                nq = len(qlist)
                for i in range(nq + SKEW):
                    if i < nq:
                        stageA(i)
                    if i >= SKEW:
                        stageB(i - SKEW)

        p2b = contextlib.ExitStack()
        with p2b:
            sb, ps = mk(p2b)
            qsT_s = sb("qsT_s", [128, 4, 128], BF16)
            ksT_s = sb("ksT_s", [128, 4, 128], BF16)
            Qbd = sb("Qbd", [128, 4, 16], BF16)
            ctab = sb("ctab", [128, 17, 64])
            bmask = sb("bmask", [64, 512])
            self_ = sb("self", [64, 8])
            selb = sb("selb", [64, 8], BF16)
            NB2 = 3
            SK2 = 2
            Kf = [sb("Kf%d" % i, [128, 512]) for i in range(NB2)]
            Vf = [sb("Vf%d" % i, [128, 512]) for i in range(NB2)]
            Kb = [sb("Kb%d" % i, [128, 512], BF16) for i in range(NB2)]
            Vb = [sb("Vb%d" % i, [128, 512], BF16) for i in range(NB2 + 1)]
            kTt = [sb("kTt%d" % i, [128, 4, 128], BF16) for i in range(NB2)]
            Pes = [sb("Pes%d" % i, [128, 64]) for i in range(NB2)]
            Pss = [sb("Pss%d" % i, [128, 64], BF16) for i in range(NB2)]
            Qbd2 = [Qbd, sb("Qbd1", [128, 4, 16], BF16)]
            vnb2 = [sb("vnb%d" % i, [8, 512], BF16) for i in range(2)]
            rdens = sb("rdens", [64, 1])
            masked = sb("masked", [64, 512], BF16)
            oT_s = sb("oT_s", [128, 4, 128], BF16)
            ps_kT = [ps("ps_kT%d" % i, [128, 8, 128], BF16) for i in range(2)]
            ps_s = [ps("ps_s%d" % i, [128, 512]) for i in range(3)]
            ps_o = ps("ps_o", [128, 512])
            ps_den = ps("ps_den", [128, 512])
            ps_r = ps("ps_r", [128, 4, 128])
            if DBG_P2S:
                S.dma("sync", "ld0", lambda e: e.dma_start(out=qsT_s[:], in_=projT[0:512, T:T + NS].rearrange("(c p) t -> p c t", p=128)),
                      reads=[projT_t], writes=[qsT_s])
                S.dma("sync", "ld1", lambda e: e.dma_start(out=ksT_s[:], in_=projT[512:1024, T:T + NS].rearrange("(c p) t -> p c t", p=128)),
                      reads=[projT_t], writes=[ksT_s])
                S.dma("sync", "ld2", lambda e: e.dma_start(out=ctab[:], in_=c_ctab[:, :, :]), writes=[ctab])
                S.dma("sync", "ld3", lambda e: e.dma_start(out=bmask[:], in_=c_bmask[:, :]), writes=[bmask])
                S.dma("sync", "ld4", lambda e: e.dma_start(out=self_[:], in_=c_sel[:, :]), writes=[self_])
                S.op("dve", lambda e: e.tensor_copy(out=selb[:], in_=self_[:]), [self_], [selb])
                S.op("pool", lambda e: e.memset(Qbd2[0][:], 0.0), writes=[Qbd2[0]])
                S.op("pool", lambda e: e.memset(Qbd2[1][:], 0.0), writes=[Qbd2[1]])
            tiles = [(b_, kt_) for b_ in range(NSEQ if DBG_P2S else 0) for kt_ in range(17)]

            def sA(idx):
                b, kt = tiles[idx]
                s_ = idx % NB2
                qb_ = Qbd2[b % 2]
                pz = ps_s[idx % 3]
                if kt == 0:
                    S.op("dve", lambda e: e.tensor_copy(out=qb_[0:64, :, 0:8], in_=qsT_s[0:64, :, 8 * b:8 * b + 8]), [qsT_s], [qb_])
                    S.op("dve", lambda e: e.tensor_copy(out=qb_[64:128, :, 8:16], in_=qsT_s[64:128, :, 8 * b:8 * b + 8]), [qsT_s], [qb_])
                    S.dma("sync", "vnl%d" % (b % 2), lambda e: e.dma_start(out=vnb2[b % 2][:], in_=vnew[b]), reads=[vnew_t], writes=[vnb2[b % 2]])
                if kt < 16:
                    vb_ = Vb[idx % (NB2 + 1)]
                    S.dma("sync", "kl%d" % s_, lambda e: e.dma_start(out=Kf[s_][:], in_=ck[b, 128 * kt:128 * kt + 128, :]), writes=[Kf[s_]])
                    S.dma("act", "vl%d" % s_, lambda e: e.dma_start(out=Vf[s_][:], in_=cv[b, 128 * kt:128 * kt + 128, :]), writes=[Vf[s_]])
                    S.op("dve", lambda e: e.tensor_copy(out=Kb[s_][:], in_=Kf[s_][:]), [Kf[s_]], [Kb[s_]])
                    S.op("pool", lambda e: e.tensor_copy(out=vb_[:], in_=Vf[s_][:]), [Vf[s_]], [vb_])
                    pk = ps_kT[idx % 2]
                    for cc in range(4):
                        S.op("pe", lambda e: e.transpose(out=pk[:, cc, :], in_=Kb[s_][:, 128 * cc:128 * cc + 128], identity=identb[:]),
                             [Kb[s_], identb], [pk])
                    S.op("act", lambda e: e.activation(out=kTt[s_][:], in_=pk[:, 0:4, :], func=AF.Copy), [pk], [kTt[s_]])
                    for cc in range(4):
                        S.op("pe", lambda e: e.matmul(pz[:, 16 * cc:16 * cc + 16], lhsT=kTt[s_][:, cc, :], rhs=qb_[:, cc, :], start=True, stop=True),
                             [kTt[s_], qb_], [pz])
                else:
                    for cc in range(4):
                        S.op("pe", lambda e: e.matmul(pz[0:8, 16 * cc:16 * cc + 16], lhsT=ksT_s[:, cc, 8 * b:8 * b + 8], rhs=qb_[:, cc, :],
                                                      start=True, stop=True), [ksT_s, qb_], [pz])

            def sB(idx):
                b, kt = tiles[idx]
                s_ = idx % NB2
                pz = ps_s[idx % 3]
                pss, pes = Pss[s_], Pes[s_]
                if kt < 16:
                    np_ = 128
                    vtt = Vb[idx % (NB2 + 1)]
                    vrhs = vtt[:, :]
                else:
                    np_ = 8
                    vtt = vnb2[b % 2]
                    vrhs = vtt[0:8, :]
                S.op("act", lambda e: e.activation(out=pes[0:np_, :], in_=pz[0:np_, 0:64], func=AF.Exp, scale=0.125), [pz], [pes])
                S.op("dve", lambda e: e.tensor_tensor(out=pss[0:np_, :], in0=pes[0:np_, :], in1=ctab[0:np_, kt, :], op=ALU.mult), [pes, ctab], [pss])
                S.op("pe", lambda e: e.matmul(ps_o[0:64, :], lhsT=pss[0:np_, :], rhs=vrhs, start=(kt == 0), stop=(kt == 16)), [pss, vtt], [ps_o])
                S.op("pe", lambda e: e.matmul(ps_den[0:64, 0:2], lhsT=pss[0:np_, :], rhs=onesb[0:np_, 0:2], start=(kt == 0), stop=(kt == 16)),
                     [pss, onesb], [ps_den])
                if kt == 16:
                    S.op("dve", lambda e: e.reciprocal(out=rdens[:], in_=ps_den[0:64, 0:1]), [ps_den], [rdens])
                    S.op("dve", lambda e: e.scalar_tensor_tensor(out=masked[:], in0=ps_o[0:64, :], scalar=rdens[:, 0:1], in1=bmask[:],
                                                                  op0=ALU.mult, op1=ALU.mult), [ps_o, rdens, bmask], [masked])
                    for cc in range(4):
                        S.op("pe", lambda e: e.matmul(ps_r[:, cc, 0:8], lhsT=masked[:, 128 * cc:128 * cc + 128], rhs=selb[:, :], start=True, stop=True),
                             [masked, selb], [ps_r])
                    S.op("act", lambda e: e.activation(out=oT_s[:, :, 8 * b:8 * b + 8], in_=ps_r[:, :, 0:8], func=AF.Copy), [ps_r], [oT_s])

            for i in range(len(tiles) + SK2 if tiles else 0):
                if i < len(tiles):
                    sA(i)
                if i >= SK2:
                    sB(i - SK2)
            if DBG_P2S:
                S.dma("sync", "aos", lambda e: e.dma_start(out=mixT[512:1024, T:T + NS].rearrange("(c p) t -> p c t", p=128), in_=oT_s[:]),
                      reads=[oT_s], writes=[mixT_t])

        p3 = contextlib.ExitStack()
        with p3:
            sb, ps = mk(p3)
            NB = 256
            Wg = sb("Wg", [128, 8, DFF], BF16)
            Wu = sb("Wu", [128, 8, DFF], BF16)
            Wd = sb("Wd", [128, NF, D], BF16)
            Wo = sb("Wo", [128, 8, D], BF16)
            wst3 = sb("wst3", [128, DFF])
            gffn_t = sb("gffn_t", [128, 8])
            ggla_t = sb("ggla_t", [128, 4])
            gfin_t = sb("gfin_t", [128, D])
            mT = sb("mT", [128, 8, NB], BF16)
            hres2 = [sb("hres%d" % i, [128, 2, D]) for i in range(2)]
            junk3 = sb("junk3", [128, D], BF16)
            ssq3 = [sb("ssq3%d" % i, [128, 1]) for i in range(2)]
            rstd3 = [sb("rstd3%d" % i, [128, 1]) for i in range(2)]
            hn = [sb("hn%d" % i, [128, D], BF16) for i in range(2)]
            hnT = sb("hnT", [128, 8, NB], BF16)
            actT = sb("actT", [128, NF, NB], BF16)
            sg = [sb("sg%d" % i, [128, NB]) for i in range(2)]
            yo = wst3
            ps_tr3 = ps("ps_tr3", [128, 8, 128], BF16)
            ps_m = [ps("ps_m%d" % i, [128, 512]) for i in range(2)]
            ps_gg = [ps("ps_gg%d" % i, [128, 512]) for i in range(2)]
            ps_uu = [ps("ps_uu%d" % i, [128, 512]) for i in range(2)]
            if DBG_P3:
                S.dma("sync", "c1", lambda e: e.dma_start(out=gffn_t[:], in_=gffn[:, :]), writes=[gffn_t])
                S.dma("sync", "c2", lambda e: e.dma_start(out=ggla_t[:], in_=ggla[:, :]), writes=[ggla_t])
                S.dma("sync", "c3", lambda e: e.dma_start(out=gfin_t[:], in_=gfin.partition_broadcast(128)), writes=[gfin_t])
                wi = 0
                for (wsrc, wdst) in ((w_fg, Wg), (w_fu, Wu)):
                    for k in range(8):
                        S.dma("sync", "w0", lambda e: e.dma_start(out=wst3[:], in_=wsrc[128 * k:128 * k + 128, :]), writes=[wst3])
                        if wi % 2 == 0:
                            S.op("act", lambda e: e.activation(out=wdst[:, k, :], in_=wst3[:], func=AF.Copy, scale=gffn_t[:, k:k + 1]), [wst3, gffn_t], [wdst])
                        else:
                            S.op("dve", lambda e: e.tensor_scalar(out=wdst[:, k, :], in0=wst3[:], scalar1=gffn_t[:, k:k + 1], scalar2=None, op0=ALU.mult),
                                 [wst3, gffn_t], [wdst])
                        wi += 1
                for f2 in range(NF // 2):
                    S.dma("sync", "w0", lambda e: e.dma_start(out=wst3[:, 0:2048].rearrange("p (f d) -> p f d", f=2),
                                                              in_=w_fd[256 * f2:256 * f2 + 256, :].rearrange("(f p) d -> p f d", p=128)), writes=[wst3])
                    eng = "act" if f2 % 2 == 0 else "dve"
                    if eng == "act":
                        S.op("act", lambda e: e.activation(out=Wd[:, 2 * f2:2 * f2 + 2, :], in_=wst3[:, 0:2048].rearrange("p (f d) -> p f d", f=2), func=AF.Copy),
                             [wst3], [Wd])
                    else:
                        S.op("dve", lambda e: e.tensor_copy(out=Wd[:, 2 * f2:2 * f2 + 2, :], in_=wst3[:, 0:2048].rearrange("p (f d) -> p f d", f=2)), [wst3], [Wd])
                for c2 in range(4):
                    S.dma("sync", "w0", lambda e: e.dma_start(out=wst3[:, 0:2048].rearrange("p (f d) -> p f d", f=2),
                                                              in_=w_out[256 * c2:256 * c2 + 256, :].rearrange("(f p) d -> p f d", p=128)), writes=[wst3])
                    for f in range(2):
                        cidx = 2 * c2 + f
                        if cidx < 4:
                            S.op("dve", lambda e: e.tensor_scalar(out=Wo[:, cidx, :], in0=wst3[:, 1024 * f:1024 * f + 1024], scalar1=ggla_t[:, cidx:cidx + 1],
                                                                  scalar2=None, op0=ALU.mult), [wst3, ggla_t], [Wo])
                        else:
                            S.op("act", lambda e: e.activation(out=Wo[:, cidx, :], in_=wst3[:, 1024 * f:1024 * f + 1024], func=AF.Copy), [wst3], [Wo])
            nb3 = T // NB + 1
            DBG_NB3 = int(os.environ.get('K_NB3', '999'))
            mcnt = [0]
            blks3 = [b_ for b_ in range(nb3 if DBG_P3 else 0) if (b_ == nb3 - 1 or b_ < DBG_NB3)]

            def blkinfo(blk):
                is_s = blk == nb3 - 1
                NT = NS if is_s else NB
                tok0 = T if is_s else NB * blk
                return is_s, NT, tok0, (xs if is_s else xp), (y_s if is_s else y_p), (0 if is_s else tok0), NT // 128

            def emit_loads(blk):
                is_s, NT, tok0, xsrc, ydst, row0, ntile = blkinfo(blk)
                hres = hres2[blk % 2]
                S.dma("sync", "m0", lambda e: e.dma_start(out=mT[:, :, 0:NT], in_=mixT[:, tok0:tok0 + NT].rearrange("(c p) t -> p c t", p=128)),
                      reads=[mixT_t], writes=[mT])
                for i in range(ntile):
                    S.dma("sync", "x%d" % i, lambda e: e.dma_start(out=hres[:, i, :], in_=xsrc[row0 + 128 * i:row0 + 128 * i + 128, :]), writes=[hres])

            if blks3:
                emit_loads(blks3[0])
            for bi, blk in enumerate(blks3):
                is_s, NT, tok0, xsrc, ydst, row0, ntile = blkinfo(blk)
                hres = hres2[blk % 2]
                for i in range(ntile):
                    cols = slice(128 * i, 128 * i + 128)
                    for dh in range(2):
                        pm = ps_m[mcnt[0] % 2]; mcnt[0] += 1
                        for c in range(8):
                            S.op("pe", lambda e: e.matmul(pm[:, :], lhsT=mT[:, c, cols], rhs=Wo[:, c, 512 * dh:512 * dh + 512], start=(c == 0), stop=(c == 7)),
                                 [mT, Wo], [pm])
                        S.op("dve", lambda e: e.tensor_tensor(out=hres[:, i, 512 * dh:512 * dh + 512], in0=pm[:, :], in1=hres[:, i, 512 * dh:512 * dh + 512], op=ALU.add),
                             [pm, hres], [hres])
                    hn_ = hn[i % 2]
                    S.op("pool", lambda e: e.memset(ssq3[i % 2][:], 0.0), writes=[ssq3[i % 2]])
                    S.op("act", lambda e: e.activation(out=junk3[:], in_=hres[:, i, :], func=AF.Square, accum_out=ssq3[i % 2][:]), [hres], [junk3, ssq3[i % 2]])
                    S.op("act", lambda e: e.activation(out=rstd3[i % 2][:], in_=ssq3[i % 2][:], func=AF.Ln, scale=1.0 / D, bias=eps_t[:]), [ssq3[i % 2], eps_t], [rstd3[i % 2]])
                    S.op("act", lambda e: e.activation(out=rstd3[i % 2][:], in_=rstd3[i % 2][:], func=AF.Exp, scale=-0.5), [rstd3[i % 2]], [rstd3[i % 2]])
                    S.op("act", lambda e: e.activation(out=hn_[:], in_=hres[:, i, :], func=AF.Copy, scale=rstd3[i % 2][:]), [hres, rstd3[i % 2]], [hn_])
                    for k in range(8):
                        S.op("pe", lambda e: e.transpose(out=ps_tr3[:, k, :], in_=hn_[:, 128 * k:128 * k + 128], identity=identb[:]), [hn_, identb], [ps_tr3])
                    S.op("dve", lambda e: e.tensor_copy(out=hnT[:, :, cols], in_=ps_tr3[:]), [ps_tr3], [hnT])
                if bi + 1 < len(blks3):
                    emit_loads(blks3[bi + 1])
                for f in range(NF):
                    pg, pu = ps_gg[f % 2], ps_uu[f % 2]
                    for k in range(8):
                        S.op("pe", lambda e: e.matmul(pg[:, 0:NT], lhsT=Wg[:, k, 128 * f:128 * f + 128], rhs=hnT[:, k, 0:NT], start=(k == 0), stop=(k == 7)),
                             [Wg, hnT], [pg])
                    for k in range(8):
                        S.op("pe", lambda e: e.matmul(pu[:, 0:NT], lhsT=Wu[:, k, 128 * f:128 * f + 128], rhs=hnT[:, k, 0:NT], start=(k == 0), stop=(k == 7)),
                             [Wu, hnT], [pu])
                    sg_ = sg[f % 2]
                    S.op("act", lambda e: e.activation(out=sg_[:, 0:NT], in_=pg[:, 0:NT], func=AF.Silu), [pg], [sg_])
                    S.op("dve", lambda e: e.tensor_tensor(out=actT[:, f, 0:NT], in0=sg_[:, 0:NT], in1=pu[:, 0:NT], op=ALU.mult), [sg_, pu], [actT])
                for i in range(ntile):
                    cols = slice(128 * i, 128 * i + 128)
                    for dh in range(2):
                        pm = ps_m[mcnt[0] % 2]; mcnt[0] += 1
                        for f in range(NF):
                            S.op("pe", lambda e: e.matmul(pm[:, :], lhsT=actT[:, f, cols], rhs=Wd[:, f, 512 * dh:512 * dh + 512], start=(f == 0), stop=(f == NF - 1)),
                                 [actT, Wd], [pm])
                        S.op("dve", lambda e: e.tensor_tensor(out=hres[:, i, 512 * dh:512 * dh + 512], in0=pm[:, :], in1=hres[:, i, 512 * dh:512 * dh + 512], op=ALU.add),
                             [pm, hres], [hres])
                    S.op("pool", lambda e: e.memset(ssq3[i % 2][:], 0.0), writes=[ssq3[i % 2]])
                    S.op("act", lambda e: e.activation(out=junk3[:], in_=hres[:, i, :], func=AF.Square, accum_out=ssq3[i % 2][:]), [hres], [junk3, ssq3[i % 2]])
                    S.op("act", lambda e: e.activation(out=rstd3[i % 2][:], in_=ssq3[i % 2][:], func=AF.Ln, scale=1.0 / D, bias=eps_t[:]), [ssq3[i % 2], eps_t], [rstd3[i % 2]])
                    S.op("act", lambda e: e.activation(out=rstd3[i % 2][:], in_=rstd3[i % 2][:], func=AF.Exp, scale=-0.5), [rstd3[i % 2]], [rstd3[i % 2]])
                    S.op("dve", lambda e: e.scalar_tensor_tensor(out=yo[:, 0:D], in0=hres[:, i, :], scalar=rstd3[i % 2][:, 0:1], in1=gfin_t[:], op0=ALU.mult, op1=ALU.mult),
                         [hres, rstd3[i % 2], gfin_t], [yo])
                    S.dma("sync", slot("yo", 2), lambda e: e.dma_start(out=ydst[row0 + 128 * i:row0 + 128 * i + 128, :], in_=yo[:, 0:D]), reads=[yo], is_out=True)

        S.run(outer)
    return nc


_CACHE = {}


def kernel(x_prompt, x_sample, state_gla, cache_swa_k, cache_swa_v, w_in, w_gate_up, b_gate,
           g_mix_norm, g_gla_norm, w_out, g_ffn_norm, w_ffn_gate, w_ffn_up, w_ffn_down, g_final):
    f = lambda a: np.ascontiguousarray(np.asarray(a), dtype=np.float32)
    if "nc" not in _CACHE:
        _CACHE["nc"] = build_program()
        _CACHE["consts"] = _consts()
    nc = _CACHE["nc"]
    consts = _CACHE["consts"]
    x_prompt, x_sample = f(x_prompt), f(x_sample)
    state_gla, cache_swa_k, cache_swa_v = f(state_gla), f(cache_swa_k), f(cache_swa_v)
    shared = {
        "w_in": f(w_in)[0],
        "wgup": np.ascontiguousarray(np.concatenate([f(w_gate_up)[0], f(b_gate)[0][None, :]], axis=0)),
        "gmix": np.ascontiguousarray(f(g_mix_norm)[0].reshape(8, 128).T),
        "ggla": np.ascontiguousarray(f(g_gla_norm)[0].reshape(4, 128).T),
        "gffn": np.ascontiguousarray(f(g_ffn_norm)[0].reshape(8, 128).T),
        "gfin": f(g_final),
        "w_out": f(w_out)[0],
        "w_fg": f(w_ffn_gate)[0],
        "w_fu": f(w_ffn_up)[0],
        "w_fd": f(w_ffn_down)[0],
    }
    shared.update(consts)
    in_maps = []
    for i in range(NCORES):
        m = dict(shared)
        m["xp"] = x_prompt[i]
        m["xs"] = x_sample[NSEQ * i:NSEQ * (i + 1)].reshape(NS, D)
        m["st_in"] = state_gla[0, NSEQ * i:NSEQ * (i + 1)]
        m["ck"] = cache_swa_k[0, NSEQ * i:NSEQ * (i + 1)].reshape(NSEQ, 2048, 512)
        m["cv"] = cache_swa_v[0, NSEQ * i:NSEQ * (i + 1)].reshape(NSEQ, 2048, 512)
        in_maps.append(m)
    res = run_bass_kernel_spmd(nc, in_maps, core_ids=list(range(NCORES)))
    R = res.results
    y_prompt = np.stack([R[i]["y_p"] for i in range(NCORES)], 0)
    y_sample = np.concatenate([R[i]["y_s"].reshape(NSEQ, TS, D) for i in range(NCORES)], 0)
    st_p = np.stack([R[i]["st_p"] for i in range(NCORES)], 0)[None]
    st_s = np.concatenate([R[i]["st_s"] for i in range(NCORES)], 0)[None]
    ck_p = np.stack([R[i]["ck_p"].reshape(2048, 8, 64) for i in range(NCORES)], 0)[None]
    cv_p = np.stack([R[i]["cv_p"].reshape(2048, 8, 64) for i in range(NCORES)], 0)[None]
    ck_s = np.concatenate([R[i]["ck_s"].reshape(NSEQ, TS, 8, 64) for i in range(NCORES)], 0)[None]
    cv_s = np.concatenate([R[i]["cv_s"].reshape(NSEQ, TS, 8, 64) for i in range(NCORES)], 0)[None]
    return (y_prompt, y_sample, st_p, st_s, ck_p, cv_p, ck_s, cv_s)
```

```python
import contextlib
import os
import numpy as np
import concourse.bass as bass
import concourse.mybir as mybir
from concourse.bass_utils import run_bass_kernel_spmd

F32 = mybir.dt.float32
BF16 = mybir.dt.bfloat16
AF = mybir.ActivationFunctionType
ALU = mybir.AluOpType

NCORES = 8
D = 1024
T = 8192
NS = 128
NSEQ = 16
TS = 8
NTOK = T + NS
PW = 3088
DFF = 2816
NF = DFF // 128
EPS = 1e-6
C_QG, C_KG, C_VG, C_R, C_LR, C_QS, C_KS, C_VS = 0, 256, 512, 1024, 1536, 1552, 2064, 2576
COMPUTE = ("pe", "act", "dve", "pool")


def ss(s, n, d=1):
    return slice(s, s + (n - 1) * d + 1, d)


class Tok:
    __slots__ = ("key", "val")

    def __init__(self, key, val):
        self.key, self.val = key, val


class Rec:
    def __init__(self):
        self.call = None

    def __getattr__(self, name):
        def f(*a, **k):
            self.call = (name, a, k)
            return self
        return f


def _rec(fn):
    if fn is None:
        return None
    r = Rec()
    fn(r)
    return r.call


class TT:
    __slots__ = ("t", "w", "r", "pr")

    def __init__(self, t):
        self.t = t
        self.w = {}
        self.r = {}
        self.pr = {}

    def __getitem__(self, idx):
        return self.t[idx]


def _merge(dst, src):
    for k, v in src.items():
        if dst.get(k, 0) < v:
            dst[k] = v


class Sched:
    def __init__(self, nc):
        self.nc = nc
        self.q = {e: [] for e in ("pe", "act", "dve", "pool", "sync")}
        self.cnt = {e: 0 for e in COMPUTE}
        self.seen = {e: {} for e in self.q}
        self.dma_cnt = {}
        self.sems = {}
        self.out_toks = []
        self.dead = False
        self.nstage = 0
        self.stop = int(os.environ.get('K_STOP', '1000000'))

    def _deps(self, reads, writes):
        need = {}
        for t in reads:
            _merge(need, t.w)
        for t in writes:
            if t.r:
                t.pr = t.r
                t.r = {}
                t.w = {}
            _merge(need, t.pr)
            _merge(need, t.w)
        return need

    def _note(self, reads, writes, tok):
        for t in reads:
            if t.r.get(tok.key, 0) < tok.val:
                t.r[tok.key] = tok.val
        for t in writes:
            if t.w.get(tok.key, 0) < tok.val:
                t.w[tok.key] = tok.val

    def _waits(self, q, need, skip_key=None):
        out = []
        seen = self.seen[q]
        for k, v in need.items():
            if k == skip_key:
                continue
            if seen.get(k, 0) >= v:
                continue
            seen[k] = v
            out.append((k, v))
        return out

    def stage(self):
        self.nstage += 1
        if self.nstage >= self.stop:
            self.dead = True

    def op(self, eng, fn, reads=(), writes=()):
        if self.dead:
            return None
        need = self._deps(reads, writes)
        w = self._waits(eng, need, skip_key="pe" if eng == "pe" else None)
        self.cnt[eng] += 1
        tok = Tok(eng, self.cnt[eng])
        self.q[eng].append((w, _rec(fn), eng, 1))
        self._note(reads, writes, tok)
        return tok

    def dma(self, queue, slot, fn, reads=(), writes=(), is_out=False):
        if self.dead:
            return None
        need = self._deps(reads, writes)
        w = self._waits(queue, need)
        key = "d_" + slot
        self.dma_cnt[key] = self.dma_cnt.get(key, 0) + 16
        tok = Tok(key, self.dma_cnt[key])
        self.q[queue].append((w, _rec(fn), key, 16))
        self._note(reads, writes, tok)
        if is_out:
            self.out_toks.append(tok)
        return tok

    def run(self, st):
        nc = self.nc
        keys = list(COMPUTE) + sorted(self.dma_cnt.keys())
        for k in keys:
            self.sems[k] = st.enter_context(nc.semaphore("s_" + k))
        need = {}
        for t in self.out_toks:
            if need.get(t.key, 0) < t.val:
                need[t.key] = t.val
        self.q["sync"].append((self._waits("sync", need), None, None, 0))
        block = st.enter_context(nc.Block())
        sems = self.sems

        def replay(e, items):
            for (w, fn, key, inc) in items:
                for (k, v) in w:
                    e.wait_ge(sems[k], v)
                if fn is None:
                    continue
                getattr(e, fn[0])(*fn[1], **fn[2]).then_inc(sems[key], inc)

        @block.tensor
        def _(e):
            replay(e, self.q["pe"])

        @block.scalar
        def _(e):
            replay(e, self.q["act"])

        @block.vector
        def _(e):
            replay(e, self.q["dve"])

        @block.gpsimd
        def _(e):
            replay(e, self.q["pool"])

        @block.sync
        def _(e):
            replay(e, self.q["sync"])


def _consts():
    c = {}
    c["ident"] = np.eye(128, dtype=np.float32)
    s = np.arange(128)[:, None]
    t = np.arange(128)[None, :]
    c["ucum"] = np.where(s <= t, -1.0 / 16.0, 0.0).astype(np.float32)
    c["lrem"] = np.where(s > t, -1.0 / 16.0, 0.0).astype(np.float32)
    c["caus"] = np.tile(np.where(s <= t, 1.0, 0.0).astype(np.float32)[:, None, :], (1, 4, 1))
    slopes = np.exp2(-8.0 * (np.arange(8) + 1.0) / 8.0)
    cm = np.zeros((3, 8, 128, 256), np.float32)
    j = np.arange(128)[:, None]
    i = np.arange(128)[None, :]
    for p, d in enumerate((1, 4, 16)):
        for h in range(8):
            prev = np.where(j >= i, np.exp(-slopes[h] * d * np.maximum(i + 128 - j, 0)), 0.0)
            cur = np.where(j <= i, np.exp(-slopes[h] * d * np.maximum(i - j, 0)), 0.0)
            cm[p, h, :, 0:128] = prev
            cm[p, h, :, 128:256] = cur
    c["cmask"] = cm
    ct = np.zeros((128, 17, 8, 8), np.float64)
    def cnt(dist):
        return ((dist >= 0) & (dist <= 128)).astype(np.float64) + \
               ((dist >= 0) & (dist % 4 == 0) & (dist <= 512)) + ((dist >= 0) & (dist % 16 == 0) & (dist <= 2048))
    for kt in range(16):
        pos = 128 * kt + np.arange(128)[:, None, None]
        tq = np.arange(8)[None, None, :]
        dist = 2048 + tq - pos
        ct[:, kt] = cnt(dist) * np.exp(-slopes[None, :, None] * dist)
    pos = np.arange(8)[:, None, None]
    tq = np.arange(8)[None, None, :]
    dist = np.broadcast_to(tq - pos, (8, 8, 8))
    ct[0:8, 16] = np.where(dist >= 0, cnt(dist) * np.exp(-slopes[None, :, None] * np.maximum(dist, 0)), 0.0)
    c["ctab"] = ct.reshape(128, 17, 64).astype(np.float32)
    bm = np.zeros((64, 8, 64), np.float32)
    for h in range(8):
        bm[8 * h:8 * h + 8, h, :] = 1.0
    c["bmask"] = bm.reshape(64, 512)
    sel = np.zeros((64, 8), np.float32)
    for h in range(8):
        sel[8 * h:8 * h + 8, :] = np.eye(8)
    c["sel"] = sel
    return c


def build_program():
    nc = bass.Bass("TRN2", target_bir_lowering=False)

    def din(name, shape, dt=F32):
        return nc.dram_tensor(name, list(shape), dt, kind="ExternalInput").ap()

    def dout(name, shape, dt=F32):
        return nc.dram_tensor(name, list(shape), dt, kind="ExternalOutput").ap()

    def dscr(name, shape, dt):
        return nc.dram_tensor(name, list(shape), dt, kind="Internal").ap()

    xp = din("xp", [T, D])
    xs = din("xs", [NS, D])
    st_in = din("st_in", [NSEQ, 4, 64, 128])
    ck = din("ck", [NSEQ, 2048, 512])
    cv = din("cv", [NSEQ, 2048, 512])
    w_in = din("w_in", [D, PW])
    wgup = din("wgup", [17, 256])
    gmix = din("gmix", [128, 8])
    ggla = din("ggla", [128, 4])
    gffn = din("gffn", [128, 8])
    gfin = din("gfin", [D])
    w_out = din("w_out", [D, D])
    w_fg = din("w_fg", [D, DFF])
    w_fu = din("w_fu", [D, DFF])
    w_fd = din("w_fd", [DFF, D])
    c_ident = din("ident", [128, 128])
    c_ucum = din("ucum", [128, 128])
    c_lrem = din("lrem", [128, 128])
    c_caus = din("caus", [128, 4, 128])
    c_cmask = din("cmask", [3, 8, 128, 256])
    c_ctab = din("ctab", [128, 17, 64])
    c_bmask = din("bmask", [64, 512])
    c_sel = din("sel", [64, 8])

    y_p = dout("y_p", [T, D])
    y_s = dout("y_s", [NS, D])
    st_p = dout("st_p", [4, 64, 128])
    st_s = dout("st_s", [NSEQ, 4, 64, 128])
    ck_p = dout("ck_p", [2048, 512])
    cv_p = dout("cv_p", [2048, 512])
    ck_s = dout("ck_s", [NS, 512])
    cv_s = dout("cv_s", [NS, 512])

    projT = dscr("projT", [12 * 128, NTOK], BF16)
    mixT = dscr("mixT", [8 * 128, NTOK], BF16)
    vnew = dscr("vnew", [NSEQ, 8, 512], BF16)
    projT_t, mixT_t, vnew_t = TT(projT), TT(mixT), TT(vnew)

    S = Sched(nc)
    outer = contextlib.ExitStack()
    with outer:
        def mk(stack):
            def sb(name, shape, dt=F32):
                return TT(stack.enter_context(nc.sbuf_tensor("sb_" + name, list(shape), dt)))

            def ps(name, shape, dt=F32):
                return TT(stack.enter_context(nc.psum_tensor("pp_" + name, list(shape), dt)))
            return sb, ps

        sb0, ps0 = mk(outer)
        identf = sb0("identf", [128, 128])
        identb = sb0("identb", [128, 128], BF16)
        eps_t = sb0("eps_t", [128, 1])
        one_t = sb0("one_t", [128, 1])
        onesb = sb0("onesb", [128, 128], BF16)
        onesf = sb0("onesf", [128, 64])
        S.dma("sync", "c0", lambda e: e.dma_start(out=identf[:], in_=c_ident[:, :]), writes=[identf])
        S.op("dve", lambda e: e.tensor_copy(out=identb[:], in_=identf[:]), [identf], [identb])
        S.op("pool", lambda e: e.memset(eps_t[:], EPS), writes=[eps_t])
        S.op("pool", lambda e: e.memset(one_t[:], 1.0), writes=[one_t])
        S.op("pool", lambda e: e.memset(onesb[:], 1.0), writes=[onesb])
        S.op("pool", lambda e: e.memset(onesf[:], 1.0), writes=[onesf])

        dcount = [0]

        def slot(prefix, n):
            dcount[0] += 1
            return "%s%d" % (prefix, dcount[0] % n)

        def rms_rstd(src, ssq, rstd, junk, dim):
            S.op("pool", lambda e: e.memset(ssq[:], 0.0), writes=[ssq])
            S.op("act", lambda e: e.activation(out=junk[:], in_=src[:], func=AF.Square, accum_out=ssq[:]),
                 [src], [junk, ssq])
            S.op("act", lambda e: e.activation(out=rstd[:], in_=ssq[:], func=AF.Ln, scale=1.0 / dim, bias=eps_t[:]),
                 [ssq, eps_t], [rstd])
            S.op("act", lambda e: e.activation(out=rstd[:], in_=rstd[:], func=AF.Exp, scale=-0.5),
                 [rstd], [rstd])

        p1 = contextlib.ExitStack()
        with p1:
            sb, ps = mk(p1)
            Wb = sb("Wb", [128, 8, PW], BF16)
            wst = [sb("wst%d" % i, [128, PW]) for i in range(2)]
            gmix_t = sb("gmix_t", [128, 8])
            wg_t = sb("wg_t", [17, 256])
            ucum = sb("ucum", [128, 128])
            lrem = sb("lrem", [128, 128])
            caus = sb("caus", [128, 4, 128])
            S.dma("sync", "c1", lambda e: e.dma_start(out=gmix_t[:], in_=gmix[:, :]), writes=[gmix_t])
            S.dma("sync", "c2", lambda e: e.dma_start(out=wg_t[:], in_=wgup[:, :]), writes=[wg_t])
            S.dma("sync", "c3", lambda e: e.dma_start(out=ucum[:], in_=c_ucum[:, :]), writes=[ucum])
            S.dma("sync", "c4", lambda e: e.dma_start(out=lrem[:], in_=c_lrem[:, :]), writes=[lrem])
            S.dma("sync", "c5", lambda e: e.dma_start(out=caus[:], in_=c_caus[:, :, :]), writes=[caus])
            for k in range(8):
                w = wst[k % 2]
                S.dma("sync", "w%d" % (k % 2), lambda e, w=w, k=k: e.dma_start(out=w[:], in_=w_in[128 * k:128 * k + 128, :]),
                      writes=[w])
                eng = "act" if k % 2 == 0 else "dve"
                if eng == "act":
                    S.op("act", lambda e, w=w, k=k: e.activation(out=Wb[:, k, :], in_=w[:], func=AF.Copy,
                                                                  scale=gmix_t[:, k:k + 1]), [w, gmix_t], [Wb])
                else:
                    S.op("dve", lambda e, w=w, k=k: e.tensor_scalar(out=Wb[:, k, :], in0=w[:], scalar1=gmix_t[:, k:k + 1],
                                                                     scalar2=None, op0=ALU.mult), [w, gmix_t], [Wb])

            S.stage()
            xt = [sb("xt%d" % i, [128, D]) for i in range(2)]
            junk = sb("junk", [128, D], BF16)
            ssq = [sb("ssq%d" % i, [128, 1]) for i in range(2)]
            rstd = [sb("rstd%d" % i, [128, 1]) for i in range(2)]
            xn = [sb("xn%d" % i, [128, D], BF16) for i in range(2)]
            xnT = sb("xnT", [128, 8, 512], BF16)
            swaT = [sb("swaT%d" % i, [128, 12, 512], BF16) for i in range(2)]
            glaT = sb("glaT", [128, 8, 512], BF16)
            lrT = sb("lrT", [32, 512])
            ktok = sb("ktok", [128, 4, 256], BF16)
            vtok = sb("vtok", [128, 4, 512], BF16)
            sp_t = sb("sp_t", [128, 4, 256])
            eg = sb("eg", [128, 256])
            cst = [sb("cst%d" % i, [128, 512]) for i in range(2)]
            mixg = [sb("mixg%d" % i, [128, 4, 512], BF16) for i in range(2)]
            Sst = sb("Sst", [128, 2, 128])
            Sbf = sb("Sbf", [128, 2, 128], BF16)
            eq = sb("eq", [128, 2, 128])
            ek = sb("ek", [128, 2, 128])
            ed = sb("ed", [128, 256])
            qz = sb("qz", [128, 4, 128], BF16)
            ktl = sb("ktl", [128, 2, 128], BF16)
            khat = sb("khat", [128, 256], BF16)
            Abf = sb("Abf", [128, 4, 128], BF16)
            osq = sb("osq", [128, 4, 128], BF16)
            rs_g = sb("rs_g", [128, 4, 128])
            vnew_st = sb("vnew_st", [8, 512], BF16)
            ps_tr = ps("ps_tr", [128, 8, 128], BF16)
            ps_f = [ps("ps_f%d" % i, [128, 512]) for i in range(2)]
            ps_t = [ps("ps_t%d" % i, [128, 512]) for i in range(2)]
            ps_g1 = ps("ps_g1", [128, 512])
            ps_g2 = ps("ps_g2", [128, 4, 128])
            ps_g3 = ps("ps_g3", [128, 4, 128])

            S.op("pool", lambda e: e.memset(lrT[:], 1.0), writes=[lrT])
            S.op("pool", lambda e: e.memset(Sst[:], 0.0), writes=[Sst])
            S.op("pool", lambda e: e.memset(qz[:], 0.0), writes=[qz])
            S.op("pool", lambda e: e.memset(Sbf[:], 0.0), writes=[Sbf])

            fchunks = ([(C_QS + 128 * i, 128, ("swa", i)) for i in range(4)] +
                       [(C_KS + 128 * i, 128, ("swa", 4 + i)) for i in range(4)] +
                       [(C_VS + 128 * i, 128, ("swa", 8 + i)) for i in range(4)] +
                       [(C_QG + 128 * i, 128, ("gla", i)) for i in range(2)] +
                       [(C_KG + 128 * i, 128, ("gla", 2 + i)) for i in range(2)] +
                       [(C_R + 128 * i, 128, ("glar", 4 + i)) for i in range(4)] +
                       [(C_LR, 16, ("lr", 0))])
            fcount = [0]
            tcount = [0]
            evc = [0]

            def gla_chunk(C, col0, ktk, vtk, spk, mixdst, mcol0):
                S.op("pe", lambda e: e.matmul(ps_g1[0:C, 256:512], lhsT=lrem[0:C, 0:C], rhs=spk, start=True, stop=True),
                     [lrem, sp_t], [ps_g1])
                for j in range(2):
                    S.op("pe", lambda e, j=j: e.matmul(ps_g2[:, j, 0:C], lhsT=spk[:, 128 * j:128 * j + 128],
                                                        rhs=ucum[0:C, 0:C], start=True, stop=True),
                         [ucum, sp_t], [ps_g2])
                S.op("act", lambda e: e.activation(out=eq[:, :, 0:C], in_=ps_g2[:, 0:2, 0:C], func=AF.Exp), [ps_g2], [eq])
                S.op("act", lambda e: e.activation(out=ek[:, :, 0:C], in_=ps_g2[:, 0:2, 0:C], func=AF.Exp, scale=-1.0),
                     [ps_g2], [ek])
                S.op("act", lambda e: e.activation(out=ed[0:C, :], in_=ps_g1[0:C, 256:512], func=AF.Exp), [ps_g1], [ed])
                S.stage()
                S.op("dve", lambda e: e.scalar_tensor_tensor(out=qz[0:64, 0:4:2, 0:C], in0=eq[0:64, :, 0:C], scalar=0.125,
                                                              in1=glaT[0:64, 0:2, col0:col0 + C], op0=ALU.mult, op1=ALU.mult),
                     [eq, glaT], [qz])
                S.op("dve", lambda e: e.scalar_tensor_tensor(out=qz[64:128, 1:4:2, 0:C], in0=eq[64:128, :, 0:C], scalar=0.125,
                                                              in1=glaT[64:128, 0:2, col0:col0 + C], op0=ALU.mult, op1=ALU.mult),
                     [eq, glaT], [qz])
                S.op("pool", lambda e: e.tensor_tensor(out=ktl[:, :, 0:C], in0=ek[:, :, 0:C],
                                                       in1=glaT[:, 2:4, col0:col0 + C], op=ALU.mult), [ek, glaT], [ktl])
                S.op("dve", lambda e: e.tensor_tensor(out=khat[0:C, :], in0=ed[0:C, :], in1=ktk, op=ALU.mult),
                     [ed, ktok], [khat])
                S.stage()
                for h in range(4):
                    j = h // 2
                    S.op("pe", lambda e, h=h, j=j: e.matmul(ps_g2[0:C, h, 0:C], lhsT=ktl[:, j, 0:C],
                                                             rhs=qz[:, h, 0:C], start=True, stop=True),
                         [ktl, qz], [ps_g2])
                S.op("dve", lambda e: e.tensor_tensor(out=Abf[0:C, :, 0:C], in0=ps_g2[0:C, :, 0:C], in1=caus[0:C, :, 0:C],
                                                      op=ALU.mult), [ps_g2, caus], [Abf])
                S.stage()
                for h in range(4):
                    j, b0 = h // 2, 64 * (h % 2)
                    S.op("pe", lambda e, h=h: e.matmul(ps_g3[:, h, 0:C], lhsT=vtk[:, 128 * h:128 * h + 128],
                                                        rhs=Abf[0:C, h, 0:C], start=True, stop=False), [vtok, Abf], [ps_g3])
                    S.op("pe", lambda e, h=h, j=j: e.matmul(ps_g3[:, h, 0:C], lhsT=Sbf[:, j, :],
                                                             rhs=qz[:, h, 0:C], start=False, stop=True),
                         [Sbf, qz], [ps_g3])
                S.stage()
                for h in range(4):
                    j = h // 2
                    S.op("pe", lambda e, h=h, j=j: e.matmul(ps_g2[:, h, :], lhsT=khat[0:C, 128 * j:128 * j + 128],
                                                             rhs=vtk[:, 128 * h:128 * h + 128], start=True, stop=True),
                         [khat, vtok], [ps_g2])
                for h in range(4):
                    j, b0 = h // 2, 64 * (h % 2)
                    S.op("dve", lambda e, h=h, j=j, b0=b0: e.scalar_tensor_tensor(
                        out=Sst[b0:b0 + 64, j, :], in0=Sst[b0:b0 + 64, j, :], scalar=eq[b0:b0 + 64, j, C - 1:C],
                        in1=ps_g2[b0:b0 + 64, h, :], op0=ALU.mult, op1=ALU.add), [Sst, eq, ps_g2], [Sst])
                S.stage()
                S.op("act", lambda e: e.activation(out=osq[:, :, 0:C], in_=ps_g3[:, :, 0:C], func=AF.Square), [ps_g3], [osq])
                S.op("pool", lambda e: e.tensor_copy(out=Sbf[:], in_=Sst[:]), [Sst], [Sbf])
                for h in range(4):
                    S.op("pe", lambda e, h=h: e.matmul(ps_g1[:, 128 * h:128 * h + C],
                                                        lhsT=onesb[:, :], rhs=osq[:, h, 0:C], start=True, stop=True),
                         [onesb, osq], [ps_g1])
                psv = ps_g1[:, :].rearrange("p (h c) -> p h c", h=4)
                S.op("act", lambda e: e.activation(out=rs_g[:, :, 0:C], in_=psv[:, :, 0:C], func=AF.Ln, scale=1.0 / 128,
                                                   bias=eps_t[:]), [ps_g1, eps_t], [rs_g])
                S.op("act", lambda e: e.activation(out=rs_g[:, :, 0:C], in_=rs_g[:, :, 0:C], func=AF.Exp, scale=-0.5),
                     [rs_g], [rs_g])
                S.op("dve", lambda e: e.tensor_tensor(out=rs_g[:, :, 0:C], in0=ps_g3[:, :, 0:C], in1=rs_g[:, :, 0:C],
                                                      op=ALU.mult), [ps_g3, rs_g], [rs_g])
                S.op("pool", lambda e: e.tensor_tensor(out=mixdst[:, :, mcol0:mcol0 + C], in0=rs_g[:, :, 0:C],
                                                       in1=glaT[:, 4:8, col0:col0 + C], op=ALU.mult),
                     [rs_g, glaT], [mixdst])

            nblk = T // 512 + 1
            preloaded = set()

            def load_x(blk_, i_):
                s__ = blk_ == nblk - 1
                src_ = xs if s__ else xp
                r_ = (0 if s__ else 512 * blk_) + 128 * i_
                x__ = xt[i_ % 2]
                S.dma("sync", "x%d" % (i_ % 2), lambda e: e.dma_start(out=x__[:], in_=src_[r_:r_ + 128, :]), writes=[x__])

            DBG_NB = int(os.environ.get('K_NBLK', '99'))
            DBG_S = int(os.environ.get('K_SAMPLE', '1'))
            for blk in range(nblk):
                is_s = blk == nblk - 1
                if (not is_s and blk >= DBG_NB) or (is_s and not DBG_S):
                    continue
                NT = NS if is_s else 512
                tok0 = T if is_s else 512 * blk
                xsrc = xs if is_s else xp
                row0 = 0 if is_s else tok0
                ntile = NT // 128
                sw = swaT[blk % 2]
                mg = mixg[blk % 2]
                for i in range(ntile):
                    x_ = xt[i % 2]
                    if (blk, i) not in preloaded:
                        load_x(blk, i)
                    rms_rstd(x_, ssq[i % 2], rstd[i % 2], junk, D)
                    xn_ = xn[i % 2]
                    S.op("act", lambda e, x_=x_, xn_=xn_, r_=rstd[i % 2]: e.activation(out=xn_[:], in_=x_[:], func=AF.Copy, scale=r_[:]),
                         [x_, rstd[i % 2]], [xn_])
                    for k in range(8):
                        S.op("pe", lambda e, k=k, xn_=xn_: e.transpose(out=ps_tr[:, k, :], in_=xn_[:, 128 * k:128 * k + 128],
                                                                        identity=identb[:]), [xn_, identb], [ps_tr])
                    S.op("dve", lambda e, i=i: e.tensor_copy(out=xnT[:, :, 128 * i:128 * i + 128], in_=ps_tr[:]), [ps_tr], [xnT])
                S.stage()
                for (c0, wd, (kind, ci)) in fchunks:
                    pf = ps_f[fcount[0] % 2]
                    fcount[0] += 1
                    for k in range(8):
                        S.op("pe", lambda e, pf=pf, c0=c0, wd=wd, k=k: e.matmul(pf[0:wd, 0:NT], lhsT=Wb[:, k, c0:c0 + wd],
                                                                                 rhs=xnT[:, k, 0:NT], start=(k == 0), stop=(k == 7)),
                             [Wb, xnT], [pf])
                    evc[0] += 1
                    if kind == "swa":
                        eng = "act" if evc[0] % 2 == 0 else "dve"
                        if eng == "act":
                            S.op("act", lambda e, pf=pf, ci=ci: e.activation(out=sw[:, ci, 0:NT], in_=pf[:, 0:NT], func=AF.Copy), [pf], [sw])
                        else:
                            S.op("dve", lambda e, pf=pf, ci=ci: e.tensor_copy(out=sw[:, ci, 0:NT], in_=pf[:, 0:NT]), [pf], [sw])
                    elif kind == "gla":
                        S.op("dve", lambda e, pf=pf, ci=ci: e.tensor_copy(out=glaT[:, ci, 0:NT], in_=pf[:, 0:NT]), [pf], [glaT])
                    elif kind == "glar":
                        S.op("act", lambda e, pf=pf, ci=ci: e.activation(out=glaT[:, ci, 0:NT], in_=pf[:, 0:NT], func=AF.Silu), [pf], [glaT])
                    else:
                        S.op("dve", lambda e, pf=pf: e.tensor_copy(out=lrT[0:16, 0:NT], in_=pf[0:16, 0:NT]), [pf], [lrT])
                S.stage()
                S.dma("sync", slot("sp", 2), lambda e, sw=sw, tok0=tok0, NT=NT: e.dma_start(
                    out=projT[:, tok0:tok0 + NT].rearrange("(c p) t -> p c t", p=128), in_=sw[:, :, 0:NT]), reads=[sw], writes=[projT_t])
                S.stage()
                for i in range(ntile):
                    cols = slice(128 * i, 128 * i + 128)
                    pt = ps_t[tcount[0] % 2]; tcount[0] += 1
                    for k in range(8):
                        S.op("pe", lambda e, pt=pt, k=k, cols=cols: e.matmul(pt[:, 0:256], lhsT=xnT[:, k, cols], rhs=Wb[:, k, C_KG:C_KG + 256],
                                                                            start=(k == 0), stop=(k == 7)), [xnT, Wb], [pt])
                    S.op("dve", lambda e, pt=pt, i=i: e.tensor_copy(out=ktok[:, i, :], in_=pt[:, 0:256]), [pt], [ktok])
                    pt = ps_t[tcount[0] % 2]; tcount[0] += 1
                    for k in range(8):
                        S.op("pe", lambda e, pt=pt, k=k, cols=cols: e.matmul(pt[:, :], lhsT=xnT[:, k, cols], rhs=Wb[:, k, C_VG:C_VG + 512],
                                                                            start=(k == 0), stop=(k == 7)), [xnT, Wb], [pt])
                    S.op("act", lambda e, pt=pt, i=i: e.activation(out=vtok[:, i, :], in_=pt[:, :], func=AF.Copy), [pt], [vtok])
                    pt = ps_t[tcount[0] % 2]; tcount[0] += 1
                    S.op("pe", lambda e, pt=pt, cols=cols: e.matmul(pt[:, 0:256], lhsT=lrT[0:17, cols], rhs=wg_t[:, :], start=True, stop=True),
                         [lrT, wg_t], [pt])
                    S.op("act", lambda e, pt=pt: e.activation(out=eg[:], in_=pt[:, 0:256], func=AF.Exp, scale=-1.0), [pt], [eg])
                    S.op("act", lambda e, i=i: e.activation(out=sp_t[:, i, :], in_=eg[:], func=AF.Ln, bias=one_t[:]), [eg, one_t], [sp_t])
                    if is_s or tok0 + 128 * i >= T - 2048:
                        for (cc, dst) in ((C_KS, ck_s if is_s else ck_p), (C_VS, cv_s if is_s else cv_p)):
                            pt = ps_t[tcount[0] % 2]; tcount[0] += 1
                            for k in range(8):
                                S.op("pe", lambda e, pt=pt, k=k, cols=cols, cc=cc: e.matmul(pt[:, :], lhsT=xnT[:, k, cols], rhs=Wb[:, k, cc:cc + 512],
                                                                                           start=(k == 0), stop=(k == 7)), [xnT, Wb], [pt])
                            cs_ = cst[tcount[0] % 2]
                            S.op("dve", lambda e, pt=pt, cs_=cs_: e.tensor_copy(out=cs_[:], in_=pt[:, :]), [pt], [cs_])
                            r = 128 * i if is_s else tok0 + 128 * i - (T - 2048)
                            S.dma("sync", slot("co", 2), lambda e, cs_=cs_, dst=dst, r=r: e.dma_start(out=dst[r:r + 128, :], in_=cs_[:]),
                                  reads=[cs_], is_out=True)
                S.stage()
                nxt = blk + 1
                if nxt < nblk and not ((nxt != nblk - 1 and nxt >= DBG_NB) or (nxt == nblk - 1 and not DBG_S)):
                    for i_ in range(2 if nxt != nblk - 1 else 1):
                        load_x(nxt, i_)
                        preloaded.add((nxt, i_))
                if not is_s:
                    for i in range(ntile):
                        gla_chunk(128, 128 * i, ktok[:, i, :], vtok[:, i, :], sp_t[:, i, :], mg, 128 * i)
                    if blk == nblk - 2:
                        for j in range(2):
                            S.dma("sync", "stp%d" % j, lambda e, j=j: e.dma_start(
                                out=st_p[2 * j:2 * j + 2].rearrange("h k v -> (h k) v"), in_=Sst[:, j, :]), reads=[Sst], is_out=True)
                else:
                    for b in range(NSEQ):
                        cols = slice(8 * b, 8 * b + 8)
                        pt = ps_t[tcount[0] % 2]; tcount[0] += 1
                        for k in range(8):
                            S.op("pe", lambda e, pt=pt, k=k, cols=cols: e.matmul(pt[0:8, 0:256], lhsT=xnT[:, k, cols], rhs=Wb[:, k, C_KG:C_KG + 256],
                                                                                start=(k == 0), stop=(k == 7)), [xnT, Wb], [pt])
                        S.op("dve", lambda e, pt=pt: e.tensor_copy(out=ktok[0:8, 0, :], in_=pt[0:8, 0:256]), [pt], [ktok])
                        pt = ps_t[tcount[0] % 2]; tcount[0] += 1
                        for k in range(8):
                            S.op("pe", lambda e, pt=pt, k=k, cols=cols: e.matmul(pt[0:8, :], lhsT=xnT[:, k, cols], rhs=Wb[:, k, C_VG:C_VG + 512],
                                                                                start=(k == 0), stop=(k == 7)), [xnT, Wb], [pt])
                        S.op("act", lambda e, pt=pt: e.activation(out=vtok[0:8, 0, :], in_=pt[0:8, :], func=AF.Copy), [pt], [vtok])
                        pt = ps_t[tcount[0] % 2]; tcount[0] += 1
                        for k in range(8):
                            S.op("pe", lambda e, pt=pt, k=k, cols=cols: e.matmul(pt[0:8, :], lhsT=xnT[:, k, cols], rhs=Wb[:, k, C_VS:C_VS + 512],
                                                                                start=(k == 0), stop=(k == 7)), [xnT, Wb], [pt])
                        S.op("act", lambda e, pt=pt: e.activation(out=vnew_st[:], in_=pt[0:8, :], func=AF.Copy), [pt], [vnew_st])
                        S.dma("sync", slot("vn", 2), lambda e, b=b: e.dma_start(out=vnew[b], in_=vnew_st[:]), reads=[vnew_st], writes=[vnew_t])
                        pt = ps_t[tcount[0] % 2]; tcount[0] += 1
                        S.op("pe", lambda e, pt=pt, cols=cols: e.matmul(pt[0:8, 0:256], lhsT=lrT[0:17, cols], rhs=wg_t[:, :], start=True, stop=True),
                             [lrT, wg_t], [pt])
                        S.op("act", lambda e, pt=pt: e.activation(out=eg[0:8, :], in_=pt[0:8, 0:256], func=AF.Exp, scale=-1.0), [pt], [eg])
                        S.op("act", lambda e: e.activation(out=sp_t[0:8, 0, :], in_=eg[0:8, :], func=AF.Ln, bias=one_t[0:8, :]), [eg, one_t], [sp_t])
                        for j in range(2):
                            S.dma("sync", "sti%d" % j, lambda e, j=j, b=b: e.dma_start(
                                out=Sst[:, j, :], in_=st_in[b, 2 * j:2 * j + 2].rearrange("h k v -> (h k) v")), writes=[Sst])
                        S.op("pool", lambda e: e.tensor_copy(out=Sbf[:], in_=Sst[:]), [Sst], [Sbf])
                        gla_chunk(8, 8 * b, ktok[0:8, 0, :], vtok[0:8, 0, :], sp_t[0:8, 0, :], mg, 8 * b)
                        for j in range(2):
                            S.dma("sync", "sto%d" % j, lambda e, j=j, b=b: e.dma_start(
                                out=st_s[b, 2 * j:2 * j + 2].rearrange("h k v -> (h k) v"), in_=Sst[:, j, :]), reads=[Sst], is_out=True)
                S.dma("sync", slot("mg", 2), lambda e, mg=mg, tok0=tok0, NT=NT: e.dma_start(
                    out=mixT[0:512, tok0:tok0 + NT].rearrange("(c p) t -> p c t", p=128), in_=mg[:, :, 0:NT]), reads=[mg], writes=[mixT_t])

        DBG_P2 = int(os.environ.get('K_P2', '1'))
        DBG_P2S = int(os.environ.get('K_P2S', '1'))
        DBG_P3 = int(os.environ.get('K_P3', '1'))
        p2 = contextlib.ExitStack()
        with p2:
            sb, ps = mk(p2)
            SKEW = int(os.environ.get('K_SKEW', '3'))
            Kc = sb("Kc", [128, T], BF16)
            Vc = sb("Vc", [128, T], BF16)
            Qzz = sb("Qzz", [128, 2, T], BF16)
            cmf = sb("cmf", [128, 6, 256])
            cmb = sb("cmb", [128, 6, 256], BF16)
            Acc = [sb("Acc%d" % i, [65, 2, 2048]) for i in range(2)]
            vt_rings = {}
            for r_ in range(1):
                vt_rings[(0, 0)] = [sb("Vt0_%d" % i, [128, 2, 65], BF16) for i in range(SKEW + 3)]
            for r_ in range(4):
                vt_rings[(1, r_)] = [sb("Vt1_%d_%d" % (r_, i), [128, 2, 65], BF16) for i in range(3)]
            for r_ in range(16):
                vt_rings[(2, r_)] = [sb("Vt2_%d_%d" % (r_, i), [128, 2, 65], BF16) for i in range(2)]
            Vt = [t_ for ring in vt_rings.values() for t_ in ring]
            Pe = [sb("Pe%d" % i, [128, 2, 256], BF16) for i in range(SKEW + 1)]
            Pm = [sb("Pm%d" % i, [128, 2, 256], BF16) for i in range(SKEW + 1)]
            rden = sb("rden", [1, 2048])
            rdenb = sb("rdenb", [1, 2048], BF16)
            oTs = [sb("oTs%d" % i, [128, 2048], BF16) for i in range(2)]
            ps_S = [ps("ps_S%d" % i, [128, 2, 256]) for i in range(SKEW + 1)]
            ps_O = [ps("ps_O%d" % i, [128, 512]) for i in range(2)]
            ps_b = ps_O[1]
            ps_vt = [ps("ps_vt%d" % i, [128, 8, 128], BF16) for i in range(2)]
            for v_ in Vt:
                S.op("pool", lambda e, v_=v_: e.memset(v_[:], 1.0), writes=[v_])
            gcnt = [0]
            for c in range(4 if DBG_P2 else 0):
                S.dma("sync", "ld0", lambda e: e.dma_start(out=Kc[:], in_=projT[(4 + c) * 128:(5 + c) * 128, 0:T]), reads=[projT_t], writes=[Kc])
                S.dma("sync", "ld1", lambda e: e.dma_start(out=Vc[:], in_=projT[(8 + c) * 128:(9 + c) * 128, 0:T]), reads=[projT_t], writes=[Vc])
                S.dma("sync", "ld2", lambda e: e.dma_start(out=Qzz[:, 0, :], in_=projT[c * 128:(c + 1) * 128, 0:T]), reads=[projT_t], writes=[Qzz])
                S.dma("act", "ld3", lambda e: e.dma_start(out=Qzz[:, 1, :], in_=projT[c * 128:(c + 1) * 128, 0:T]), reads=[projT_t], writes=[Qzz])
                S.op("pool", lambda e: e.memset(Qzz[64:128, 0, :], 0.0), [Qzz], [Qzz])
                S.op("pool", lambda e: e.memset(Qzz[0:64, 1, :], 0.0), [Qzz], [Qzz])
                for p_ in range(3):
                    S.dma("sync", "ld4", lambda e: e.dma_start(out=cmf[:, 2 * p_:2 * p_ + 2, :],
                                                               in_=c_cmask[p_, 2 * c:2 * c + 2].rearrange("h k q -> k h q")), writes=[cmf])
                S.op("dve", lambda e: e.tensor_copy(out=cmb[:], in_=cmf[:]), [cmf], [cmb])
                qlist = []
                for SBi in range(4):
                    base = 2048 * SBi
                    qbs = []
                    for i in range(16):
                        qbs.append((0, base + 128 * i, 1, (base + 128 * i - 128) if base + 128 * i >= 128 else None))
                    for g in range(4):
                        for r in range(4):
                            q0 = base + 512 * g + r
                            qbs.append((1, q0, 4, (q0 - 512) if q0 >= 512 else None))
                    for r in range(16):
                        q0 = base + r
                        qbs.append((2, q0, 16, (q0 - 2048) if q0 >= 2048 else None))
                    for qi_, (p, q0, d, pv) in enumerate(qbs):
                        qlist.append((SBi, p, q0, d, pv, qi_ == len(qbs) - 1))
                gbase = gcnt[0]
                gcnt[0] += len(qlist)
                chain = {}
                vts = {}

                def stageA(idx):
                    (SBi, p, q0, d, pv, last) = qlist[idx]
                    gi = gbase + idx
                    pvt, pS = ps_vt[gi % 2], ps_S[gi % (SKEW + 1)]
                    ck_ = (p, 0 if p == 0 else (q0 % d))
                    st_ = chain.setdefault(ck_, [0, None])
                    ring = vt_rings[ck_]
                    vt_cur = ring[st_[0] % len(ring)]
                    vt_prev = st_[1]
                    assert (pv is None) == (vt_prev is None)
                    st_[0] += 1
                    st_[1] = vt_cur
                    vts[idx] = (vt_prev, vt_cur)
                    pe_, pm_ = Pe[gi % (SKEW + 1)], Pm[gi % (SKEW + 1)]
                    if pv is not None:
                        S.op("pe", lambda e: e.matmul(pS[:, :, 0:128], lhsT=Kc[:, ss(pv, 128, d)], rhs=Qzz[:, :, ss(q0, 128, d)],
                                                      start=True, stop=True), [Kc, Qzz], [pS])
                    S.op("pe", lambda e: e.matmul(pS[:, :, 128:256], lhsT=Kc[:, ss(q0, 128, d)], rhs=Qzz[:, :, ss(q0, 128, d)],
                                                  start=True, stop=True), [Kc, Qzz], [pS])
                    c0 = 0 if pv is not None else 128
                    S.op("act", lambda e: e.activation(out=pe_[:, :, c0:256], in_=pS[:, :, c0:256], func=AF.Exp, scale=0.125), [pS], [pe_])
                    S.op("pe", lambda e: e.transpose(out=pvt[:, 0, :], in_=Vc[:, ss(q0, 128, d)], identity=identb[:]), [Vc, identb], [pvt])
                    S.op("act", lambda e: e.activation(out=vt_cur[:, :, 0:64], in_=pvt[:, 0, :].rearrange("p (h e) -> p h e", h=2), func=AF.Copy),
                         [pvt], [vt_cur])
                    S.op("pool" if gi % 3 == 0 else "dve", lambda e: e.tensor_tensor(out=pm_[:, :, c0:256], in0=pe_[:, :, c0:256],
                                                                                     in1=cmb[:, 2 * p:2 * p + 2, c0:256], op=ALU.mult), [pe_, cmb], [pm_])

                def stageB(idx):
                    (SBi, p, q0, d, pv, last) = qlist[idx]
                    gi = gbase + idx
                    base = 2048 * SBi
                    acc = Acc[SBi % 2]
                    (vt_prev, vt_cur) = vts[idx]
                    pm_, pO = Pm[gi % (SKEW + 1)], ps_O[gi % 2]
                    for hh in range(2):
                        if pv is not None:
                            S.op("pe", lambda e: e.matmul(pO[0:65, 128 * hh:128 * hh + 128], lhsT=vt_prev[:, hh, :], rhs=pm_[:, hh, 0:128], start=True, stop=False),
                                 [vt_prev, pm_], [pO])
                        S.op("pe", lambda e: e.matmul(pO[0:65, 128 * hh:128 * hh + 128], lhsT=vt_cur[:, hh, :], rhs=pm_[:, hh, 128:256], start=(pv is None), stop=True),
                             [vt_cur, pm_], [pO])
                    dst = acc[0:65, :, ss(q0 - base, 128, d)]
                    if p == 0:
                        S.op("dve", lambda e: e.tensor_copy(out=dst, in_=pO[0:65, 0:256].rearrange("p (h q) -> p h q", h=2)), [pO], [acc])
                    else:
                        S.op("dve", lambda e: e.tensor_tensor(out=dst, in0=pO[0:65, 0:256].rearrange("p (h q) -> p h q", h=2), in1=dst, op=ALU.add), [pO, acc], [acc])
                    if last and not os.environ.get('K_NONORM'):
                        ot = oTs[SBi % 2]
                        for hh in range(2):
                            S.op("dve", lambda e: e.tensor_copy(out=rden[0:1, :], in_=acc[64:65, hh, :]), [acc], [rden])
                            S.op("act", lambda e: e.activation(out=rden[0:1, :], in_=rden[0:1, :], func=AF.Ln), [rden], [rden])
                            S.op("act", lambda e: e.activation(out=rdenb[0:1, :], in_=rden[0:1, :], func=AF.Exp, scale=-1.0), [rden], [rdenb])
                            for cb in range(4):
                                cs = slice(512 * cb, 512 * cb + 512)
                                S.op("pe", lambda e: e.matmul(ps_b[0:64, :], lhsT=onesb[0:1, 0:64], rhs=rdenb[0:1, cs], start=True, stop=True),
                                     [onesb, rdenb], [ps_b])
                                S.op("dve", lambda e: e.tensor_tensor(out=ot[64 * hh:64 * hh + 64, cs], in0=acc[0:64, hh, cs], in1=ps_b[0:64, :],
                                                                      op=ALU.mult), [acc, ps_b], [ot])
                        S.dma("sync", slot("ao", 2), lambda e: e.dma_start(out=mixT[(4 + c) * 128:(5 + c) * 128, base:base + 2048], in_=ot[:]),
                              reads=[ot], writes=[mixT_t])

                nq = len(qlist)
                for i in range(nq + SKEW):
                    if i < nq:
                        stageA(i)
                    if i >= SKEW:
                        stageB(i - SKEW)

        p2b = contextlib.ExitStack()
        with p2b:
            sb, ps = mk(p2b)
            qsT_s = sb("qsT_s", [128, 4, 128], BF16)
            ksT_s = sb("ksT_s", [128, 4, 128], BF16)
            Qbd = sb("Qbd", [128, 4, 16], BF16)
            ctab = sb("ctab", [128, 17, 64])
            bmask = sb("bmask", [64, 512])
            self_ = sb("self", [64, 8])
            selb = sb("selb", [64, 8], BF16)
            NB2 = 6
            SK2 = 2
            Kf = [sb("Kf%d" % i, [128, 512]) for i in range(NB2)]
            Vf = [sb("Vf%d" % i, [128, 512]) for i in range(NB2)]
            Kb = [sb("Kb%d" % i, [128, 512], BF16) for i in range(NB2)]
            Vb = [sb("Vb%d" % i, [128, 512], BF16) for i in range(NB2 + 1)]
            kTt = [sb("kTt%d" % i, [128, 4, 128], BF16) for i in range(NB2)]
            Pes = [sb("Pes%d" % i, [128, 64]) for i in range(NB2)]
            Pss = [sb("Pss%d" % i, [128, 64], BF16) for i in range(NB2)]
            Qbd2 = [Qbd, sb("Qbd1", [128, 4, 16], BF16)]
            vnb2 = [sb("vnb%d" % i, [8, 512], BF16) for i in range(2)]
            rdens = sb("rdens", [64, 1])
            masked = sb("masked", [64, 512], BF16)
            oT_s = sb("oT_s", [128, 4, 128], BF16)
            ps_kT = [ps("ps_kT%d" % i, [128, 8, 128], BF16) for i in range(2)]
            ps_s = [ps("ps_s%d" % i, [128, 512]) for i in range(3)]
            ps_o = ps("ps_o", [128, 512])
            ps_den = ps("ps_den", [128, 512])
            ps_r = ps("ps_r", [128, 4, 128])
            if DBG_P2S:
                S.dma("sync", "ld0", lambda e: e.dma_start(out=qsT_s[:], in_=projT[0:512, T:T + NS].rearrange("(c p) t -> p c t", p=128)),
                      reads=[projT_t], writes=[qsT_s])
                S.dma("sync", "ld1", lambda e: e.dma_start(out=ksT_s[:], in_=projT[512:1024, T:T + NS].rearrange("(c p) t -> p c t", p=128)),
                      reads=[projT_t], writes=[ksT_s])
                S.dma("sync", "ld2", lambda e: e.dma_start(out=ctab[:], in_=c_ctab[:, :, :]), writes=[ctab])
                S.dma("sync", "ld3", lambda e: e.dma_start(out=bmask[:], in_=c_bmask[:, :]), writes=[bmask])
                S.dma("sync", "ld4", lambda e: e.dma_start(out=self_[:], in_=c_sel[:, :]), writes=[self_])
                S.op("dve", lambda e: e.tensor_copy(out=selb[:], in_=self_[:]), [self_], [selb])
                S.op("pool", lambda e: e.memset(Qbd2[0][:], 0.0), writes=[Qbd2[0]])
                S.op("pool", lambda e: e.memset(Qbd2[1][:], 0.0), writes=[Qbd2[1]])
            tiles = [(b_, kt_) for b_ in range(NSEQ if DBG_P2S else 0) for kt_ in range(17)]

            def sA(idx):
                b, kt = tiles[idx]
                s_ = idx % NB2
                qb_ = Qbd2[b % 2]
                pz = ps_s[idx % 3]
                if kt == 0:
                    S.op("dve", lambda e: e.tensor_copy(out=qb_[0:64, :, 0:8], in_=qsT_s[0:64, :, 8 * b:8 * b + 8]), [qsT_s], [qb_])
                    S.op("dve", lambda e: e.tensor_copy(out=qb_[64:128, :, 8:16], in_=qsT_s[64:128, :, 8 * b:8 * b + 8]), [qsT_s], [qb_])
                    S.dma("sync", "vnl%d" % (b % 2), lambda e: e.dma_start(out=vnb2[b % 2][:], in_=vnew[b]), reads=[vnew_t], writes=[vnb2[b % 2]])
                if kt < 16:
                    vb_ = Vb[idx % (NB2 + 1)]
                    S.dma("sync", "kl%d" % s_, lambda e: e.dma_start(out=Kf[s_][:], in_=ck[b, 128 * kt:128 * kt + 128, :]), writes=[Kf[s_]])
                    S.dma("act", "vl%d" % s_, lambda e: e.dma_start(out=Vf[s_][:], in_=cv[b, 128 * kt:128 * kt + 128, :]), writes=[Vf[s_]])
                    S.op("dve", lambda e: e.tensor_copy(out=Kb[s_][:], in_=Kf[s_][:]), [Kf[s_]], [Kb[s_]])
                    S.op("pool", lambda e: e.tensor_copy(out=vb_[:], in_=Vf[s_][:]), [Vf[s_]], [vb_])
                    pk = ps_kT[idx % 2]
                    for cc in range(4):
                        S.op("pe", lambda e: e.transpose(out=pk[:, cc, :], in_=Kb[s_][:, 128 * cc:128 * cc + 128], identity=identb[:]),
                             [Kb[s_], identb], [pk])
                    S.op("act", lambda e: e.activation(out=kTt[s_][:], in_=pk[:, 0:4, :], func=AF.Copy), [pk], [kTt[s_]])
                    for cc in range(4):
                        S.op("pe", lambda e: e.matmul(pz[:, 16 * cc:16 * cc + 16], lhsT=kTt[s_][:, cc, :], rhs=qb_[:, cc, :], start=True, stop=True),
                             [kTt[s_], qb_], [pz])
                else:
                    for cc in range(4):
                        S.op("pe", lambda e: e.matmul(pz[0:8, 16 * cc:16 * cc + 16], lhsT=ksT_s[:, cc, 8 * b:8 * b + 8], rhs=qb_[:, cc, :],
                                                      start=True, stop=True), [ksT_s, qb_], [pz])

            def sB(idx):
                b, kt = tiles[idx]
                s_ = idx % NB2
                pz = ps_s[idx % 3]
                pss, pes = Pss[s_], Pes[s_]
                if kt < 16:
                    np_ = 128
                    vtt = Vb[idx % (NB2 + 1)]
                    vrhs = vtt[:, :]
                else:
                    np_ = 8
                    vtt = vnb2[b % 2]
                    vrhs = vtt[0:8, :]
                S.op("act", lambda e: e.activation(out=pes[0:np_, :], in_=pz[0:np_, 0:64], func=AF.Exp, scale=0.125), [pz], [pes])
                S.op("dve", lambda e: e.tensor_tensor(out=pss[0:np_, :], in0=pes[0:np_, :], in1=ctab[0:np_, kt, :], op=ALU.mult), [pes, ctab], [pss])
                S.op("pe", lambda e: e.matmul(ps_o[0:64, :], lhsT=pss[0:np_, :], rhs=vrhs, start=(kt == 0), stop=(kt == 16)), [pss, vtt], [ps_o])
                S.op("pe", lambda e: e.matmul(ps_den[0:64, 0:2], lhsT=pss[0:np_, :], rhs=onesb[0:np_, 0:2], start=(kt == 0), stop=(kt == 16)),
                     [pss, onesb], [ps_den])
                if kt == 16:
                    S.op("dve", lambda e: e.reciprocal(out=rdens[:], in_=ps_den[0:64, 0:1]), [ps_den], [rdens])
                    S.op("dve", lambda e: e.scalar_tensor_tensor(out=masked[:], in0=ps_o[0:64, :], scalar=rdens[:, 0:1], in1=bmask[:],
                                                                  op0=ALU.mult, op1=ALU.mult), [ps_o, rdens, bmask], [masked])
                    for cc in range(4):
                        S.op("pe", lambda e: e.matmul(ps_r[:, cc, 0:8], lhsT=masked[:, 128 * cc:128 * cc + 128], rhs=selb[:, :], start=True, stop=True),
                             [masked, selb], [ps_r])
                    S.op("act", lambda e: e.activation(out=oT_s[:, :, 8 * b:8 * b + 8], in_=ps_r[:, :, 0:8], func=AF.Copy), [ps_r], [oT_s])

            for i in range(len(tiles) + SK2 if tiles else 0):
                if i < len(tiles):
                    sA(i)
                if i >= SK2:
                    sB(i - SK2)
            if DBG_P2S:
                S.dma("sync", "aos", lambda e: e.dma_start(out=mixT[512:1024, T:T + NS].rearrange("(c p) t -> p c t", p=128), in_=oT_s[:]),
                      reads=[oT_s], writes=[mixT_t])

        p3 = contextlib.ExitStack()
        with p3:
            sb, ps = mk(p3)
            NB = 256
            Wg = sb("Wg", [128, 8, DFF], BF16)
            Wu = sb("Wu", [128, 8, DFF], BF16)
            Wd = sb("Wd", [128, NF, D], BF16)
            Wo = sb("Wo", [128, 8, D], BF16)
            wst3 = sb("wst3", [128, DFF])
            gffn_t = sb("gffn_t", [128, 8])
            ggla_t = sb("ggla_t", [128, 4])
            gfin_t = sb("gfin_t", [128, D])
            mT = sb("mT", [128, 8, NB], BF16)
            hres2 = [sb("hres%d" % i, [128, 2, D]) for i in range(2)]
            junk3 = sb("junk3", [128, D], BF16)
            ssq3 = [sb("ssq3%d" % i, [128, 1]) for i in range(2)]
            rstd3 = [sb("rstd3%d" % i, [128, 1]) for i in range(2)]
            hn = [sb("hn%d" % i, [128, D], BF16) for i in range(2)]
            hnT = sb("hnT", [128, 8, NB], BF16)
            actT = sb("actT", [128, NF, NB], BF16)
            sg = [sb("sg%d" % i, [128, NB]) for i in range(2)]
            yo = wst3
            ps_tr3 = ps("ps_tr3", [128, 8, 128], BF16)
            ps_m = [ps("ps_m%d" % i, [128, 512]) for i in range(2)]
            ps_gg = [ps("ps_gg%d" % i, [128, 512]) for i in range(2)]
            ps_uu = [ps("ps_uu%d" % i, [128, 512]) for i in range(2)]
            if DBG_P3:
                S.dma("sync", "c1", lambda e: e.dma_start(out=gffn_t[:], in_=gffn[:, :]), writes=[gffn_t])
                S.dma("sync", "c2", lambda e: e.dma_start(out=ggla_t[:], in_=ggla[:, :]), writes=[ggla_t])
                S.dma("sync", "c3", lambda e: e.dma_start(out=gfin_t[:], in_=gfin.partition_broadcast(128)), writes=[gfin_t])
                wi = 0
                for (wsrc, wdst) in ((w_fg, Wg), (w_fu, Wu)):
                    for k in range(8):
                        S.dma("sync", "w0", lambda e: e.dma_start(out=wst3[:], in_=wsrc[128 * k:128 * k + 128, :]), writes=[wst3])
                        if wi % 2 == 0:
                            S.op("act", lambda e: e.activation(out=wdst[:, k, :], in_=wst3[:], func=AF.Copy, scale=gffn_t[:, k:k + 1]), [wst3, gffn_t], [wdst])
                        else:
                            S.op("dve", lambda e: e.tensor_scalar(out=wdst[:, k, :], in0=wst3[:], scalar1=gffn_t[:, k:k + 1], scalar2=None, op0=ALU.mult),
                                 [wst3, gffn_t], [wdst])
                        wi += 1
                for f2 in range(NF // 2):
                    S.dma("sync", "w0", lambda e: e.dma_start(out=wst3[:, 0:2048].rearrange("p (f d) -> p f d", f=2),
                                                              in_=w_fd[256 * f2:256 * f2 + 256, :].rearrange("(f p) d -> p f d", p=128)), writes=[wst3])
                    eng = "act" if f2 % 2 == 0 else "dve"
                    if eng == "act":
                        S.op("act", lambda e: e.activation(out=Wd[:, 2 * f2:2 * f2 + 2, :], in_=wst3[:, 0:2048].rearrange("p (f d) -> p f d", f=2), func=AF.Copy),
                             [wst3], [Wd])
                    else:
                        S.op("dve", lambda e: e.tensor_copy(out=Wd[:, 2 * f2:2 * f2 + 2, :], in_=wst3[:, 0:2048].rearrange("p (f d) -> p f d", f=2)), [wst3], [Wd])
                for c2 in range(4):
                    S.dma("sync", "w0", lambda e: e.dma_start(out=wst3[:, 0:2048].rearrange("p (f d) -> p f d", f=2),
                                                              in_=w_out[256 * c2:256 * c2 + 256, :].rearrange("(f p) d -> p f d", p=128)), writes=[wst3])
                    for f in range(2):
                        cidx = 2 * c2 + f
                        if cidx < 4:
                            S.op("dve", lambda e: e.tensor_scalar(out=Wo[:, cidx, :], in0=wst3[:, 1024 * f:1024 * f + 1024], scalar1=ggla_t[:, cidx:cidx + 1],
                                                                  scalar2=None, op0=ALU.mult), [wst3, ggla_t], [Wo])
                        else:
                            S.op("act", lambda e: e.activation(out=Wo[:, cidx, :], in_=wst3[:, 1024 * f:1024 * f + 1024], func=AF.Copy), [wst3], [Wo])
            nb3 = T // NB + 1
            DBG_NB3 = int(os.environ.get('K_NB3', '999'))
            mcnt = [0]
            blks3 = [b_ for b_ in range(nb3 if DBG_P3 else 0) if (b_ == nb3 - 1 or b_ < DBG_NB3)]

            def blkinfo(blk):
                is_s = blk == nb3 - 1
                NT = NS if is_s else NB
                tok0 = T if is_s else NB * blk
                return is_s, NT, tok0, (xs if is_s else xp), (y_s if is_s else y_p), (0 if is_s else tok0), NT // 128

            def emit_loads(blk):
                is_s, NT, tok0, xsrc, ydst, row0, ntile = blkinfo(blk)
                hres = hres2[blk % 2]
                S.dma("sync", "m0", lambda e: e.dma_start(out=mT[:, :, 0:NT], in_=mixT[:, tok0:tok0 + NT].rearrange("(c p) t -> p c t", p=128)),
                      reads=[mixT_t], writes=[mT])
                for i in range(ntile):
                    S.dma("sync", "x%d" % i, lambda e: e.dma_start(out=hres[:, i, :], in_=xsrc[row0 + 128 * i:row0 + 128 * i + 128, :]), writes=[hres])

            if blks3:
                emit_loads(blks3[0])
            for bi, blk in enumerate(blks3):
                is_s, NT, tok0, xsrc, ydst, row0, ntile = blkinfo(blk)
                hres = hres2[blk % 2]
                for i in range(ntile):
                    cols = slice(128 * i, 128 * i + 128)
                    for dh in range(2):
                        pm = ps_m[mcnt[0] % 2]; mcnt[0] += 1
                        for c in range(8):
                            S.op("pe", lambda e: e.matmul(pm[:, :], lhsT=mT[:, c, cols], rhs=Wo[:, c, 512 * dh:512 * dh + 512], start=(c == 0), stop=(c == 7)),
                                 [mT, Wo], [pm])
                        S.op("dve", lambda e: e.tensor_tensor(out=hres[:, i, 512 * dh:512 * dh + 512], in0=pm[:, :], in1=hres[:, i, 512 * dh:512 * dh + 512], op=ALU.add),
                             [pm, hres], [hres])
                    hn_ = hn[i % 2]
                    S.op("pool", lambda e: e.memset(ssq3[i % 2][:], 0.0), writes=[ssq3[i % 2]])
                    S.op("act", lambda e: e.activation(out=junk3[:], in_=hres[:, i, :], func=AF.Square, accum_out=ssq3[i % 2][:]), [hres], [junk3, ssq3[i % 2]])
                    S.op("act", lambda e: e.activation(out=rstd3[i % 2][:], in_=ssq3[i % 2][:], func=AF.Ln, scale=1.0 / D, bias=eps_t[:]), [ssq3[i % 2], eps_t], [rstd3[i % 2]])
                    S.op("act", lambda e: e.activation(out=rstd3[i % 2][:], in_=rstd3[i % 2][:], func=AF.Exp, scale=-0.5), [rstd3[i % 2]], [rstd3[i % 2]])
                    S.op("act", lambda e: e.activation(out=hn_[:], in_=hres[:, i, :], func=AF.Copy, scale=rstd3[i % 2][:]), [hres, rstd3[i % 2]], [hn_])
                    for k in range(8):
                        S.op("pe", lambda e: e.transpose(out=ps_tr3[:, k, :], in_=hn_[:, 128 * k:128 * k + 128], identity=identb[:]), [hn_, identb], [ps_tr3])
                    S.op("dve", lambda e: e.tensor_copy(out=hnT[:, :, cols], in_=ps_tr3[:]), [ps_tr3], [hnT])
                if bi + 1 < len(blks3):
                    emit_loads(blks3[bi + 1])
                for f in range(NF):
                    pg, pu = ps_gg[f % 2], ps_uu[f % 2]
                    for k in range(8):
                        S.op("pe", lambda e: e.matmul(pg[:, 0:NT], lhsT=Wg[:, k, 128 * f:128 * f + 128], rhs=hnT[:, k, 0:NT], start=(k == 0), stop=(k == 7)),
                             [Wg, hnT], [pg])
                    for k in range(8):
                        S.op("pe", lambda e: e.matmul(pu[:, 0:NT], lhsT=Wu[:, k, 128 * f:128 * f + 128], rhs=hnT[:, k, 0:NT], start=(k == 0), stop=(k == 7)),
                             [Wu, hnT], [pu])
                    sg_ = sg[f % 2]
                    S.op("act", lambda e: e.activation(out=sg_[:, 0:NT], in_=pg[:, 0:NT], func=AF.Silu), [pg], [sg_])
                    S.op("dve", lambda e: e.tensor_tensor(out=actT[:, f, 0:NT], in0=sg_[:, 0:NT], in1=pu[:, 0:NT], op=ALU.mult), [sg_, pu], [actT])
                for i in range(ntile):
                    cols = slice(128 * i, 128 * i + 128)
                    for dh in range(2):
                        pm = ps_m[mcnt[0] % 2]; mcnt[0] += 1
                        for f in range(NF):
                            S.op("pe", lambda e: e.matmul(pm[:, :], lhsT=actT[:, f, cols], rhs=Wd[:, f, 512 * dh:512 * dh + 512], start=(f == 0), stop=(f == NF - 1)),
                                 [actT, Wd], [pm])
                        S.op("dve", lambda e: e.tensor_tensor(out=hres[:, i, 512 * dh:512 * dh + 512], in0=pm[:, :], in1=hres[:, i, 512 * dh:512 * dh + 512], op=ALU.add),
                             [pm, hres], [hres])
                    S.op("pool", lambda e: e.memset(ssq3[i % 2][:], 0.0), writes=[ssq3[i % 2]])
                    S.op("act", lambda e: e.activation(out=junk3[:], in_=hres[:, i, :], func=AF.Square, accum_out=ssq3[i % 2][:]), [hres], [junk3, ssq3[i % 2]])
                    S.op("act", lambda e: e.activation(out=rstd3[i % 2][:], in_=ssq3[i % 2][:], func=AF.Ln, scale=1.0 / D, bias=eps_t[:]), [ssq3[i % 2], eps_t], [rstd3[i % 2]])
                    S.op("act", lambda e: e.activation(out=rstd3[i % 2][:], in_=rstd3[i % 2][:], func=AF.Exp, scale=-0.5), [rstd3[i % 2]], [rstd3[i % 2]])
                    S.op("dve", lambda e: e.scalar_tensor_tensor(out=yo[:, 0:D], in0=hres[:, i, :], scalar=rstd3[i % 2][:, 0:1], in1=gfin_t[:], op0=ALU.mult, op1=ALU.mult),
                         [hres, rstd3[i % 2], gfin_t], [yo])
                    S.dma("sync", slot("yo", 2), lambda e: e.dma_start(out=ydst[row0 + 128 * i:row0 + 128 * i + 128, :], in_=yo[:, 0:D]), reads=[yo], is_out=True)

        S.run(outer)
    return nc


_CACHE = {}


def kernel(x_prompt, x_sample, state_gla, cache_swa_k, cache_swa_v, w_in, w_gate_up, b_gate,
           g_mix_norm, g_gla_norm, w_out, g_ffn_norm, w_ffn_gate, w_ffn_up, w_ffn_down, g_final):
    f = lambda a: np.ascontiguousarray(np.asarray(a), dtype=np.float32)
    if "nc" not in _CACHE:
        _CACHE["nc"] = build_program()
        _CACHE["consts"] = _consts()
    nc = _CACHE["nc"]
    consts = _CACHE["consts"]
    x_prompt, x_sample = f(x_prompt), f(x_sample)
    state_gla, cache_swa_k, cache_swa_v = f(state_gla), f(cache_swa_k), f(cache_swa_v)
    shared = {
        "w_in": f(w_in)[0],
        "wgup": np.ascontiguousarray(np.concatenate([f(w_gate_up)[0], f(b_gate)[0][None, :]], axis=0)),
        "gmix": np.ascontiguousarray(f(g_mix_norm)[0].reshape(8, 128).T),
        "ggla": np.ascontiguousarray(f(g_gla_norm)[0].reshape(4, 128).T),
        "gffn": np.ascontiguousarray(f(g_ffn_norm)[0].reshape(8, 128).T),
        "gfin": f(g_final),
        "w_out": f(w_out)[0],
        "w_fg": f(w_ffn_gate)[0],
        "w_fu": f(w_ffn_up)[0],
        "w_fd": f(w_ffn_down)[0],
    }
    shared.update(consts)
    in_maps = []
    for i in range(NCORES):
        m = dict(shared)
        m["xp"] = x_prompt[i]
        m["xs"] = x_sample[NSEQ * i:NSEQ * (i + 1)].reshape(NS, D)
        m["st_in"] = state_gla[0, NSEQ * i:NSEQ * (i + 1)]
        m["ck"] = cache_swa_k[0, NSEQ * i:NSEQ * (i + 1)].reshape(NSEQ, 2048, 512)
        m["cv"] = cache_swa_v[0, NSEQ * i:NSEQ * (i + 1)].reshape(NSEQ, 2048, 512)
        in_maps.append(m)
    res = run_bass_kernel_spmd(nc, in_maps, core_ids=list(range(NCORES)))
    R = res.results
    y_prompt = np.stack([R[i]["y_p"] for i in range(NCORES)], 0)
    y_sample = np.concatenate([R[i]["y_s"].reshape(NSEQ, TS, D) for i in range(NCORES)], 0)
    st_p = np.stack([R[i]["st_p"] for i in range(NCORES)], 0)[None]
    st_s = np.concatenate([R[i]["st_s"] for i in range(NCORES)], 0)[None]
    ck_p = np.stack([R[i]["ck_p"].reshape(2048, 8, 64) for i in range(NCORES)], 0)[None]
    cv_p = np.stack([R[i]["cv_p"].reshape(2048, 8, 64) for i in range(NCORES)], 0)[None]
    ck_s = np.concatenate([R[i]["ck_s"].reshape(NSEQ, TS, 8, 64) for i in range(NCORES)], 0)[None]
    cv_s = np.concatenate([R[i]["cv_s"].reshape(NSEQ, TS, 8, 64) for i in range(NCORES)], 0)[None]
    return (y_prompt, y_sample, st_p, st_s, ck_p, cv_p, ck_s, cv_s)
```

```python
import contextlib
import os
import numpy as np
import concourse.bass as bass
import concourse.mybir as mybir
from concourse.bass_utils import run_bass_kernel_spmd

F32 = mybir.dt.float32
BF16 = mybir.dt.bfloat16
AF = mybir.ActivationFunctionType
ALU = mybir.AluOpType

NCORES = 8
D = 1024
T = 8192
NS = 128
NSEQ = 16
TS = 8
NTOK = T + NS
PW = 3088
DFF = 2816
NF = DFF // 128
EPS = 1e-6
C_QG, C_KG, C_VG, C_R, C_LR, C_QS, C_KS, C_VS = 0, 256, 512, 1024, 1536, 1552, 2064, 2576
COMPUTE = ("pe", "act", "dve", "pool")


def ss(s, n, d=1):
    return slice(s, s + (n - 1) * d + 1, d)


class Tok:
    __slots__ = ("key", "val")

    def __init__(self, key, val):
        self.key, self.val = key, val


class Rec:
    def __init__(self):
        self.call = None

    def __getattr__(self, name):
        def f(*a, **k):
            self.call = (name, a, k)
            return self
        return f


def _rec(fn):
    if fn is None:
        return None
    r = Rec()
    fn(r)
    return r.call


class TT:
    __slots__ = ("t", "w", "r", "pr")

    def __init__(self, t):
        self.t = t
        self.w = {}
        self.r = {}
        self.pr = {}

    def __getitem__(self, idx):
        return self.t[idx]


def _merge(dst, src):
    for k, v in src.items():
        if dst.get(k, 0) < v:
            dst[k] = v


class Sched:
    def __init__(self, nc):
        self.nc = nc
        self.q = {e: [] for e in ("pe", "act", "dve", "pool", "sync")}
        self.cnt = {e: 0 for e in COMPUTE}
        self.seen = {e: {} for e in self.q}
        self.dma_cnt = {}
        self.sems = {}
        self.out_toks = []
        self.dead = False
        self.nstage = 0
        self.stop = int(os.environ.get('K_STOP', '1000000'))

    def _deps(self, reads, writes):
        need = {}
        for t in reads:
            _merge(need, t.w)
        for t in writes:
            if t.r:
                t.pr = t.r
                t.r = {}
                t.w = {}
            _merge(need, t.pr)
            _merge(need, t.w)
        return need

    def _note(self, reads, writes, tok):
        for t in reads:
            if t.r.get(tok.key, 0) < tok.val:
                t.r[tok.key] = tok.val
        for t in writes:
            if t.w.get(tok.key, 0) < tok.val:
                t.w[tok.key] = tok.val

    def _waits(self, q, need, skip_key=None):
        out = []
        seen = self.seen[q]
        for k, v in need.items():
            if k == skip_key:
                continue
            if seen.get(k, 0) >= v:
                continue
            seen[k] = v
            out.append((k, v))
        return out

    def stage(self):
        self.nstage += 1
        if self.nstage >= self.stop:
            self.dead = True

    def op(self, eng, fn, reads=(), writes=()):
        if self.dead:
            return None
        need = self._deps(reads, writes)
        w = self._waits(eng, need, skip_key="pe" if eng == "pe" else None)
        self.cnt[eng] += 1
        tok = Tok(eng, self.cnt[eng])
        self.q[eng].append((w, _rec(fn), eng, 1))
        self._note(reads, writes, tok)
        return tok

    def dma(self, queue, slot, fn, reads=(), writes=(), is_out=False):
        if self.dead:
            return None
        need = self._deps(reads, writes)
        w = self._waits(queue, need)
        key = "d_" + slot
        self.dma_cnt[key] = self.dma_cnt.get(key, 0) + 16
        tok = Tok(key, self.dma_cnt[key])
        self.q[queue].append((w, _rec(fn), key, 16))
        self._note(reads, writes, tok)
        if is_out:
            self.out_toks.append(tok)
        return tok

    def run(self, st):
        nc = self.nc
        keys = list(COMPUTE) + sorted(self.dma_cnt.keys())
        for k in keys:
            self.sems[k] = st.enter_context(nc.semaphore("s_" + k))
        need = {}
        for t in self.out_toks:
            if need.get(t.key, 0) < t.val:
                need[t.key] = t.val
        self.q["sync"].append((self._waits("sync", need), None, None, 0))
        block = st.enter_context(nc.Block())
        sems = self.sems

        def replay(e, items):
            for (w, fn, key, inc) in items:
                for (k, v) in w:
                    e.wait_ge(sems[k], v)
                if fn is None:
                    continue
                getattr(e, fn[0])(*fn[1], **fn[2]).then_inc(sems[key], inc)

        @block.tensor
        def _(e):
            replay(e, self.q["pe"])

        @block.scalar
        def _(e):
            replay(e, self.q["act"])

        @block.vector
        def _(e):
            replay(e, self.q["dve"])

        @block.gpsimd
        def _(e):
            replay(e, self.q["pool"])

        @block.sync
        def _(e):
            replay(e, self.q["sync"])


def _consts():
    c = {}
    c["ident"] = np.eye(128, dtype=np.float32)
    s = np.arange(128)[:, None]
    t = np.arange(128)[None, :]
    c["ucum"] = np.where(s <= t, -1.0 / 16.0, 0.0).astype(np.float32)
    c["lrem"] = np.where(s > t, -1.0 / 16.0, 0.0).astype(np.float32)
    c["caus"] = np.tile(np.where(s <= t, 1.0, 0.0).astype(np.float32)[:, None, :], (1, 4, 1))
    slopes = np.exp2(-8.0 * (np.arange(8) + 1.0) / 8.0)
    cm = np.zeros((3, 8, 128, 256), np.float32)
    j = np.arange(128)[:, None]
    i = np.arange(128)[None, :]
    for p, d in enumerate((1, 4, 16)):
        for h in range(8):
            prev = np.where(j >= i, np.exp(-slopes[h] * d * np.maximum(i + 128 - j, 0)), 0.0)
            cur = np.where(j <= i, np.exp(-slopes[h] * d * np.maximum(i - j, 0)), 0.0)
            cm[p, h, :, 0:128] = prev
            cm[p, h, :, 128:256] = cur
    c["cmask"] = cm
    ct = np.zeros((128, 17, 8, 8), np.float64)
    def cnt(dist):
        return ((dist >= 0) & (dist <= 128)).astype(np.float64) + \
               ((dist >= 0) & (dist % 4 == 0) & (dist <= 512)) + ((dist >= 0) & (dist % 16 == 0) & (dist <= 2048))
    for kt in range(16):
        pos = 128 * kt + np.arange(128)[:, None, None]
        tq = np.arange(8)[None, None, :]
        dist = 2048 + tq - pos
        ct[:, kt] = cnt(dist) * np.exp(-slopes[None, :, None] * dist)
    pos = np.arange(8)[:, None, None]
    tq = np.arange(8)[None, None, :]
    dist = np.broadcast_to(tq - pos, (8, 8, 8))
    ct[0:8, 16] = np.where(dist >= 0, cnt(dist) * np.exp(-slopes[None, :, None] * np.maximum(dist, 0)), 0.0)
    c["ctab"] = ct.reshape(128, 17, 64).astype(np.float32)
    bm = np.zeros((64, 8, 64), np.float32)
    for h in range(8):
        bm[8 * h:8 * h + 8, h, :] = 1.0
    c["bmask"] = bm.reshape(64, 512)
    sel = np.zeros((64, 8), np.float32)
    for h in range(8):
        sel[8 * h:8 * h + 8, :] = np.eye(8)
    c["sel"] = sel
    return c


def build_program():
    nc = bass.Bass("TRN2", target_bir_lowering=False)

    def din(name, shape, dt=F32):
        return nc.dram_tensor(name, list(shape), dt, kind="ExternalInput").ap()

    def dout(name, shape, dt=F32):
        return nc.dram_tensor(name, list(shape), dt, kind="ExternalOutput").ap()

    def dscr(name, shape, dt):
        return nc.dram_tensor(name, list(shape), dt, kind="Internal").ap()

    xp = din("xp", [T, D])
    xs = din("xs", [NS, D])
    st_in = din("st_in", [NSEQ, 4, 64, 128])
    ck = din("ck", [NSEQ, 2048, 512])
    cv = din("cv", [NSEQ, 2048, 512])
    w_in = din("w_in", [D, PW])
    wgup = din("wgup", [17, 256])
    gmix = din("gmix", [128, 8])
    ggla = din("ggla", [128, 4])
    gffn = din("gffn", [128, 8])
    gfin = din("gfin", [D])
    w_out = din("w_out", [D, D])
    w_fg = din("w_fg", [D, DFF])
    w_fu = din("w_fu", [D, DFF])
    w_fd = din("w_fd", [DFF, D])
    c_ident = din("ident", [128, 128])
    c_ucum = din("ucum", [128, 128])
    c_lrem = din("lrem", [128, 128])
    c_caus = din("caus", [128, 4, 128])
    c_cmask = din("cmask", [3, 8, 128, 256])
    c_ctab = din("ctab", [128, 17, 64])
    c_bmask = din("bmask", [64, 512])
    c_sel = din("sel", [64, 8])

    y_p = dout("y_p", [T, D])
    y_s = dout("y_s", [NS, D])
    st_p = dout("st_p", [4, 64, 128])
    st_s = dout("st_s", [NSEQ, 4, 64, 128])
    ck_p = dout("ck_p", [2048, 512])
    cv_p = dout("cv_p", [2048, 512])
    ck_s = dout("ck_s", [NS, 512])
    cv_s = dout("cv_s", [NS, 512])

    projT = dscr("projT", [12 * 128, NTOK], BF16)
    mixT = dscr("mixT", [8 * 128, NTOK], BF16)
    vnew = dscr("vnew", [NSEQ, 8, 512], BF16)
    projT_t, mixT_t, vnew_t = TT(projT), TT(mixT), TT(vnew)

    S = Sched(nc)
    outer = contextlib.ExitStack()
    with outer:
        def mk(stack):
            def sb(name, shape, dt=F32):
                return TT(stack.enter_context(nc.sbuf_tensor("sb_" + name, list(shape), dt)))

            def ps(name, shape, dt=F32):
                return TT(stack.enter_context(nc.psum_tensor("pp_" + name, list(shape), dt)))
            return sb, ps

        sb0, ps0 = mk(outer)
        identf = sb0("identf", [128, 128])
        identb = sb0("identb", [128, 128], BF16)
        eps_t = sb0("eps_t", [128, 1])
        one_t = sb0("one_t", [128, 1])
        onesb = sb0("onesb", [128, 128], BF16)
        onesf = sb0("onesf", [128, 64])
        S.dma("sync", "c0", lambda e: e.dma_start(out=identf[:], in_=c_ident[:, :]), writes=[identf])
        S.op("dve", lambda e: e.tensor_copy(out=identb[:], in_=identf[:]), [identf], [identb])
        S.op("pool", lambda e: e.memset(eps_t[:], EPS), writes=[eps_t])
        S.op("pool", lambda e: e.memset(one_t[:], 1.0), writes=[one_t])
        S.op("pool", lambda e: e.memset(onesb[:], 1.0), writes=[onesb])
        S.op("pool", lambda e: e.memset(onesf[:], 1.0), writes=[onesf])

        dcount = [0]

        def slot(prefix, n):
            dcount[0] += 1
            return "%s%d" % (prefix, dcount[0] % n)

        def rms_rstd(src, ssq, rstd, junk, dim):
            S.op("pool", lambda e: e.memset(ssq[:], 0.0), writes=[ssq])
            S.op("act", lambda e: e.activation(out=junk[:], in_=src[:], func=AF.Square, accum_out=ssq[:]),
                 [src], [junk, ssq])
            S.op("act", lambda e: e.activation(out=rstd[:], in_=ssq[:], func=AF.Ln, scale=1.0 / dim, bias=eps_t[:]),
                 [ssq, eps_t], [rstd])
            S.op("act", lambda e: e.activation(out=rstd[:], in_=rstd[:], func=AF.Exp, scale=-0.5),
                 [rstd], [rstd])

        p1 = contextlib.ExitStack()
        with p1:
            sb, ps = mk(p1)
            Wb = sb("Wb", [128, 8, PW], BF16)
            wst = [sb("wst%d" % i, [128, PW]) for i in range(2)]
            gmix_t = sb("gmix_t", [128, 8])
            wg_t = sb("wg_t", [17, 256])
            ucum = sb("ucum", [128, 128])
            lrem = sb("lrem", [128, 128])
            caus = sb("caus", [128, 4, 128])
            S.dma("sync", "c1", lambda e: e.dma_start(out=gmix_t[:], in_=gmix[:, :]), writes=[gmix_t])
            S.dma("sync", "c2", lambda e: e.dma_start(out=wg_t[:], in_=wgup[:, :]), writes=[wg_t])
            S.dma("sync", "c3", lambda e: e.dma_start(out=ucum[:], in_=c_ucum[:, :]), writes=[ucum])
            S.dma("sync", "c4", lambda e: e.dma_start(out=lrem[:], in_=c_lrem[:, :]), writes=[lrem])
            S.dma("sync", "c5", lambda e: e.dma_start(out=caus[:], in_=c_caus[:, :, :]), writes=[caus])
            for k in range(8):
                w = wst[k % 2]
                S.dma("sync", "w%d" % (k % 2), lambda e, w=w, k=k: e.dma_start(out=w[:], in_=w_in[128 * k:128 * k + 128, :]),
                      writes=[w])
                eng = "act" if k % 2 == 0 else "dve"
                if eng == "act":
                    S.op("act", lambda e, w=w, k=k: e.activation(out=Wb[:, k, :], in_=w[:], func=AF.Copy,
                                                                  scale=gmix_t[:, k:k + 1]), [w, gmix_t], [Wb])
                else:
                    S.op("dve", lambda e, w=w, k=k: e.tensor_scalar(out=Wb[:, k, :], in0=w[:], scalar1=gmix_t[:, k:k + 1],
                                                                     scalar2=None, op0=ALU.mult), [w, gmix_t], [Wb])

            S.stage()
            xt = [sb("xt%d" % i, [128, D]) for i in range(2)]
            junk = sb("junk", [128, D], BF16)
            ssq = [sb("ssq%d" % i, [128, 1]) for i in range(2)]
            rstd = [sb("rstd%d" % i, [128, 1]) for i in range(2)]
            xn = [sb("xn%d" % i, [128, D], BF16) for i in range(2)]
            xnT = sb("xnT", [128, 8, 512], BF16)
            swaT = [sb("swaT%d" % i, [128, 12, 512], BF16) for i in range(2)]
            glaT = sb("glaT", [128, 8, 512], BF16)
            lrT = sb("lrT", [32, 512])
            ktok = sb("ktok", [128, 4, 256], BF16)
            vtok = sb("vtok", [128, 4, 512], BF16)
            sp_t = sb("sp_t", [128, 4, 256])
            eg = sb("eg", [128, 256])
            cst = [sb("cst%d" % i, [128, 512]) for i in range(2)]
            mixg = [sb("mixg%d" % i, [128, 4, 512], BF16) for i in range(2)]
            Sst = sb("Sst", [128, 2, 128])
            Sbf = sb("Sbf", [128, 2, 128], BF16)
            eq = sb("eq", [128, 2, 128])
            ek = sb("ek", [128, 2, 128])
            ed = sb("ed", [128, 256])
            qz = sb("qz", [128, 4, 128], BF16)
            ktl = sb("ktl", [128, 2, 128], BF16)
            khat = sb("khat", [128, 256], BF16)
            Abf = sb("Abf", [128, 4, 128], BF16)
            osq = sb("osq", [128, 4, 128], BF16)
            rs_g = sb("rs_g", [128, 4, 128])
            vnew_st = sb("vnew_st", [8, 512], BF16)
            ps_tr = ps("ps_tr", [128, 8, 128], BF16)
            ps_f = [ps("ps_f%d" % i, [128, 512]) for i in range(2)]
            ps_t = [ps("ps_t%d" % i, [128, 512]) for i in range(2)]
            ps_g1 = ps("ps_g1", [128, 512])
            ps_g2 = ps("ps_g2", [128, 4, 128])
            ps_g3 = ps("ps_g3", [128, 4, 128])

            S.op("pool", lambda e: e.memset(lrT[:], 1.0), writes=[lrT])
            S.op("pool", lambda e: e.memset(Sst[:], 0.0), writes=[Sst])
            S.op("pool", lambda e: e.memset(qz[:], 0.0), writes=[qz])
            S.op("pool", lambda e: e.memset(Sbf[:], 0.0), writes=[Sbf])

            fchunks = ([(C_QS + 128 * i, 128, ("swa", i)) for i in range(4)] +
                       [(C_KS + 128 * i, 128, ("swa", 4 + i)) for i in range(4)] +
                       [(C_VS + 128 * i, 128, ("swa", 8 + i)) for i in range(4)] +
                       [(C_QG + 128 * i, 128, ("gla", i)) for i in range(2)] +
                       [(C_KG + 128 * i, 128, ("gla", 2 + i)) for i in range(2)] +
                       [(C_R + 128 * i, 128, ("glar", 4 + i)) for i in range(4)] +
                       [(C_LR, 16, ("lr", 0))])
            fcount = [0]
            tcount = [0]
            evc = [0]

            def gla_chunk(C, col0, ktk, vtk, spk, mixdst, mcol0):
                S.op("pe", lambda e: e.matmul(ps_g1[0:C, 256:512], lhsT=lrem[0:C, 0:C], rhs=spk, start=True, stop=True),
                     [lrem, sp_t], [ps_g1])
                for j in range(2):
                    S.op("pe", lambda e, j=j: e.matmul(ps_g2[:, j, 0:C], lhsT=spk[:, 128 * j:128 * j + 128],
                                                        rhs=ucum[0:C, 0:C], start=True, stop=True),
                         [ucum, sp_t], [ps_g2])
                S.op("act", lambda e: e.activation(out=eq[:, :, 0:C], in_=ps_g2[:, 0:2, 0:C], func=AF.Exp), [ps_g2], [eq])
                S.op("act", lambda e: e.activation(out=ek[:, :, 0:C], in_=ps_g2[:, 0:2, 0:C], func=AF.Exp, scale=-1.0),
                     [ps_g2], [ek])
                S.op("act", lambda e: e.activation(out=ed[0:C, :], in_=ps_g1[0:C, 256:512], func=AF.Exp), [ps_g1], [ed])
                S.stage()
                S.op("dve", lambda e: e.scalar_tensor_tensor(out=qz[0:64, 0:4:2, 0:C], in0=eq[0:64, :, 0:C], scalar=0.125,
                                                              in1=glaT[0:64, 0:2, col0:col0 + C], op0=ALU.mult, op1=ALU.mult),
                     [eq, glaT], [qz])
                S.op("dve", lambda e: e.scalar_tensor_tensor(out=qz[64:128, 1:4:2, 0:C], in0=eq[64:128, :, 0:C], scalar=0.125,
                                                              in1=glaT[64:128, 0:2, col0:col0 + C], op0=ALU.mult, op1=ALU.mult),
                     [eq, glaT], [qz])
                S.op("pool", lambda e: e.tensor_tensor(out=ktl[:, :, 0:C], in0=ek[:, :, 0:C],
                                                       in1=glaT[:, 2:4, col0:col0 + C], op=ALU.mult), [ek, glaT], [ktl])
                S.op("dve", lambda e: e.tensor_tensor(out=khat[0:C, :], in0=ed[0:C, :], in1=ktk, op=ALU.mult),
                     [ed, ktok], [khat])
                S.stage()
                for h in range(4):
                    j = h // 2
                    S.op("pe", lambda e, h=h, j=j: e.matmul(ps_g2[0:C, h, 0:C], lhsT=ktl[:, j, 0:C],
                                                             rhs=qz[:, h, 0:C], start=True, stop=True),
                         [ktl, qz], [ps_g2])
                S.op("dve", lambda e: e.tensor_tensor(out=Abf[0:C, :, 0:C], in0=ps_g2[0:C, :, 0:C], in1=caus[0:C, :, 0:C],
                                                      op=ALU.mult), [ps_g2, caus], [Abf])
                S.stage()
                for h in range(4):
                    j, b0 = h // 2, 64 * (h % 2)
                    S.op("pe", lambda e, h=h: e.matmul(ps_g3[:, h, 0:C], lhsT=vtk[:, 128 * h:128 * h + 128],
                                                        rhs=Abf[0:C, h, 0:C], start=True, stop=False), [vtok, Abf], [ps_g3])
                    S.op("pe", lambda e, h=h, j=j: e.matmul(ps_g3[:, h, 0:C], lhsT=Sbf[:, j, :],
                                                             rhs=qz[:, h, 0:C], start=False, stop=True),
                         [Sbf, qz], [ps_g3])
                S.stage()
                for h in range(4):
                    j = h // 2
                    S.op("pe", lambda e, h=h, j=j: e.matmul(ps_g2[:, h, :], lhsT=khat[0:C, 128 * j:128 * j + 128],
                                                             rhs=vtk[:, 128 * h:128 * h + 128], start=True, stop=True),
                         [khat, vtok], [ps_g2])
                for h in range(4):
                    j, b0 = h // 2, 64 * (h % 2)
                    S.op("dve", lambda e, h=h, j=j, b0=b0: e.scalar_tensor_tensor(
                        out=Sst[b0:b0 + 64, j, :], in0=Sst[b0:b0 + 64, j, :], scalar=eq[b0:b0 + 64, j, C - 1:C],
                        in1=ps_g2[b0:b0 + 64, h, :], op0=ALU.mult, op1=ALU.add), [Sst, eq, ps_g2], [Sst])
                S.stage()
                S.op("act", lambda e: e.activation(out=osq[:, :, 0:C], in_=ps_g3[:, :, 0:C], func=AF.Square), [ps_g3], [osq])
                S.op("pool", lambda e: e.tensor_copy(out=Sbf[:], in_=Sst[:]), [Sst], [Sbf])
                for h in range(4):
                    S.op("pe", lambda e, h=h: e.matmul(ps_g1[:, 128 * h:128 * h + C],
                                                        lhsT=onesb[:, :], rhs=osq[:, h, 0:C], start=True, stop=True),
                         [onesb, osq], [ps_g1])
                psv = ps_g1[:, :].rearrange("p (h c) -> p h c", h=4)
                S.op("act", lambda e: e.activation(out=rs_g[:, :, 0:C], in_=psv[:, :, 0:C], func=AF.Ln, scale=1.0 / 128,
                                                   bias=eps_t[:]), [ps_g1, eps_t], [rs_g])
                S.op("act", lambda e: e.activation(out=rs_g[:, :, 0:C], in_=rs_g[:, :, 0:C], func=AF.Exp, scale=-0.5),
                     [rs_g], [rs_g])
                S.op("dve", lambda e: e.tensor_tensor(out=rs_g[:, :, 0:C], in0=ps_g3[:, :, 0:C], in1=rs_g[:, :, 0:C],
                                                      op=ALU.mult), [ps_g3, rs_g], [rs_g])
                S.op("pool", lambda e: e.tensor_tensor(out=mixdst[:, :, mcol0:mcol0 + C], in0=rs_g[:, :, 0:C],
                                                       in1=glaT[:, 4:8, col0:col0 + C], op=ALU.mult),
                     [rs_g, glaT], [mixdst])

            nblk = T // 512 + 1
            preloaded = set()

            def load_x(blk_, i_):
                s__ = blk_ == nblk - 1
                src_ = xs if s__ else xp
                r_ = (0 if s__ else 512 * blk_) + 128 * i_
                x__ = xt[i_ % 2]
                S.dma("sync", "x%d" % (i_ % 2), lambda e: e.dma_start(out=x__[:], in_=src_[r_:r_ + 128, :]), writes=[x__])

            DBG_NB = int(os.environ.get('K_NBLK', '99'))
            DBG_S = int(os.environ.get('K_SAMPLE', '1'))
            for blk in range(nblk):
                is_s = blk == nblk - 1
                if (not is_s and blk >= DBG_NB) or (is_s and not DBG_S):
                    continue
                NT = NS if is_s else 512
                tok0 = T if is_s else 512 * blk
                xsrc = xs if is_s else xp
                row0 = 0 if is_s else tok0
                ntile = NT // 128
                sw = swaT[blk % 2]
                mg = mixg[blk % 2]
                for i in range(ntile):
                    x_ = xt[i % 2]
                    if (blk, i) not in preloaded:
                        load_x(blk, i)
                    rms_rstd(x_, ssq[i % 2], rstd[i % 2], junk, D)
                    xn_ = xn[i % 2]
                    S.op("act", lambda e, x_=x_, xn_=xn_, r_=rstd[i % 2]: e.activation(out=xn_[:], in_=x_[:], func=AF.Copy, scale=r_[:]),
                         [x_, rstd[i % 2]], [xn_])
                    for k in range(8):
                        S.op("pe", lambda e, k=k, xn_=xn_: e.transpose(out=ps_tr[:, k, :], in_=xn_[:, 128 * k:128 * k + 128],
                                                                        identity=identb[:]), [xn_, identb], [ps_tr])
                    S.op("dve", lambda e, i=i: e.tensor_copy(out=xnT[:, :, 128 * i:128 * i + 128], in_=ps_tr[:]), [ps_tr], [xnT])
                S.stage()
                for (c0, wd, (kind, ci)) in fchunks:
                    pf = ps_f[fcount[0] % 2]
                    fcount[0] += 1
                    for k in range(8):
                        S.op("pe", lambda e, pf=pf, c0=c0, wd=wd, k=k: e.matmul(pf[0:wd, 0:NT], lhsT=Wb[:, k, c0:c0 + wd],
                                                                                 rhs=xnT[:, k, 0:NT], start=(k == 0), stop=(k == 7)),
                             [Wb, xnT], [pf])
                    evc[0] += 1
                    if kind == "swa":
                        eng = "act" if evc[0] % 2 == 0 else "dve"
                        if eng == "act":
                            S.op("act", lambda e, pf=pf, ci=ci: e.activation(out=sw[:, ci, 0:NT], in_=pf[:, 0:NT], func=AF.Copy), [pf], [sw])
                        else:
                            S.op("dve", lambda e, pf=pf, ci=ci: e.tensor_copy(out=sw[:, ci, 0:NT], in_=pf[:, 0:NT]), [pf], [sw])
                    elif kind == "gla":
                        S.op("dve", lambda e, pf=pf, ci=ci: e.tensor_copy(out=glaT[:, ci, 0:NT], in_=pf[:, 0:NT]), [pf], [glaT])
                    elif kind == "glar":
                        S.op("act", lambda e, pf=pf, ci=ci: e.activation(out=glaT[:, ci, 0:NT], in_=pf[:, 0:NT], func=AF.Silu), [pf], [glaT])
                    else:
                        S.op("dve", lambda e, pf=pf: e.tensor_copy(out=lrT[0:16, 0:NT], in_=pf[0:16, 0:NT]), [pf], [lrT])
                S.stage()
                S.dma("sync", slot("sp", 2), lambda e, sw=sw, tok0=tok0, NT=NT: e.dma_start(
                    out=projT[:, tok0:tok0 + NT].rearrange("(c p) t -> p c t", p=128), in_=sw[:, :, 0:NT]), reads=[sw], writes=[projT_t])
                S.stage()
                for i in range(ntile):
                    cols = slice(128 * i, 128 * i + 128)
                    pt = ps_t[tcount[0] % 2]; tcount[0] += 1
                    for k in range(8):
                        S.op("pe", lambda e, pt=pt, k=k, cols=cols: e.matmul(pt[:, 0:256], lhsT=xnT[:, k, cols], rhs=Wb[:, k, C_KG:C_KG + 256],
                                                                            start=(k == 0), stop=(k == 7)), [xnT, Wb], [pt])
                    S.op("dve", lambda e, pt=pt, i=i: e.tensor_copy(out=ktok[:, i, :], in_=pt[:, 0:256]), [pt], [ktok])
                    pt = ps_t[tcount[0] % 2]; tcount[0] += 1
                    for k in range(8):
                        S.op("pe", lambda e, pt=pt, k=k, cols=cols: e.matmul(pt[:, :], lhsT=xnT[:, k, cols], rhs=Wb[:, k, C_VG:C_VG + 512],
                                                                            start=(k == 0), stop=(k == 7)), [xnT, Wb], [pt])
                    S.op("act", lambda e, pt=pt, i=i: e.activation(out=vtok[:, i, :], in_=pt[:, :], func=AF.Copy), [pt], [vtok])
                    pt = ps_t[tcount[0] % 2]; tcount[0] += 1
                    S.op("pe", lambda e, pt=pt, cols=cols: e.matmul(pt[:, 0:256], lhsT=lrT[0:17, cols], rhs=wg_t[:, :], start=True, stop=True),
                         [lrT, wg_t], [pt])
                    S.op("act", lambda e, pt=pt: e.activation(out=eg[:], in_=pt[:, 0:256], func=AF.Exp, scale=-1.0), [pt], [eg])
                    S.op("act", lambda e, i=i: e.activation(out=sp_t[:, i, :], in_=eg[:], func=AF.Ln, bias=one_t[:]), [eg, one_t], [sp_t])
                    if is_s or tok0 + 128 * i >= T - 2048:
                        for (cc, dst) in ((C_KS, ck_s if is_s else ck_p), (C_VS, cv_s if is_s else cv_p)):
                            pt = ps_t[tcount[0] % 2]; tcount[0] += 1
                            for k in range(8):
                                S.op("pe", lambda e, pt=pt, k=k, cols=cols, cc=cc: e.matmul(pt[:, :], lhsT=xnT[:, k, cols], rhs=Wb[:, k, cc:cc + 512],
                                                                                           start=(k == 0), stop=(k == 7)), [xnT, Wb], [pt])
                            cs_ = cst[tcount[0] % 2]
                            S.op("dve", lambda e, pt=pt, cs_=cs_: e.tensor_copy(out=cs_[:], in_=pt[:, :]), [pt], [cs_])
                            r = 128 * i if is_s else tok0 + 128 * i - (T - 2048)
                            S.dma("sync", slot("co", 2), lambda e, cs_=cs_, dst=dst, r=r: e.dma_start(out=dst[r:r + 128, :], in_=cs_[:]),
                                  reads=[cs_], is_out=True)
                S.stage()
                nxt = blk + 1
                if nxt < nblk and not ((nxt != nblk - 1 and nxt >= DBG_NB) or (nxt == nblk - 1 and not DBG_S)):
                    for i_ in range(2 if nxt != nblk - 1 else 1):
                        load_x(nxt, i_)
                        preloaded.add((nxt, i_))
                if not is_s:
                    for i in range(ntile):
                        gla_chunk(128, 128 * i, ktok[:, i, :], vtok[:, i, :], sp_t[:, i, :], mg, 128 * i)
                    if blk == nblk - 2:
                        for j in range(2):
                            S.dma("sync", "stp%d" % j, lambda e, j=j: e.dma_start(
                                out=st_p[2 * j:2 * j + 2].rearrange("h k v -> (h k) v"), in_=Sst[:, j, :]), reads=[Sst], is_out=True)
                else:
                    for b in range(NSEQ):
                        cols = slice(8 * b, 8 * b + 8)
                        pt = ps_t[tcount[0] % 2]; tcount[0] += 1
                        for k in range(8):
                            S.op("pe", lambda e, pt=pt, k=k, cols=cols: e.matmul(pt[0:8, 0:256], lhsT=xnT[:, k, cols], rhs=Wb[:, k, C_KG:C_KG + 256],
                                                                                start=(k == 0), stop=(k == 7)), [xnT, Wb], [pt])
                        S.op("dve", lambda e, pt=pt: e.tensor_copy(out=ktok[0:8, 0, :], in_=pt[0:8, 0:256]), [pt], [ktok])
                        pt = ps_t[tcount[0] % 2]; tcount[0] += 1
                        for k in range(8):
                            S.op("pe", lambda e, pt=pt, k=k, cols=cols: e.matmul(pt[0:8, :], lhsT=xnT[:, k, cols], rhs=Wb[:, k, C_VG:C_VG + 512],
                                                                                start=(k == 0), stop=(k == 7)), [xnT, Wb], [pt])
                        S.op("act", lambda e, pt=pt: e.activation(out=vtok[0:8, 0, :], in_=pt[0:8, :], func=AF.Copy), [pt], [vtok])
                        pt = ps_t[tcount[0] % 2]; tcount[0] += 1
                        for k in range(8):
                            S.op("pe", lambda e, pt=pt, k=k, cols=cols: e.matmul(pt[0:8, :], lhsT=xnT[:, k, cols], rhs=Wb[:, k, C_VS:C_VS + 512],
                                                                                start=(k == 0), stop=(k == 7)), [xnT, Wb], [pt])
                        S.op("act", lambda e, pt=pt: e.activation(out=vnew_st[:], in_=pt[0:8, :], func=AF.Copy), [pt], [vnew_st])
                        S.dma("sync", slot("vn", 2), lambda e, b=b: e.dma_start(out=vnew[b], in_=vnew_st[:]), reads=[vnew_st], writes=[vnew_t])
                        pt = ps_t[tcount[0] % 2]; tcount[0] += 1
                        S.op("pe", lambda e, pt=pt, cols=cols: e.matmul(pt[0:8, 0:256], lhsT=lrT[0:17, cols], rhs=wg_t[:, :], start=True, stop=True),
                             [lrT, wg_t], [pt])
                        S.op("act", lambda e, pt=pt: e.activation(out=eg[0:8, :], in_=pt[0:8, 0:256], func=AF.Exp, scale=-1.0), [pt], [eg])
                        S.op("act", lambda e: e.activation(out=sp_t[0:8, 0, :], in_=eg[0:8, :], func=AF.Ln, bias=one_t[0:8, :]), [eg, one_t], [sp_t])
                        for j in range(2):
                            S.dma("sync", "sti%d" % j, lambda e, j=j, b=b: e.dma_start(
                                out=Sst[:, j, :], in_=st_in[b, 2 * j:2 * j + 2].rearrange("h k v -> (h k) v")), writes=[Sst])
                        S.op("pool", lambda e: e.tensor_copy(out=Sbf[:], in_=Sst[:]), [Sst], [Sbf])
                        gla_chunk(8, 8 * b, ktok[0:8, 0, :], vtok[0:8, 0, :], sp_t[0:8, 0, :], mg, 8 * b)
                        for j in range(2):
                            S.dma("sync", "sto%d" % j, lambda e, j=j, b=b: e.dma_start(
                                out=st_s[b, 2 * j:2 * j + 2].rearrange("h k v -> (h k) v"), in_=Sst[:, j, :]), reads=[Sst], is_out=True)
                S.dma("sync", slot("mg", 2), lambda e, mg=mg, tok0=tok0, NT=NT: e.dma_start(
                    out=mixT[0:512, tok0:tok0 + NT].rearrange("(c p) t -> p c t", p=128), in_=mg[:, :, 0:NT]), reads=[mg], writes=[mixT_t])

        DBG_P2 = int(os.environ.get('K_P2', '1'))
        DBG_P2S = int(os.environ.get('K_P2S', '1'))
        DBG_P3 = int(os.environ.get('K_P3', '1'))
        p2 = contextlib.ExitStack()
        with p2:
            sb, ps = mk(p2)
            SKEW = int(os.environ.get('K_SKEW', '3'))
            Kc = sb("Kc", [128, T], BF16)
            Vc = sb("Vc", [128, T], BF16)
            Qzz = sb("Qzz", [128, 2, T], BF16)
            cmf = sb("cmf", [128, 6, 256])
            cmb = sb("cmb", [128, 6, 256], BF16)
            Acc = [sb("Acc%d" % i, [65, 2, 2048]) for i in range(2)]
            vt_rings = {}
            for r_ in range(1):
                vt_rings[(0, 0)] = [sb("Vt0_%d" % i, [128, 2, 65], BF16) for i in range(SKEW + 3)]
            for r_ in range(4):
                vt_rings[(1, r_)] = [sb("Vt1_%d_%d" % (r_, i), [128, 2, 65], BF16) for i in range(3)]
            for r_ in range(16):
                vt_rings[(2, r_)] = [sb("Vt2_%d_%d" % (r_, i), [128, 2, 65], BF16) for i in range(2)]
            Vt = [t_ for ring in vt_rings.values() for t_ in ring]
            Pe = [sb("Pe%d" % i, [128, 2, 256], BF16) for i in range(SKEW + 1)]
            Pm = [sb("Pm%d" % i, [128, 2, 256], BF16) for i in range(SKEW + 1)]
            rden = sb("rden", [1, 2048])
            rdenb = sb("rdenb", [1, 2048], BF16)
            oTs = [sb("oTs%d" % i, [128, 2048], BF16) for i in range(2)]
            ps_S = [ps("ps_S%d" % i, [128, 2, 256]) for i in range(SKEW + 1)]
            ps_O = [ps("ps_O%d" % i, [128, 512]) for i in range(2)]
            ps_b = ps_O[1]
            ps_vt = [ps("ps_vt%d" % i, [128, 8, 128], BF16) for i in range(2)]
            for v_ in Vt:
                S.op("pool", lambda e, v_=v_: e.memset(v_[:], 1.0), writes=[v_])
            gcnt = [0]
            for c in range(4 if DBG_P2 else 0):
                S.dma("sync", "ld0", lambda e: e.dma_start(out=Kc[:], in_=projT[(4 + c) * 128:(5 + c) * 128, 0:T]), reads=[projT_t], writes=[Kc])
                S.dma("sync", "ld1", lambda e: e.dma_start(out=Vc[:], in_=projT[(8 + c) * 128:(9 + c) * 128, 0:T]), reads=[projT_t], writes=[Vc])
                S.dma("sync", "ld2", lambda e: e.dma_start(out=Qzz[:, 0, :], in_=projT[c * 128:(c + 1) * 128, 0:T]), reads=[projT_t], writes=[Qzz])
                S.dma("act", "ld3", lambda e: e.dma_start(out=Qzz[:, 1, :], in_=projT[c * 128:(c + 1) * 128, 0:T]), reads=[projT_t], writes=[Qzz])
                S.op("pool", lambda e: e.memset(Qzz[64:128, 0, :], 0.0), [Qzz], [Qzz])
                S.op("pool", lambda e: e.memset(Qzz[0:64, 1, :], 0.0), [Qzz], [Qzz])
                for p_ in range(3):
                    S.dma("sync", "ld4", lambda e: e.dma_start(out=cmf[:, 2 * p_:2 * p_ + 2, :],
                                                               in_=c_cmask[p_, 2 * c:2 * c + 2].rearrange("h k q -> k h q")), writes=[cmf])
                S.op("dve", lambda e: e.tensor_copy(out=cmb[:], in_=cmf[:]), [cmf], [cmb])
                qlist = []
                for SBi in range(4):
                    base = 2048 * SBi
                    qbs = []
                    for i in range(16):
                        qbs.append((0, base + 128 * i, 1, (base + 128 * i - 128) if base + 128 * i >= 128 else None))
                    for g in range(4):
                        for r in range(4):
                            q0 = base + 512 * g + r
                            qbs.append((1, q0, 4, (q0 - 512) if q0 >= 512 else None))
                    for r in range(16):
                        q0 = base + r
                        qbs.append((2, q0, 16, (q0 - 2048) if q0 >= 2048 else None))
                    for qi_, (p, q0, d, pv) in enumerate(qbs):
                        qlist.append((SBi, p, q0, d, pv, qi_ == len(qbs) - 1))
                gbase = gcnt[0]
                gcnt[0] += len(qlist)
                chain = {}
                vts = {}

                def stageA(idx):
                    (SBi, p, q0, d, pv, last) = qlist[idx]
                    gi = gbase + idx
                    pvt, pS = ps_vt[gi % 2], ps_S[gi % (SKEW + 1)]
                    ck_ = (p, 0 if p == 0 else (q0 % d))
                    st_ = chain.setdefault(ck_, [0, None])
                    ring = vt_rings[ck_]
                    vt_cur = ring[st_[0] % len(ring)]
                    vt_prev = st_[1]
                    assert (pv is None) == (vt_prev is None)
                    st_[0] += 1
                    st_[1] = vt_cur
                    vts[idx] = (vt_prev, vt_cur)
                    pe_, pm_ = Pe[gi % (SKEW + 1)], Pm[gi % (SKEW + 1)]
                    if pv is not None:
                        S.op("pe", lambda e: e.matmul(pS[:, :, 0:128], lhsT=Kc[:, ss(pv, 128, d)], rhs=Qzz[:, :, ss(q0, 128, d)],
                                                      start=True, stop=True), [Kc, Qzz], [pS])
                    S.op("pe", lambda e: e.matmul(pS[:, :, 128:256], lhsT=Kc[:, ss(q0, 128, d)], rhs=Qzz[:, :, ss(q0, 128, d)],
                                                  start=True, stop=True), [Kc, Qzz], [pS])
                    c0 = 0 if pv is not None else 128
                    S.op("act", lambda e: e.activation(out=pe_[:, :, c0:256], in_=pS[:, :, c0:256], func=AF.Exp, scale=0.125), [pS], [pe_])
                    S.op("pe", lambda e: e.transpose(out=pvt[:, 0, :], in_=Vc[:, ss(q0, 128, d)], identity=identb[:]), [Vc, identb], [pvt])
                    S.op("act", lambda e: e.activation(out=vt_cur[:, :, 0:64], in_=pvt[:, 0, :].rearrange("p (h e) -> p h e", h=2), func=AF.Copy),
                         [pvt], [vt_cur])
                    S.op("pool" if gi % 3 == 0 else "dve", lambda e: e.tensor_tensor(out=pm_[:, :, c0:256], in0=pe_[:, :, c0:256],
                                                                                     in1=cmb[:, 2 * p:2 * p + 2, c0:256], op=ALU.mult), [pe_, cmb], [pm_])

                def stageB(idx):
                    (SBi, p, q0, d, pv, last) = qlist[idx]
                    gi = gbase + idx
                    base = 2048 * SBi
                    acc = Acc[SBi % 2]
                    (vt_prev, vt_cur) = vts[idx]
                    pm_, pO = Pm[gi % (SKEW + 1)], ps_O[gi % 2]
                    for hh in range(2):
                        if pv is not None:
                            S.op("pe", lambda e: e.matmul(pO[0:65, 128 * hh:128 * hh + 128], lhsT=vt_prev[:, hh, :], rhs=pm_[:, hh, 0:128], start=True, stop=False),
                                 [vt_prev, pm_], [pO])
                        S.op("pe", lambda e: e.matmul(pO[0:65, 128 * hh:128 * hh + 128], lhsT=vt_cur[:, hh, :], rhs=pm_[:, hh, 128:256], start=(pv is None), stop=True),
                             [vt_cur, pm_], [pO])
                    dst = acc[0:65, :, ss(q0 - base, 128, d)]
                    if p == 0:
                        S.op("dve", lambda e: e.tensor_copy(out=dst, in_=pO[0:65, 0:256].rearrange("p (h q) -> p h q", h=2)), [pO], [acc])
                    else:
                        S.op("dve", lambda e: e.tensor_tensor(out=dst, in0=pO[0:65, 0:256].rearrange("p (h q) -> p h q", h=2), in1=dst, op=ALU.add), [pO, acc], [acc])
                    if last and not os.environ.get('K_NONORM'):
                        ot = oTs[SBi % 2]
                        for hh in range(2):
                            S.op("dve", lambda e: e.tensor_copy(out=rden[0:1, :], in_=acc[64:65, hh, :]), [acc], [rden])
                            S.op("act", lambda e: e.activation(out=rden[0:1, :], in_=rden[0:1, :], func=AF.Ln), [rden], [rden])
                            S.op("act", lambda e: e.activation(out=rdenb[0:1, :], in_=rden[0:1, :], func=AF.Exp, scale=-1.0), [rden], [rdenb])
                            for cb in range(4):
                                cs = slice(512 * cb, 512 * cb + 512)
                                S.op("pe", lambda e: e.matmul(ps_b[0:64, :], lhsT=onesb[0:1, 0:64], rhs=rdenb[0:1, cs], start=True, stop=True),
                                     [onesb, rdenb], [ps_b])
                                S.op("dve", lambda e: e.tensor_tensor(out=ot[64 * hh:64 * hh + 64, cs], in0=acc[0:64, hh, cs], in1=ps_b[0:64, :],
                                                                      op=ALU.mult), [acc, ps_b], [ot])
                        S.dma("sync", slot("ao", 2), lambda e: e.dma_start(out=mixT[(4 + c) * 128:(5 + c) * 128, base:base + 2048], in_=ot[:]),
                              reads=[ot], writes=[mixT_t])

                nq = len(qlist)
                for i in range(nq + SKEW):
                    if i < nq:
                        stageA(i)
                    if i >= SKEW:
                        stageB(i - SKEW)

        p23 = contextlib.ExitStack()
        p23.__enter__()
        sb23, _ = mk(p23)
        Wg = sb23("Wg", [128, 8, DFF], BF16)
        Wu = sb23("Wu", [128, 8, DFF], BF16)
        Wd = sb23("Wd", [128, NF, D], BF16)
        Wo = sb23("Wo", [128, 8, D], BF16)
        wst3s = [sb23("wst3_%d" % i, [128, 1024]) for i in range(2)]
        gffn_t = sb23("gffn_t", [128, 8])
        ggla_t = sb23("ggla_t", [128, 4])
        wsteps = []
        if DBG_P3:
            S.dma("sync", "c1", lambda e: e.dma_start(out=gffn_t[:], in_=gffn[:, :]), writes=[gffn_t])
            S.dma("sync", "c2", lambda e: e.dma_start(out=ggla_t[:], in_=ggla[:, :]), writes=[ggla_t])
            for (wsrc_, wdst_) in ((w_fg, Wg), (w_fu, Wu)):
                for k_ in range(8):
                    for c0_ in range(0, DFF, 1024):
                        wsteps.append((wsrc_[128 * k_:128 * k_ + 128, c0_:min(c0_ + 1024, DFF)], wdst_[:, k_, c0_:min(c0_ + 1024, DFF)],
                                       wdst_, min(1024, DFF - c0_), gffn_t, gffn_t[:, k_:k_ + 1]))
            for f_ in range(NF):
                wsteps.append((w_fd[128 * f_:128 * f_ + 128, :], Wd[:, f_, :], Wd, 1024, None, None))
            for c_ in range(8):
                wsteps.append((w_out[128 * c_:128 * c_ + 128, :], Wo[:, c_, :], Wo, 1024,
                               ggla_t if c_ < 4 else None, ggla_t[:, c_:c_ + 1] if c_ < 4 else None))
        wstep_i = [0]

        def emit_wstep():
            if wstep_i[0] >= len(wsteps):
                return
            i_ = wstep_i[0]
            wstep_i[0] += 1
            (src_, dst_, dst_tt, n_, g_tt, g_ap) = wsteps[i_]
            st_ = wst3s[i_ % 2]
            S.dma("sync", "w%d" % (i_ % 2), lambda e: e.dma_start(out=st_[:, 0:n_], in_=src_), writes=[st_])
            if g_ap is None:
                if i_ % 2 == 0:
                    S.op("act", lambda e: e.activation(out=dst_, in_=st_[:, 0:n_], func=AF.Copy), [st_], [dst_tt])
                else:
                    S.op("dve", lambda e: e.tensor_copy(out=dst_, in_=st_[:, 0:n_]), [st_], [dst_tt])
            else:
                if i_ % 2 == 0:
                    S.op("act", lambda e: e.activation(out=dst_, in_=st_[:, 0:n_], func=AF.Copy, scale=g_ap), [st_, g_tt], [dst_tt])
                else:
                    S.op("dve", lambda e: e.tensor_scalar(out=dst_, in0=st_[:, 0:n_], scalar1=g_ap, scalar2=None, op0=ALU.mult), [st_, g_tt], [dst_tt])

        p2b = contextlib.ExitStack()
        with p2b:
            sb, ps = mk(p2b)
            qsT_s = sb("qsT_s", [128, 4, 128], BF16)
            ksT_s = sb("ksT_s", [128, 4, 128], BF16)
            Qbd = sb("Qbd", [128, 4, 16], BF16)
            ctab = sb("ctab", [128, 17, 64])
            bmask = sb("bmask", [64, 512])
            self_ = sb("self", [64, 8])
            selb = sb("selb", [64, 8], BF16)
            NB2 = 3
            SK2 = 2
            Kf = [sb("Kf%d" % i, [128, 512]) for i in range(NB2)]
            Vf = [sb("Vf%d" % i, [128, 512]) for i in range(NB2)]
            Kb = [sb("Kb%d" % i, [128, 512], BF16) for i in range(NB2)]
            Vb = [sb("Vb%d" % i, [128, 512], BF16) for i in range(NB2 + 1)]
            kTt = [sb("kTt%d" % i, [128, 4, 128], BF16) for i in range(NB2)]
            Pes = [sb("Pes%d" % i, [128, 64]) for i in range(NB2)]
            Pss = [sb("Pss%d" % i, [128, 64], BF16) for i in range(NB2)]
            Qbd2 = [Qbd, sb("Qbd1", [128, 4, 16], BF16)]
            vnb2 = [sb("vnb%d" % i, [8, 512], BF16) for i in range(2)]
            rdens = sb("rdens", [64, 1])
            masked = sb("masked", [64, 512], BF16)
            oT_s = sb("oT_s", [128, 4, 128], BF16)
            ps_kT = [ps("ps_kT%d" % i, [128, 8, 128], BF16) for i in range(2)]
            ps_s = [ps("ps_s%d" % i, [128, 512]) for i in range(3)]
            ps_o = ps("ps_o", [128, 512])
            ps_den = ps("ps_den", [128, 512])
            ps_r = ps("ps_r", [128, 4, 128])
            if DBG_P2S:
                S.dma("sync", "ld0", lambda e: e.dma_start(out=qsT_s[:], in_=projT[0:512, T:T + NS].rearrange("(c p) t -> p c t", p=128)),
                      reads=[projT_t], writes=[qsT_s])
                S.dma("sync", "ld1", lambda e: e.dma_start(out=ksT_s[:], in_=projT[512:1024, T:T + NS].rearrange("(c p) t -> p c t", p=128)),
                      reads=[projT_t], writes=[ksT_s])
                S.dma("sync", "ld2", lambda e: e.dma_start(out=ctab[:], in_=c_ctab[:, :, :]), writes=[ctab])
                S.dma("sync", "ld3", lambda e: e.dma_start(out=bmask[:], in_=c_bmask[:, :]), writes=[bmask])
                S.dma("sync", "ld4", lambda e: e.dma_start(out=self_[:], in_=c_sel[:, :]), writes=[self_])
                S.op("dve", lambda e: e.tensor_copy(out=selb[:], in_=self_[:]), [self_], [selb])
                S.op("pool", lambda e: e.memset(Qbd2[0][:], 0.0), writes=[Qbd2[0]])
                S.op("pool", lambda e: e.memset(Qbd2[1][:], 0.0), writes=[Qbd2[1]])
            tiles = [(b_, kt_) for b_ in range(NSEQ if DBG_P2S else 0) for kt_ in range(17)]

            def sA(idx):
                b, kt = tiles[idx]
                s_ = idx % NB2
                qb_ = Qbd2[b % 2]
                pz = ps_s[idx % 3]
                if kt == 0:
                    S.op("dve", lambda e: e.tensor_copy(out=qb_[0:64, :, 0:8], in_=qsT_s[0:64, :, 8 * b:8 * b + 8]), [qsT_s], [qb_])
                    S.op("dve", lambda e: e.tensor_copy(out=qb_[64:128, :, 8:16], in_=qsT_s[64:128, :, 8 * b:8 * b + 8]), [qsT_s], [qb_])
                    S.dma("sync", "vnl%d" % (b % 2), lambda e: e.dma_start(out=vnb2[b % 2][:], in_=vnew[b]), reads=[vnew_t], writes=[vnb2[b % 2]])
                if kt < 16:
                    vb_ = Vb[idx % (NB2 + 1)]
                    S.dma("sync", "kl%d" % s_, lambda e: e.dma_start(out=Kf[s_][:], in_=ck[b, 128 * kt:128 * kt + 128, :]), writes=[Kf[s_]])
                    S.dma("act", "vl%d" % s_, lambda e: e.dma_start(out=Vf[s_][:], in_=cv[b, 128 * kt:128 * kt + 128, :]), writes=[Vf[s_]])
                    S.op("dve", lambda e: e.tensor_copy(out=Kb[s_][:], in_=Kf[s_][:]), [Kf[s_]], [Kb[s_]])
                    S.op("pool", lambda e: e.tensor_copy(out=vb_[:], in_=Vf[s_][:]), [Vf[s_]], [vb_])
                    pk = ps_kT[idx % 2]
                    for cc in range(4):
                        S.op("pe", lambda e: e.transpose(out=pk[:, cc, :], in_=Kb[s_][:, 128 * cc:128 * cc + 128], identity=identb[:]),
                             [Kb[s_], identb], [pk])
                    S.op("act", lambda e: e.activation(out=kTt[s_][:], in_=pk[:, 0:4, :], func=AF.Copy), [pk], [kTt[s_]])
                    for cc in range(4):
                        S.op("pe", lambda e: e.matmul(pz[:, 16 * cc:16 * cc + 16], lhsT=kTt[s_][:, cc, :], rhs=qb_[:, cc, :], start=True, stop=True),
                             [kTt[s_], qb_], [pz])
                else:
                    for cc in range(4):
                        S.op("pe", lambda e: e.matmul(pz[0:8, 16 * cc:16 * cc + 16], lhsT=ksT_s[:, cc, 8 * b:8 * b + 8], rhs=qb_[:, cc, :],
                                                      start=True, stop=True), [ksT_s, qb_], [pz])

            def sB(idx):
                b, kt = tiles[idx]
                s_ = idx % NB2
                pz = ps_s[idx % 3]
                pss, pes = Pss[s_], Pes[s_]
                if kt < 16:
                    np_ = 128
                    vtt = Vb[idx % (NB2 + 1)]
                    vrhs = vtt[:, :]
                else:
                    np_ = 8
                    vtt = vnb2[b % 2]
                    vrhs = vtt[0:8, :]
                S.op("act", lambda e: e.activation(out=pes[0:np_, :], in_=pz[0:np_, 0:64], func=AF.Exp, scale=0.125), [pz], [pes])
                S.op("dve", lambda e: e.tensor_tensor(out=pss[0:np_, :], in0=pes[0:np_, :], in1=ctab[0:np_, kt, :], op=ALU.mult), [pes, ctab], [pss])
                S.op("pe", lambda e: e.matmul(ps_o[0:64, :], lhsT=pss[0:np_, :], rhs=vrhs, start=(kt == 0), stop=(kt == 16)), [pss, vtt], [ps_o])
                S.op("pe", lambda e: e.matmul(ps_den[0:64, 0:2], lhsT=pss[0:np_, :], rhs=onesb[0:np_, 0:2], start=(kt == 0), stop=(kt == 16)),
                     [pss, onesb], [ps_den])
                if kt == 16:
                    S.op("dve", lambda e: e.reciprocal(out=rdens[:], in_=ps_den[0:64, 0:1]), [ps_den], [rdens])
                    S.op("dve", lambda e: e.scalar_tensor_tensor(out=masked[:], in0=ps_o[0:64, :], scalar=rdens[:, 0:1], in1=bmask[:],
                                                                  op0=ALU.mult, op1=ALU.mult), [ps_o, rdens, bmask], [masked])
                    for cc in range(4):
                        S.op("pe", lambda e: e.matmul(ps_r[:, cc, 0:8], lhsT=masked[:, 128 * cc:128 * cc + 128], rhs=selb[:, :], start=True, stop=True),
                             [masked, selb], [ps_r])
                    S.op("act", lambda e: e.activation(out=oT_s[:, :, 8 * b:8 * b + 8], in_=ps_r[:, :, 0:8], func=AF.Copy), [ps_r], [oT_s])

            for i in range(len(tiles) + SK2 if tiles else 0):
                if i < len(tiles):
                    sA(i)
                if i >= SK2:
                    sB(i - SK2)
                if i % 3 == 2:
                    emit_wstep()
            while wstep_i[0] < len(wsteps):
                emit_wstep()
            if DBG_P2S:
                S.dma("sync", "aos", lambda e: e.dma_start(out=mixT[512:1024, T:T + NS].rearrange("(c p) t -> p c t", p=128), in_=oT_s[:]),
                      reads=[oT_s], writes=[mixT_t])

        p3 = contextlib.ExitStack()
        with p3:
            sb, ps = mk(p3)
            NB = 256
            gfin_t = sb("gfin_t", [128, D])
            mT = sb("mT", [128, 8, NB], BF16)
            hres2 = [sb("hres%d" % i, [128, 2, D]) for i in range(2)]
            junk3 = sb("junk3", [128, D], BF16)
            ssq3 = [sb("ssq3%d" % i, [128, 1]) for i in range(2)]
            rstd3 = [sb("rstd3%d" % i, [128, 1]) for i in range(2)]
            hn = [sb("hn%d" % i, [128, D], BF16) for i in range(2)]
            hnT = sb("hnT", [128, 8, NB], BF16)
            actT = sb("actT", [128, NF, NB], BF16)
            sg = [sb("sg%d" % i, [128, NB]) for i in range(2)]
            yo = wst3s[0]
            ps_tr3 = ps("ps_tr3", [128, 8, 128], BF16)
            ps_m = [ps("ps_m%d" % i, [128, 512]) for i in range(2)]
            ps_gg = [ps("ps_gg%d" % i, [128, 512]) for i in range(2)]
            ps_uu = [ps("ps_uu%d" % i, [128, 512]) for i in range(2)]
            if DBG_P3:
                S.dma("sync", "c3", lambda e: e.dma_start(out=gfin_t[:], in_=gfin.partition_broadcast(128)), writes=[gfin_t])
            nb3 = T // NB + 1
            DBG_NB3 = int(os.environ.get('K_NB3', '999'))
            mcnt = [0]
            blks3 = [b_ for b_ in range(nb3 if DBG_P3 else 0) if (b_ == nb3 - 1 or b_ < DBG_NB3)]

            def blkinfo(blk):
                is_s = blk == nb3 - 1
                NT = NS if is_s else NB
                tok0 = T if is_s else NB * blk
                return is_s, NT, tok0, (xs if is_s else xp), (y_s if is_s else y_p), (0 if is_s else tok0), NT // 128

            def emit_loads(blk):
                is_s, NT, tok0, xsrc, ydst, row0, ntile = blkinfo(blk)
                hres = hres2[blk % 2]
                S.dma("sync", "m0", lambda e: e.dma_start(out=mT[:, :, 0:NT], in_=mixT[:, tok0:tok0 + NT].rearrange("(c p) t -> p c t", p=128)),
                      reads=[mixT_t], writes=[mT])
                for i in range(ntile):
                    S.dma("sync", "x%d" % i, lambda e: e.dma_start(out=hres[:, i, :], in_=xsrc[row0 + 128 * i:row0 + 128 * i + 128, :]), writes=[hres])

            if blks3:
                emit_loads(blks3[0])
            for bi, blk in enumerate(blks3):
                is_s, NT, tok0, xsrc, ydst, row0, ntile = blkinfo(blk)
                hres = hres2[blk % 2]
                for i in range(ntile):
                    cols = slice(128 * i, 128 * i + 128)
                    for dh in range(2):
                        pm = ps_m[mcnt[0] % 2]; mcnt[0] += 1
                        for c in range(8):
                            S.op("pe", lambda e: e.matmul(pm[:, :], lhsT=mT[:, c, cols], rhs=Wo[:, c, 512 * dh:512 * dh + 512], start=(c == 0), stop=(c == 7)),
                                 [mT, Wo], [pm])
                        S.op("dve", lambda e: e.tensor_tensor(out=hres[:, i, 512 * dh:512 * dh + 512], in0=pm[:, :], in1=hres[:, i, 512 * dh:512 * dh + 512], op=ALU.add),
                             [pm, hres], [hres])
                    hn_ = hn[i % 2]
                    S.op("pool", lambda e: e.memset(ssq3[i % 2][:], 0.0), writes=[ssq3[i % 2]])
                    S.op("act", lambda e: e.activation(out=junk3[:], in_=hres[:, i, :], func=AF.Square, accum_out=ssq3[i % 2][:]), [hres], [junk3, ssq3[i % 2]])
                    S.op("act", lambda e: e.activation(out=rstd3[i % 2][:], in_=ssq3[i % 2][:], func=AF.Ln, scale=1.0 / D, bias=eps_t[:]), [ssq3[i % 2], eps_t], [rstd3[i % 2]])
                    S.op("act", lambda e: e.activation(out=rstd3[i % 2][:], in_=rstd3[i % 2][:], func=AF.Exp, scale=-0.5), [rstd3[i % 2]], [rstd3[i % 2]])
                    S.op("act", lambda e: e.activation(out=hn_[:], in_=hres[:, i, :], func=AF.Copy, scale=rstd3[i % 2][:]), [hres, rstd3[i % 2]], [hn_])
                    for k in range(8):
                        S.op("pe", lambda e: e.transpose(out=ps_tr3[:, k, :], in_=hn_[:, 128 * k:128 * k + 128], identity=identb[:]), [hn_, identb], [ps_tr3])
                    S.op("dve", lambda e: e.tensor_copy(out=hnT[:, :, cols], in_=ps_tr3[:]), [ps_tr3], [hnT])
                if bi + 1 < len(blks3):
                    emit_loads(blks3[bi + 1])
                for f in range(NF):
                    pg, pu = ps_gg[f % 2], ps_uu[f % 2]
                    for k in range(8):
                        S.op("pe", lambda e: e.matmul(pg[:, 0:NT], lhsT=Wg[:, k, 128 * f:128 * f + 128], rhs=hnT[:, k, 0:NT], start=(k == 0), stop=(k == 7)),
                             [Wg, hnT], [pg])
                    for k in range(8):
                        S.op("pe", lambda e: e.matmul(pu[:, 0:NT], lhsT=Wu[:, k, 128 * f:128 * f + 128], rhs=hnT[:, k, 0:NT], start=(k == 0), stop=(k == 7)),
                             [Wu, hnT], [pu])
                    sg_ = sg[f % 2]
                    S.op("act", lambda e: e.activation(out=sg_[:, 0:NT], in_=pg[:, 0:NT], func=AF.Silu), [pg], [sg_])
                    S.op("dve", lambda e: e.tensor_tensor(out=actT[:, f, 0:NT], in0=sg_[:, 0:NT], in1=pu[:, 0:NT], op=ALU.mult), [sg_, pu], [actT])
                for i in range(ntile):
                    cols = slice(128 * i, 128 * i + 128)
                    for dh in range(2):
                        pm = ps_m[mcnt[0] % 2]; mcnt[0] += 1
                        for f in range(NF):
                            S.op("pe", lambda e: e.matmul(pm[:, :], lhsT=actT[:, f, cols], rhs=Wd[:, f, 512 * dh:512 * dh + 512], start=(f == 0), stop=(f == NF - 1)),
                                 [actT, Wd], [pm])
                        S.op("dve", lambda e: e.tensor_tensor(out=hres[:, i, 512 * dh:512 * dh + 512], in0=pm[:, :], in1=hres[:, i, 512 * dh:512 * dh + 512], op=ALU.add),
                             [pm, hres], [hres])
                    S.op("pool", lambda e: e.memset(ssq3[i % 2][:], 0.0), writes=[ssq3[i % 2]])
                    S.op("act", lambda e: e.activation(out=junk3[:], in_=hres[:, i, :], func=AF.Square, accum_out=ssq3[i % 2][:]), [hres], [junk3, ssq3[i % 2]])
                    S.op("act", lambda e: e.activation(out=rstd3[i % 2][:], in_=ssq3[i % 2][:], func=AF.Ln, scale=1.0 / D, bias=eps_t[:]), [ssq3[i % 2], eps_t], [rstd3[i % 2]])
                    S.op("act", lambda e: e.activation(out=rstd3[i % 2][:], in_=rstd3[i % 2][:], func=AF.Exp, scale=-0.5), [rstd3[i % 2]], [rstd3[i % 2]])
                    S.op("dve", lambda e: e.scalar_tensor_tensor(out=yo[:, 0:D], in0=hres[:, i, :], scalar=rstd3[i % 2][:, 0:1], in1=gfin_t[:], op0=ALU.mult, op1=ALU.mult),
                         [hres, rstd3[i % 2], gfin_t], [yo])
                    S.dma("sync", slot("yo", 2), lambda e: e.dma_start(out=ydst[row0 + 128 * i:row0 + 128 * i + 128, :], in_=yo[:, 0:D]), reads=[yo], is_out=True)

        p23.__exit__(None, None, None)
        S.run(outer)
    return nc


_CACHE = {}


def kernel(x_prompt, x_sample, state_gla, cache_swa_k, cache_swa_v, w_in, w_gate_up, b_gate,
           g_mix_norm, g_gla_norm, w_out, g_ffn_norm, w_ffn_gate, w_ffn_up, w_ffn_down, g_final):
    f = lambda a: np.ascontiguousarray(np.asarray(a), dtype=np.float32)
    if "nc" not in _CACHE:
        _CACHE["nc"] = build_program()
        _CACHE["consts"] = _consts()
    nc = _CACHE["nc"]
    consts = _CACHE["consts"]
    x_prompt, x_sample = f(x_prompt), f(x_sample)
    state_gla, cache_swa_k, cache_swa_v = f(state_gla), f(cache_swa_k), f(cache_swa_v)
    shared = {
        "w_in": f(w_in)[0],
        "wgup": np.ascontiguousarray(np.concatenate([f(w_gate_up)[0], f(b_gate)[0][None, :]], axis=0)),
        "gmix": np.ascontiguousarray(f(g_mix_norm)[0].reshape(8, 128).T),
        "ggla": np.ascontiguousarray(f(g_gla_norm)[0].reshape(4, 128).T),
        "gffn": np.ascontiguousarray(f(g_ffn_norm)[0].reshape(8, 128).T),
        "gfin": f(g_final),
        "w_out": f(w_out)[0],
        "w_fg": f(w_ffn_gate)[0],
        "w_fu": f(w_ffn_up)[0],
        "w_fd": f(w_ffn_down)[0],
    }
    shared.update(consts)
    in_maps = []
    for i in range(NCORES):
        m = dict(shared)
        m["xp"] = x_prompt[i]
        m["xs"] = x_sample[NSEQ * i:NSEQ * (i + 1)].reshape(NS, D)
        m["st_in"] = state_gla[0, NSEQ * i:NSEQ * (i + 1)]
        m["ck"] = cache_swa_k[0, NSEQ * i:NSEQ * (i + 1)].reshape(NSEQ, 2048, 512)
        m["cv"] = cache_swa_v[0, NSEQ * i:NSEQ * (i + 1)].reshape(NSEQ, 2048, 512)
        in_maps.append(m)
    res = run_bass_kernel_spmd(nc, in_maps, core_ids=list(range(NCORES)))
    R = res.results
    y_prompt = np.stack([R[i]["y_p"] for i in range(NCORES)], 0)
    y_sample = np.concatenate([R[i]["y_s"].reshape(NSEQ, TS, D) for i in range(NCORES)], 0)
    st_p = np.stack([R[i]["st_p"] for i in range(NCORES)], 0)[None]
    st_s = np.concatenate([R[i]["st_s"] for i in range(NCORES)], 0)[None]
    ck_p = np.stack([R[i]["ck_p"].reshape(2048, 8, 64) for i in range(NCORES)], 0)[None]
    cv_p = np.stack([R[i]["cv_p"].reshape(2048, 8, 64) for i in range(NCORES)], 0)[None]
    ck_s = np.concatenate([R[i]["ck_s"].reshape(NSEQ, TS, 8, 64) for i in range(NCORES)], 0)[None]
    cv_s = np.concatenate([R[i]["cv_s"].reshape(NSEQ, TS, 8, 64) for i in range(NCORES)], 0)[None]
    return (y_prompt, y_sample, st_p, st_s, ck_p, cv_p, ck_s, cv_s)
```
